# Optimizing a Trainium2 kernel written in Bass

```python
import jax, jax.numpy as jnp
from jax import lax
import numpy as np

D_MODEL = 1024
BATCH = 4
SEQ = 4096
DEPTH = 2

N_GROUPS = 4
GROUP_WIDTH = D_MODEL // N_GROUPS
D_MIX = N_GROUPS * GROUP_WIDTH
N_HEADS = 4
HEAD_DIM = GROUP_WIDTH // N_HEADS
CHUNK = 128
RET_CHUNK = 128
Q_BLOCK = 128
MLA_Q_LORA = D_MODEL // 4
MLA_KV_LORA = D_MODEL // 8
MLA_NOPE = HEAD_DIM
MLA_ROPE = HEAD_DIM // 2
MLA_V = HEAD_DIM
ROPE_BASE = 10000.0
D_FF = 4 * D_MODEL
EPS = 1e-6

IN_SPLIT_SIZES = (
    2 * GROUP_WIDTH,
    GROUP_WIDTH, GROUP_WIDTH, GROUP_WIDTH, GROUP_WIDTH,
    GROUP_WIDTH, GROUP_WIDTH, GROUP_WIDTH, N_HEADS,
    MLA_Q_LORA, MLA_KV_LORA, MLA_ROPE,
)
N_IN = sum(IN_SPLIT_SIZES)
IN_SPLIT_POINTS = tuple(np.cumsum(IN_SPLIT_SIZES)[:-1].tolist())

kernel_name = 'hymba_style_four_mixer_hybrid'


def rms_norm(t, g):
    tf = t.astype(jnp.float32)
    y = tf * lax.rsqrt(jnp.mean(tf * tf, axis=-1, keepdims=True) + EPS)
    return (y * g.astype(jnp.float32)).astype(t.dtype)


def standardize(t):
    mu = jnp.mean(t, axis=-1, keepdims=True)
    var = jnp.mean(jnp.square(t - mu), axis=-1, keepdims=True)
    return (t - mu) * lax.rsqrt(var + EPS)


def to_heads(t, h=N_HEADS):
    b, s, _ = t.shape
    return t.reshape(b, s, h, -1).transpose(0, 2, 1, 3)


def from_heads(t):
    b, h, s, d = t.shape
    return t.transpose(0, 2, 1, 3).reshape(b, s, h * d)


def rotary(t):
    s, d = t.shape[-2], t.shape[-1]
    half = d // 2
    inv_freq = jnp.power(ROPE_BASE, -jnp.arange(half, dtype=jnp.float32) / half)
    ang = jnp.arange(s, dtype=jnp.float32)[:, None] * inv_freq[None, :]
    cos, sin = jnp.cos(ang), jnp.sin(ang)
    t1 = t[..., :half].astype(jnp.float32)
    t2 = t[..., half:].astype(jnp.float32)
    return jnp.concatenate([t1 * cos - t2 * sin, t1 * sin + t2 * cos], axis=-1)


def causal_block_attention(q, k, v, scale, cum_log_f=None):
    b, h, s, dq = q.shape
    dv = v.shape[-1]
    nb = s // Q_BLOCK
    kf = k.astype(jnp.float32)
    vf = v.astype(jnp.float32)
    q_blocks = q.astype(jnp.float32).reshape(b, h, nb, Q_BLOCK, dq).transpose(2, 0, 1, 3, 4)
    key_pos = jnp.arange(s)
    blk_ids = jnp.arange(nb)

    def attend(i, q_blk, cum_blk):
        logits = jnp.einsum('bhqd,bhkd->bhqk', q_blk, kf) * scale
        if cum_blk is not None:
            logits = logits + cum_blk[..., :, None] - cum_log_f[:, :, None, :]
        q_pos = i * Q_BLOCK + jnp.arange(Q_BLOCK)
        mask = key_pos[None, :] <= q_pos[:, None]
        logits = jnp.where(mask, logits, -jnp.inf)
        p = jax.nn.softmax(logits, axis=-1)
        return jnp.einsum('bhqk,bhkd->bhqd', p, vf)

    if cum_log_f is None:
        out = lax.map(lambda a: attend(a[0], a[1], None), (blk_ids, q_blocks))
    else:
        cum_log_f = cum_log_f.astype(jnp.float32)
        cum_blocks = cum_log_f.reshape(b, h, nb, Q_BLOCK).transpose(2, 0, 1, 3)
        out = lax.map(lambda a: attend(a[0], a[1], a[2]), (blk_ids, q_blocks, cum_blocks))
    return out.transpose(1, 2, 0, 3, 4).reshape(b, h, s, dv)


def spatial_gating_chunked(uv, ln_gain, w_s, b_s):
    b, s, _ = uv.shape
    uvf = jax.nn.gelu(uv.astype(jnp.float32))
    u, v = jnp.split(uvf, 2, axis=-1)
    v = standardize(v.reshape(b, s, N_HEADS, HEAD_DIM)) * ln_gain.astype(jnp.float32).reshape(N_HEADS, HEAD_DIM)
    v = v.reshape(b, s // CHUNK, CHUNK, N_HEADS, HEAD_DIM)
    w_causal = jnp.tril(w_s.astype(jnp.float32))
    mixed = jnp.einsum('hts,bnshd->bnthd', w_causal, v) + b_s.astype(jnp.float32).T[None, None, :, :, None]
    return (u * mixed.reshape(b, s, GROUP_WIDTH)).astype(uv.dtype)


def retention_chunkwise(q, k, v, log_gamma):
    b, h, s, dk = q.shape
    dv = v.shape[-1]
    c = RET_CHUNK
    nc = s // c
    qc = q.reshape(b, h, nc, c, dk)
    kc = k.reshape(b, h, nc, c, dk)
    vc = v.reshape(b, h, nc, c, dv)
    j = jnp.arange(c, dtype=jnp.float32)
    lg = log_gamma[:, None]
    rel = j[:, None] - j[None, :]
    intra_decay = jnp.where(rel[None] >= 0, jnp.exp(jnp.maximum(rel, 0.0)[None] * log_gamma[:, None, None]), 0.0)
    scores = jnp.einsum('bhncd,bhnmd->bhncm', qc, kc) * intra_decay[None, :, None]
    intra = jnp.einsum('bhncm,bhnme->bhnce', scores, vc)
    key_w = jnp.exp((c - 1 - j)[None, :] * lg)
    chunk_kv = jnp.einsum('bhncd,bhnce->nbhde', kc * key_w[None, :, None, :, None], vc)
    chunk_decay = jnp.exp(c * log_gamma)[None, :, None, None]

    def step(state, kv):
        return chunk_decay * state + kv, state

    _, prev_states = lax.scan(step, jnp.zeros((b, h, dk, dv), jnp.float32), chunk_kv)
    query_w = jnp.exp((j + 1.0)[None, :] * lg)
    cross = jnp.einsum('bhncd,nbhde->bhnce', qc * query_w[None, :, None, :, None], prev_states)
    return (intra + cross).reshape(b, h, s, dv)


def retention_mixer(q, k, v, g, log_gamma):
    qh = rotary(to_heads(q))
    kh = rotary(to_heads(k)) * (HEAD_DIM ** -0.5)
    vh = to_heads(v).astype(jnp.float32)
    y = standardize(retention_chunkwise(qh, kh, vh, log_gamma))
    return (jax.nn.silu(g.astype(jnp.float32)) * from_heads(y)).astype(q.dtype)


def forgetting_attention(q, k, v, f_logit, b_f):
    log_f = jax.nn.log_sigmoid(f_logit.astype(jnp.float32) + b_f.astype(jnp.float32))
    cum = jnp.cumsum(log_f, axis=1).transpose(0, 2, 1)
    y = causal_block_attention(to_heads(q), to_heads(k), to_heads(v), HEAD_DIM ** -0.5, cum)
    return from_heads(y).astype(q.dtype)


def mla_mixer(c_q, c_kv, k_rope, g_q, w_uq, g_kv, w_ukv):
    b, s, _ = c_q.shape
    q = (rms_norm(c_q, g_q) @ w_uq).reshape(b, s, N_HEADS, MLA_NOPE + MLA_ROPE).transpose(0, 2, 1, 3)
    kv = (rms_norm(c_kv, g_kv) @ w_ukv).reshape(b, s, N_HEADS, MLA_NOPE + MLA_V).transpose(0, 2, 1, 3)
    q = jnp.concatenate([q[..., :MLA_NOPE].astype(jnp.float32), rotary(q[..., MLA_NOPE:])], axis=-1)
    k_r = jnp.broadcast_to(rotary(k_rope[:, None]), (b, N_HEADS, s, MLA_ROPE))
    k = jnp.concatenate([kv[..., :MLA_NOPE].astype(jnp.float32), k_r], axis=-1)
    v = kv[..., MLA_NOPE:]
    y = causal_block_attention(q, k, v, (MLA_NOPE + MLA_ROPE) ** -0.5)
    return from_heads(y).astype(c_q.dtype)


def setup_inputs(seed: int = 0) -> dict:
    key = jax.random.key(seed)
    ks = jax.random.split(key, 18)
    f32 = jnp.float32

    def nrm(k, shape, scale):
        return jax.random.normal(k, shape, f32) * scale

    def gain(k, shape):
        return 1.0 + 0.02 * jax.random.normal(k, shape, f32)

    return {
        'x': nrm(ks[0], (BATCH, SEQ, D_MODEL), 1.0),
        'g_mix_norm': gain(ks[1], (DEPTH, D_MODEL)),
        'w_in': nrm(ks[2], (DEPTH, D_MODEL, N_IN), D_MODEL ** -0.5),
        'b_forget': 2.0 + 0.1 * jax.random.normal(ks[3], (DEPTH, N_HEADS), f32),
        'g_sgu': gain(ks[4], (DEPTH, GROUP_WIDTH)),
        'w_spatial': nrm(ks[5], (DEPTH, N_HEADS, CHUNK, CHUNK), CHUNK ** -0.5),
        'b_spatial': gain(ks[6], (DEPTH, N_HEADS, CHUNK)),
        'g_mla_q': gain(ks[7], (DEPTH, MLA_Q_LORA)),
        'w_uq': nrm(ks[8], (DEPTH, MLA_Q_LORA, N_HEADS * (MLA_NOPE + MLA_ROPE)), MLA_Q_LORA ** -0.5),
        'g_mla_kv': gain(ks[9], (DEPTH, MLA_KV_LORA)),
        'w_ukv': nrm(ks[10], (DEPTH, MLA_KV_LORA, N_HEADS * (MLA_NOPE + MLA_V)), MLA_KV_LORA ** -0.5),
        'g_group_out': gain(ks[11], (DEPTH, D_MIX)),
        'w_out': nrm(ks[12], (DEPTH, D_MIX, D_MODEL), D_MIX ** -0.5),
        'g_ffn_norm': gain(ks[13], (DEPTH, D_MODEL)),
        'w_up': nrm(ks[14], (DEPTH, D_MODEL, D_FF), D_MODEL ** -0.5),
        'w_down': nrm(ks[15], (DEPTH, D_FF, D_MODEL), D_FF ** -0.5),
        'g_final': gain(ks[16], (D_MODEL,)),
    }


def reference(x, g_mix_norm, w_in, b_forget, g_sgu, w_spatial, b_spatial, g_mla_q, w_uq, g_mla_kv, w_ukv,
              g_group_out, w_out, g_ffn_norm, w_up, w_down, g_final):
    b, s, _ = x.shape
    log_gamma = jnp.log1p(-jnp.exp2(-5.0 - jnp.arange(N_HEADS, dtype=jnp.float32)))
    for l in range(DEPTH):
        h = rms_norm(x, g_mix_norm[l])
        z = h @ w_in[l]
        (a_uv, b_q, b_k, b_v, b_g, c_q, c_k, c_v, c_f, d_cq, d_ckv, d_kr) = jnp.split(z, IN_SPLIT_POINTS, axis=-1)
        y_a = spatial_gating_chunked(a_uv, g_sgu[l], w_spatial[l], b_spatial[l])
        y_b = retention_mixer(b_q, b_k, b_v, b_g, log_gamma)
        y_c = forgetting_attention(c_q, c_k, c_v, c_f, b_forget[l])
        y_d = mla_mixer(d_cq, d_ckv, d_kr, g_mla_q[l], w_uq[l], g_mla_kv[l], w_ukv[l])
        y = jnp.stack([y_a, y_b, y_c, y_d], axis=2)
        y = rms_norm(y, g_group_out[l].reshape(N_GROUPS, GROUP_WIDTH)).reshape(b, s, D_MIX)
        x = x + y @ w_out[l]
        h = rms_norm(x, g_ffn_norm[l])
        x = x + jnp.square(jax.nn.relu(h @ w_up[l])) @ w_down[l]
    return rms_norm(x, g_final)
```

```python
import contextlib
import numpy as np
import concourse.bass as bass
import concourse.mybir as mybir
from concourse.alu_op_type import AluOpType as ALU
from concourse.bass_utils import run_bass_kernel_spmd

F32 = mybir.dt.float32
BF16 = mybir.dt.bfloat16
AF = mybir.ActivationFunctionType

import os
PARTS = os.environ.get("AB_PARTS", "ABCDT")
INTERLEAVE = os.environ.get("AB_INTERLEAVE", "1") == "1"
S = 4096
DM = 1024
EPS = 1e-6
NT = S // 512
ENGS = ("pe", "act", "dve", "pool", "sp")


class Op:
    __slots__ = ("eng", "fn", "deps", "dma", "stream", "sig", "signals", "prev")

    def __init__(self, eng, fn, dma=False, stream=None):
        self.eng = eng
        self.fn = fn
        self.deps = set()
        self.dma = dma
        self.stream = stream
        self.sig = None
        self.signals = False


class Sched:
    def __init__(self, nc):
        self.nc = nc
        self.ops = {e: [] for e in ENGS}
        self.last_w = {}
        self.readers = {}
        self.all = []
        self.alias = {}
        self.cur_barrier = None
        self.since_barrier = []

    def _expand(self, keys):
        out = []
        for k in keys:
            out.append(k)
            a = self.alias.get(k)
            if a is not None:
                out.extend(a)
        return out

    def op(self, eng, fn, reads=(), writes=(), dma=False, stream=None):
        reads = self._expand(reads)
        writes = self._expand(writes)
        o = Op(eng, fn, dma, stream)
        deps = set()
        for k in reads:
            w = self.last_w.get(k)
            if w is not None:
                deps.add(w)
        for k in writes:
            w = self.last_w.get(k)
            if w is not None:
                deps.add(w)
            for r in self.readers.get(k, ()):
                deps.add(r)
        if self.cur_barrier is not None:
            deps.add(self.cur_barrier)
        o.deps = deps
        self.since_barrier.append(o)
        for k in reads:
            self.readers.setdefault(k, []).append(o)
        for k in writes:
            self.last_w[k] = o
            self.readers[k] = []
        self.all.append(o)
        self.ops[eng].append(o)
        return o

    def barrier(self, fn):
        o = Op("dve", fn)
        last = {}
        deps = set()
        for p in self.since_barrier:
            if p.dma:
                deps.add(p)
            else:
                last[p.eng] = p
        deps.update(last.values())
        if self.cur_barrier is not None:
            deps.add(self.cur_barrier)
        o.deps = deps
        self.all.append(o)
        self.ops["dve"].append(o)
        self.cur_barrier = o
        self.since_barrier = []
        return o

    def dma(self, q, out, in_, reads=(), writes=(), stream="d0", **kw):
        return self.op(q, CALL("dma_start", out=out, in_=in_, **kw), reads, writes,
                       dma=True, stream=(q, stream))

    def emit(self, dma_pool=None):
        nc = self.nc
        dma_pool = dma_pool or {"sp": 24, "pool": 8, "act": 4, "dve": 2, "pe": 2}
        for o in self.all:
            if o.dma:
                o.signals = True
            for d in o.deps:
                if d.eng == "pe" and o.eng == "pe" and not d.dma:
                    continue
                d.signals = True
        stack = contextlib.ExitStack()
        sems = {}
        counts = {}

        def get_sem(name):
            if name not in sems:
                sems[name] = stack.enter_context(nc.semaphore("s%d" % len(sems)))
                counts[name] = 0
            return sems[name]

        ndma = {e: 0 for e in ENGS}
        prev_use = {}
        for o in self.all:
            if not o.signals:
                continue
            if o.dma:
                n = ndma[o.eng]
                ndma[o.eng] += 1
                name = ("dma", o.eng, n % dma_pool[o.eng])
                get_sem(name)
                o.prev = (name, counts[name]) if counts[name] > 0 else None
                counts[name] += 16
            else:
                name = o.eng
                get_sem(name)
                counts[name] += 1
            o.sig = (name, counts[name])
        engmap = {"pe": "tensor", "act": "scalar", "dve": "vector", "pool": "gpsimd", "sp": "sync"}
        final = [(nm, cnt) for nm, cnt in counts.items() if isinstance(nm, tuple)]
        with stack, nc.Block() as block:
            for eng in ENGS:
                ops = self.ops[eng]
                extra_final = final if eng == "sp" else []
                if not ops and not extra_final:
                    continue

                def body(e, ops=ops, eng=eng, extra_final=extra_final):
                    waited = {}
                    for o in ops:
                        need = {}
                        for d in o.deps:
                            if d.eng == "pe" and eng == "pe" and not d.dma:
                                continue
                            nm, val = d.sig
                            if val > need.get(nm, 0):
                                need[nm] = val
                        if o.dma and o.prev is not None:
                            nm, val = o.prev
                            if val > need.get(nm, 0):
                                need[nm] = val
                        for nm, val in need.items():
                            if waited.get(nm, 0) >= val:
                                continue
                            e.wait_ge(sems[nm], val)
                            waited[nm] = val
                        inst = o.fn(e)
                        if o.signals:
                            inst.then_inc(sems[o.sig[0]], 16 if o.dma else 1)
                    for nm, val in extra_final:
                        if waited.get(nm, 0) >= val:
                            continue
                        e.wait_ge(sems[nm], val)
                        waited[nm] = val

                getattr(block, engmap[eng])(body)


def CALL(name, *args, **kwargs):
    return lambda e: getattr(e, name)(*args, **kwargs)


class Ctx:
    def __init__(self, nc, st, prefix=""):
        self.nc = nc
        self.st = st
        self.n = 0
        self.prefix = prefix

    def sb(self, shape, dt, name=None):
        self.n += 1
        return self.st.enter_context(self.nc.sbuf_tensor(self.prefix + (name or ("t%d" % self.n)), list(shape), dt))

    def ps(self, shape, dt, name=None):
        self.n += 1
        return self.st.enter_context(self.nc.psum_tensor(self.prefix + (name or ("p%d" % self.n)), list(shape), dt))


PERM64 = np.concatenate([np.arange(32, 64), np.arange(0, 32)])
PERM32 = np.concatenate([np.arange(16, 32), np.arange(0, 16)])


def _rot_tables(dim, s=S):
    half = dim // 2
    inv = np.power(np.float32(10000.0), -np.arange(half, dtype=np.float32) / np.float32(half)).astype(np.float32)
    ang = (np.arange(s, dtype=np.float32)[:, None] * inv[None, :]).astype(np.float32)
    c = np.cos(ang.astype(np.float64)).T
    sn = np.sin(ang.astype(np.float64)).T
    cos = np.concatenate([c, c], 0)
    sin = np.concatenate([-sn, sn], 0)
    return cos.astype(np.float32), sin.astype(np.float32)


def consts_for_parity(p):
    cB, sB = _rot_tables(64)
    tabB = np.stack([np.concatenate([cB, cB], 0), np.concatenate([sB, sB], 0)])
    cD, sD = _rot_tables(32)
    z = np.zeros((64, S), np.float32)
    tabD = np.stack([np.concatenate([z, cD], 0), np.concatenate([z, sD], 0)])
    lg = np.log1p(-np.exp2(-5.0 - np.arange(4, dtype=np.float64)))
    j = np.arange(128, dtype=np.float64)
    decT = np.zeros((2, 128, 128), np.float32)
    qwt = np.zeros((128, 128), np.float32)
    kwt = np.zeros((128, 2), np.float32)
    cdv = np.zeros((128, 1), np.float32)
    for hl in range(2):
        g = lg[2 * p + hl]
        rel = j[None, :] - j[:, None]
        decT[hl] = np.where(rel >= 0, 0.125 * np.exp(np.maximum(rel, 0) * g), 0.0)
        qwt[hl * 64:(hl + 1) * 64, :] = np.exp((j + 1.0) * g)[None, :]
        kwt[:, hl] = 0.125 * np.exp((127 - j) * g)
        cdv[hl * 64:(hl + 1) * 64, 0] = np.exp(128 * g)
    sidx = np.arange(128)
    mask01T = (sidx[:, None] <= sidx[None, :]).astype(np.float32)
    maskneg = np.where(sidx[:, None] <= sidx[None, :], 0.0, -30000.0).astype(np.float32)
    ident = np.eye(128, dtype=np.float32)
    return dict(tabB=tabB, tabD=tabD, decT=decT, qwt=qwt, kwt=kwt, cdv=cdv, mask01T=mask01T,
                maskneg=maskneg, ident=ident)


O_AU, O_AV = 0, 256
O_BQ, O_BK, O_BV, O_BG = 512, 768, 1024, 1280
O_CQ, O_CK, O_CV, O_CF = 1536, 1792, 2048, 2304
O_DCQ, O_DCKV, O_DKR = 2308, 2564, 2692

FM = {}
_o = 0
for _n, _m in [("au", 128), ("bq", 128), ("bqs", 128), ("bk", 128), ("bks", 128), ("bg", 128),
               ("cq1", 64), ("ck1", 64), ("cf", 2),
               ("dcq0", 128), ("dcq1", 128), ("dckv", 128), ("dkr", 96), ("dkrs", 96)]:
    FM[_n] = (_o, _m)
    _o += _m
NFM = _o
NTM = 384


def ab_inputs(x_b, l, p, P, C):
    w = P["w_in"][l]
    hs = slice(128 * p, 128 * p + 128)

    def grp(o):
        return w[:, o:o + 256][:, hs]

    def swap64(m):
        return np.concatenate([m[:, 0:64][:, PERM64], m[:, 64:128][:, PERM64]], 1)

    bq, bk = grp(O_BQ), grp(O_BK)
    cq, ck = grp(O_CQ), grp(O_CK)
    kr = w[:, O_DKR:O_DKR + 32]
    z64 = np.zeros((DM, 64), np.float32)
    wfm = np.concatenate([
        grp(O_AU), bq, swap64(bq), bk, swap64(bk), grp(O_BG),
        cq[:, 64:128], ck[:, 64:128],
        w[:, O_CF + 2 * p:O_CF + 2 * p + 2],
        w[:, O_DCQ:O_DCQ + 128], w[:, O_DCQ + 128:O_DCQ + 256], w[:, O_DCKV:O_DCKV + 128],
        np.concatenate([cq[:, 0:64], kr], 1), np.concatenate([ck[:, 0:64], kr[:, PERM32]], 1)], 1)
    assert wfm.shape[1] == NFM
    wtm = np.concatenate([grp(O_AV), grp(O_BV), grp(O_CV)], 1)
    wuq = P["w_uq"][l].reshape(256, 4, 96)
    z = np.zeros((256, 64), np.float32)
    uq = []
    for hl in range(2):
        wh = wuq[:, 2 * p + hl, :]
        uq += [wh, np.concatenate([z, wh[:, 64:96][:, PERM32]], 1)]
    wuqc = np.concatenate(uq, 1)
    wukv = P["w_ukv"][l].reshape(128, 4, 128)
    wukvc = np.concatenate([wukv[:, 2 * p, 0:64], wukv[:, 2 * p + 1, 0:64],
                            wukv[:, 2 * p, 64:128], wukv[:, 2 * p + 1, 64:128]], 1)
    ws = P["w_spatial"][l][2 * p:2 * p + 2]
    d = dict(
        xT=np.ascontiguousarray(x_b.T),
        gmix=np.ascontiguousarray(P["g_mix_norm"][l].reshape(8, 128).T),
        wfm=np.ascontiguousarray(wfm), wtm=np.ascontiguousarray(wtm),
        bf=np.ascontiguousarray(P["b_forget"][l][2 * p:2 * p + 2].reshape(2, 1)),
        gsgu=np.ascontiguousarray(np.broadcast_to(P["g_sgu"][l][hs][None, :], (128, 128))),
        wsT=np.ascontiguousarray(ws.transpose(0, 2, 1)),
        bs=np.ascontiguousarray(P["b_spatial"][l][2 * p:2 * p + 2].reshape(1, 256)),
        gq=np.ascontiguousarray(P["g_mla_q"][l].reshape(2, 128).T),
        gkv=np.ascontiguousarray(P["g_mla_kv"][l].reshape(128, 1)),
        wuq=np.ascontiguousarray(wuqc), wukv=np.ascontiguousarray(wukvc),
    )
    d.update(C[p])
    return d


AB_IN_SHAPES = dict(
    xT=[DM, S], gmix=[128, 8], wfm=[DM, NFM], wtm=[DM, NTM], bf=[2, 1], gsgu=[128, 128],
    wsT=[2, 128, 128], bs=[1, 256], gq=[128, 2], gkv=[128, 1], wuq=[256, 384], wukv=[128, 256],
    tabB=[2, 128, S], tabD=[2, 96, S], decT=[2, 128, 128], qwt=[128, 128], kwt=[128, 2],
    cdv=[128, 1], mask01T=[128, 128], maskneg=[128, 128], ident=[128, 128])


def phase_ab(s, cx, D, xsrc, ydst, ntiles=NT, tag="ab"):
    nc = s.nc
    K = lambda *a: (tag,) + a

    wfm = cx.sb([128, 8, NFM], BF16)
    wtm = cx.sb([128, 8, NTM], BF16)
    WSPLIT = FM["cq1"][0]
    wfm_v = D["wfm"].rearrange("(k p) n -> p k n", p=128)
    s.dma("pool", wtm[:], D["wtm"].rearrange("(k p) n -> p k n", p=128), writes=[K("wtm")], stream="w")
    s.dma("pool", wfm[:, :, WSPLIT:NFM], wfm_v[:, :, WSPLIT:NFM], writes=[K("wfm", 1)], stream="w")
    s.dma("pool", wfm[:, :, 0:WSPLIT], wfm_v[:, :, 0:WSPLIT], writes=[K("wfm", 0)], stream="w")
    wuq = cx.sb([128, 2, 384], BF16)
    s.dma("pool", wuq[:], D["wuq"].rearrange("(k p) n -> p k n", p=128), writes=[K("wuq")], stream="w")
    wukv = cx.sb([128, 256], BF16)
    s.dma("pool", wukv[:], D["wukv"], writes=[K("wukv")], stream="w")
    ident = cx.sb([128, 128], BF16)
    s.dma("pool", ident[:], D["ident"], writes=[K("ident")], stream="w")
    maskneg = cx.sb([128, 128], BF16)
    s.dma("pool", maskneg[:], D["maskneg"], writes=[K("maskneg")], stream="w")

    def ld(name, shape, src=None, dt=F32):
        t = cx.sb(shape, dt)
        s.dma("sp", t[:], D[name] if src is None else src, writes=[K(name)], stream="c")
        return t

    gmix = ld("gmix", [128, 8])
    bf = ld("bf", [2, 1])
    gsgu = ld("gsgu", [128, 128])
    wsT = ld("wsT", [128, 2, 128], D["wsT"].rearrange("h s t -> s h t"))
    mask01 = ld("mask01T", [128, 128])
    bs = ld("bs", [1, 256])
    gq = ld("gq", [128, 2])
    gkv = ld("gkv", [128, 1])
    decT = ld("decT", [128, 2, 128], D["decT"].rearrange("h s t -> s h t"))
    qwt = ld("qwt", [128, 128])
    kwt = ld("kwt", [128, 2])
    cdv = ld("cdv", [128, 1])

    ones_bf = cx.sb([128, 128], BF16)
    s.op("dve", CALL("memset", ones_bf[:], 1.0), writes=[K("ones_bf")])
    ones_f = cx.sb([128, 128], F32)
    s.op("dve", CALL("memset", ones_f[:], 1.0), writes=[K("ones_f")])
    blk = cx.sb([128, 128], BF16)
    s.op("dve", CALL("memset", blk[:], 0.0), writes=[K("blk")])
    s.op("dve", CALL("memset", blk[0:64, 0:64], 1.0 / 64), writes=[K("blk")])
    s.op("dve", CALL("memset", blk[64:128, 64:128], 1.0 / 64), writes=[K("blk")])
    negbf = cx.sb([2, 1], F32)
    s.op("dve", CALL("tensor_scalar", out=negbf[:], in0=bf[:], scalar1=-1.0, scalar2=None, op0=ALU.mult),
         reads=[K("bf")], writes=[K("negbf")])
    wcT = cx.sb([128, 2, 128], BF16)
    for h in range(2):
        s.op("dve", CALL("tensor_tensor", out=wcT[:, h, :], in0=wsT[:, h, :], in1=mask01[:], op=ALU.mult),
             reads=[K("wsT"), K("mask01T")], writes=[K("wcT")])

    kc = [cx.sb([128, S], BF16) for _ in range(2)]
    vc = [cx.sb([128, 32, 128], BF16) for _ in range(2)]
    kd = [cx.sb([128, S], BF16) for _ in range(2)]
    vd = [cx.sb([128, 32, 128], BF16) for _ in range(2)]
    for t_, nm in ((vc, "vc"), (vd, "vd")):
        for h in range(2):
            s.op("pool", lambda e, t=t_[h]: e.memset(t[:], 0.0), writes=[K(nm, h, t2) for t2 in range(NT)])
            s.op("pool", CALL("memset", t_[h][:, :, 0:1], 1.0), writes=[K(nm, h, t2) for t2 in range(NT)])
    for h in range(2):
        s.op("pool", CALL("memset", kc[h][64:70, :], 1.0), writes=[K("kc", h, t2) for t2 in range(NT)])
    state_f = cx.sb([128, 64], F32)
    state_b = cx.sb([128, 2, 64], BF16)
    s.op("dve", CALL("memset", state_f[:], 0.0), writes=[K("state_f")])
    s.op("dve", CALL("memset", state_b[:], 0.0), writes=[K("state_b")])
    cum_prev = cx.sb([2, 1], F32)
    s.op("dve", CALL("memset", cum_prev[:], 0.0), writes=[K("cum_prev")])

    PB = [cx.ps([128, 512], F32) for _ in range(8)]
    pj_rr = [0]

    def pj_bank():
        b = pj_rr[0] % 2
        pj_rr[0] += 1
        return b

    st_rr = [0]

    def st_bank():
        b = 3 + st_rr[0] % 3
        st_rr[0] += 1
        return b

    xt = cx.sb([128, 8, 512], F32)
    hT = cx.sb([128, 8, 512], BF16)
    sq = hT
    XK = lambda k: K("xt", k)
    for k in range(8):
        s.alias[K("sq", k)] = [K("hT", k)]
    s.alias[K("osb", 0)] = [XK(0)]
    s.alias[K("osb", 1)] = [XK(1)]
    s.alias[K("cqf")] = [XK(3), XK(4)]
    s.alias[K("ckf")] = [XK(5)]
    s.alias[K("oB")] = [XK(6)]
    s.alias[K("cen")] = [XK(6)]
    s.alias[K("uT")] = [XK(7)]
    rstd = cx.sb([128, 512], F32)
    tB = cx.sb([128, 2, 512], F32)
    tD = cx.sb([96, 2, 512], F32)
    uT = xt[:, 7, :]
    gT = cx.sb([128, 512], F32)
    r1 = cx.sb([128, 512], F32)
    r2 = cx.sb([128, 512], F32)
    qB = cx.sb([128, 512], BF16)
    qwB = cx.sb([128, 512], BF16)
    kB = cx.sb([128, 2, 512], BF16)
    s.op("pool", CALL("memset", kB[:], 0.0), writes=[K("kB")])
    vtok = cx.sb([128, 4, NTM], BF16)
    avf = cx.sb([128, 128], F32)
    avn = cx.sb([128, 128], BF16)
    stats = cx.sb([128, 2, 6], F32)
    mv = cx.sb([128, 2, 2], F32)
    rs2 = cx.sb([128, 2], F32)
    ktok = cx.sb([128, 128], BF16)
    pB = cx.sb([128, 2, 128], BF16)
    oB = xt[:, 6, :]
    oBb = cx.sb([128, 512], BF16)
    cen = oB
    csq = cx.sb([128, 512], BF16)
    yt = [cx.sb([128, 512], F32) for _ in range(6)]
    qc = [[cx.sb([128, 512], BF16) for _ in range(2)] for _ in range(2)]
    qd = [[cx.sb([128, 512], BF16) for _ in range(2)] for _ in range(2)]
    for b_ in range(2):
        for h in range(2):
            s.op("pool", lambda e, t=qc[b_][h]: e.memset(t[64:70, :], 1.0), writes=[K("qc", b_, h)])
    fl = cx.sb([2, 512], F32)
    ones2 = cx.sb([2, 512], F32)
    s.op("dve", CALL("memset", ones2[:], 1.0), writes=[K("ones2")])
    cpos = cx.sb([2, 512], F32)
    csp = cx.sb([2, 3, 512], BF16)
    csn = cx.sb([2, 3, 512], BF16)
    cr = cx.sb([2, 512], F32)
    cqf = xt[:, 3:5, :]
    cqs = cx.sb([128, 2, 512], BF16)
    cqn = cx.sb([128, 2, 512], BF16)
    ckf = xt[:, 5, :]
    cks = cx.sb([128, 512], BF16)
    ckn = cx.sb([128, 512], BF16)
    pT = [cx.sb([128, 512], BF16) for _ in range(3)]
    osb = [xt[:, 0, :], xt[:, 1, :]]

    def mm(out, lhsT, rhs, start, stop, reads, writes):
        return s.op("pe", CALL("matmul", out, lhsT=lhsT, rhs=rhs, start=start, stop=stop),
                    reads=reads, writes=writes)

    def rms_rstd(dst, src_ps, n, reads, writes):
        s.op("act", CALL("activation", out=dst, in_=src_ps, func=AF.Ln, scale=1.0 / n, bias=EPS_AP[0][0:dst.shape[0], :]),
             reads=reads, writes=writes)
        s.op("act", CALL("activation", out=dst, in_=dst, func=AF.Exp, scale=-0.5), reads=writes, writes=writes)

    eps_t = cx.sb([128, 1], F32)
    s.op("dve", CALL("memset", eps_t[:], EPS), writes=[K("eps")])
    EPS_AP = [eps_t]
    one_t = cx.sb([128, 1], F32)
    s.op("dve", CALL("memset", one_t[:], 1.0), writes=[K("one_t")])

    def project(i):
        tok = slice(i * 512, (i + 1) * 512)
        qb = i % 2
        s.dma("sp", xt[:], xsrc(i), writes=[XK(k) for k in range(8)], stream="x")
        s.dma("sp", tB[:], D["tabB"][:, :, tok].rearrange("c p t -> p c t"), writes=[K("tB")], stream="t")
        s.dma("sp", tD[:], D["tabD"][:, :, tok].rearrange("c p t -> p c t"), writes=[K("tD")], stream="t")
        for k in range(8):
            if k % 2 == 0:
                s.op("act", CALL("activation", out=sq[:, k, :], in_=xt[:, k, :], func=AF.Square),
                     reads=[XK(k)], writes=[K("sq", k)])
            else:
                s.op("dve", CALL("tensor_tensor", out=sq[:, k, :], in0=xt[:, k, :], in1=xt[:, k, :], op=ALU.mult),
                     reads=[XK(k)], writes=[K("sq", k)])
        b = pj_bank()
        for k in range(8):
            mm(PB[b][:, :], ones_bf[:], sq[:, k, :], k == 0, k == 7, [K("sq", k), K("ones_bf")], [K("pb", b)])
        rms_rstd(rstd[:], PB[b][:, :], DM, [K("pb", b), K("eps")], [K("rstd")])
        for k in range(8):
            eng = "dve"
            if eng == "dve":
                s.op("dve", CALL("scalar_tensor_tensor", out=hT[:, k, :], in0=xt[:, k, :], scalar=gmix[:, k:k + 1],
                                                                   in1=rstd[:], op0=ALU.mult, op1=ALU.mult),
                     reads=[XK(k), K("gmix"), K("rstd")], writes=[K("hT", k)])
            else:
                s.op("pool", CALL("tensor_tensor", out=r1[:], in0=xt[:, k, :], in1=rstd[:], op=ALU.mult),
                     reads=[XK(k), K("rstd")], writes=[K("r1")])
                s.op("pool", CALL("tensor_scalar", out=hT[:, k, :], in0=r1[:], scalar1=gmix[:, k:k + 1], scalar2=None,
                                                             op0=ALU.mult),
                     reads=[K("r1"), K("gmix")], writes=[K("hT", k)])
        yield

        def proj(name):
            off, m = FM[name]
            b = pj_bank()
            for k in range(8):
                mm(PB[b][0:m, :], wfm[:, k, off:off + m], hT[:, k, :], k == 0, k == 7,
                   [K("wfm", 1 if off >= WSPLIT else 0), K("hT", k)], [K("pb", b)])
            return b

        hTk = [K("hT", k) for k in range(8)]
        for j in range(4):
            tb = pj_bank()
            for k in range(8):
                mm(PB[tb][:, 0:NTM], hT[:, k, j * 128:(j + 1) * 128], wtm[:, k, :], k == 0, k == 7,
                   [K("wtm"), K("hT", k)], [K("pb", tb)])
            s.op("act", CALL("activation", out=vtok[:, j, 0:128], in_=PB[tb][:, 0:128], func=AF.Gelu_apprx_tanh),
                 reads=[K("pb", tb)], writes=[K("vtok", j)])
            s.op("dve", CALL("tensor_copy", out=vtok[:, j, 128:384], in_=PB[tb][:, 128:384]),
                 reads=[K("pb", tb)], writes=[K("vtok", j)])
            blkid = i * 4 + j
            s.op("pool", CALL("tensor_copy", out=vc[0][:, blkid, 64:128], in_=vtok[:, j, 256:320]),
                 reads=[K("vtok", j)], writes=[K("vc", 0, i)])
            s.op("pool", CALL("tensor_copy", out=vc[1][:, blkid, 64:128], in_=vtok[:, j, 320:384]),
                 reads=[K("vtok", j)], writes=[K("vc", 1, i)])
            yield

        if "C" in PARTS:
            yield from mixer_c(i, proj, tok, qb)
        if "D" in PARTS:
            yield from mixer_d(i, proj, tok, qb)
        if "A" in PARTS:
            yield from mixer_a(i, proj)
        if "B" in PARTS:
            yield from mixer_b(i, proj, tok)

    def mixer_a(i, proj):
        b = proj("au")
        s.op("act", CALL("activation", out=uT[:], in_=PB[b][:, :], func=AF.Gelu_apprx_tanh),
             reads=[K("pb", b)], writes=[K("uT")])
        yield
        for j in range(4):
            s.op("dve", CALL("tensor_copy", out=avf[:], in_=vtok[:, j, 0:128]), reads=[K("vtok", j)], writes=[K("avf")])
            for h in range(2):
                s.op("dve", CALL("bn_stats", out=stats[:, h, :], in_=avf[:, h * 64:(h + 1) * 64]),
                     reads=[K("avf")], writes=[K("stats")])
                s.op("dve", CALL("bn_aggr", out=mv[:, h, :], in_=stats[:, h, :]), reads=[K("stats")], writes=[K("mv")])
            s.op("act", CALL("activation", out=rs2[:], in_=mv[:, :, 1], func=AF.Ln, bias=eps_t[:], scale=1.0),
                 reads=[K("mv"), K("eps")], writes=[K("rs2")])
            s.op("act", CALL("activation", out=rs2[:], in_=rs2[:], func=AF.Exp, scale=-0.5), reads=[K("rs2")], writes=[K("rs2")])
            for h in range(2):
                s.op("dve", CALL("tensor_scalar", out=avf[:, h * 64:(h + 1) * 64], in0=avf[:, h * 64:(h + 1) * 64],
                                                            scalar1=mv[:, h, 0:1], scalar2=rs2[:, h:h + 1],
                                                            op0=ALU.subtract, op1=ALU.mult),
                     reads=[K("avf"), K("mv"), K("rs2")], writes=[K("avf")])
            s.op("dve", CALL("tensor_tensor", out=avn[:], in0=avf[:], in1=gsgu[:], op=ALU.mult),
                 reads=[K("avf"), K("gsgu")], writes=[K("avn")])
            for h in range(2):
                ps = PB[2][h * 64:(h + 1) * 64, 384:512]
                mm(ps, avn[:, h * 64:(h + 1) * 64], wcT[:, h, :], True, False, [K("avn"), K("wcT")], [K("pb", 2)])
                s.op("pe", CALL("matmul", ps, lhsT=ones_f[0:1, 0:64], rhs=bs[0:1, h * 128:(h + 1) * 128],
                                                           start=False, stop=True),
                     reads=[K("ones_f"), K("bs")], writes=[K("pb", 2)])
            s.op("dve", CALL("tensor_tensor", out=yt[0][:, j * 128:(j + 1) * 128], in0=uT[:, j * 128:(j + 1) * 128],
                                                        in1=PB[2][:, 384:512], op=ALU.mult),
                 reads=[K("uT"), K("pb", 2)], writes=[K("yt", 0)])
            yield
        s.dma("sp", ydst(0, i), yt[0][:], reads=[K("yt", 0)], stream="y")

    def mixer_b(i, proj, tok):
        def rotary_fm(bq_, bs_, dst, tab, rows, extra=None):
            s.op("dve", CALL("tensor_tensor", out=r1[rows, :], in0=PB[bq_][rows, :], in1=tab[rows, 0, :], op=ALU.mult),
                 reads=[K("pb", bq_), K("tab", id(tab))], writes=[K("r1")])
            s.op("dve", CALL("tensor_tensor", out=r2[rows, :], in0=PB[bs_][rows, :], in1=tab[rows, 1, :], op=ALU.mult),
                 reads=[K("pb", bs_), K("tab", id(tab))], writes=[K("r2")])

        tabkeyB = K("tB")
        tabkeyD = K("tD")
        b1 = proj("bq")
        b2 = proj("bqs")
        s.op("dve", CALL("tensor_tensor", out=r1[:], in0=PB[b1][:, :], in1=tB[:, 0, :], op=ALU.mult),
             reads=[K("pb", b1), tabkeyB], writes=[K("r1")])
        s.op("dve", CALL("tensor_tensor", out=r2[:], in0=PB[b2][:, :], in1=tB[:, 1, :], op=ALU.mult),
             reads=[K("pb", b2), tabkeyB], writes=[K("r2")])
        s.op("pool", CALL("tensor_tensor", out=r1[:], in0=r1[:], in1=r2[:], op=ALU.add),
             reads=[K("r1"), K("r2")], writes=[K("r1")])
        s.op("act", CALL("copy", out=qB[:], in_=r1[:]), reads=[K("r1")], writes=[K("qB")])
        for j in range(4):
            s.op("pool", CALL("tensor_tensor", out=qwB[:, j * 128:(j + 1) * 128], in0=r1[:, j * 128:(j + 1) * 128],
                                                         in1=qwt[:], op=ALU.mult),
                 reads=[K("r1"), K("qwt")], writes=[K("qwB")])
        yield
        b1 = proj("bk")
        b2 = proj("bks")
        s.op("dve", CALL("tensor_tensor", out=r1[:], in0=PB[b1][:, :], in1=tB[:, 0, :], op=ALU.mult),
             reads=[K("pb", b1), tabkeyB], writes=[K("r1")])
        s.op("dve", CALL("tensor_tensor", out=r2[:], in0=PB[b2][:, :], in1=tB[:, 1, :], op=ALU.mult),
             reads=[K("pb", b2), tabkeyB], writes=[K("r2")])
        for h in range(2):
            hr = slice(h * 64, (h + 1) * 64)
            s.op("pool", CALL("tensor_tensor", out=kB[hr, h, :], in0=r1[hr, :], in1=r2[hr, :], op=ALU.add),
                 reads=[K("r1"), K("r2")], writes=[K("kB")])
        yield
        b = proj("bg")
        s.op("act", CALL("activation", out=gT[:], in_=PB[b][:, :], func=AF.Silu), reads=[K("pb", b)], writes=[K("gT")])
        yield
        for j in range(4):
            cs = slice(j * 128, (j + 1) * 128)
            for h in range(2):
                mm(PB[2][:, 384:512], kB[:, h, cs], ident[:], True, True, [K("kB"), K("ident")], [K("pb", 2)])
                s.op("act", CALL("activation", out=ktok[:, h * 64:(h + 1) * 64], in_=PB[2][:, 384 + h * 64:384 + (h + 1) * 64],
                                 func=AF.Copy, scale=kwt[:, h:h + 1]),
                     reads=[K("pb", 2), K("kwt")], writes=[K("ktok")])
            sbs = [st_bank(), st_bank()]
            for h in range(2):
                mm(PB[sbs[h]][:, 0:128], kB[:, h, cs], qB[:, cs], True, True, [K("kB"), K("qB")], [K("pb", sbs[h])])
            for h in range(2):
                s.op("dve", CALL("tensor_tensor", out=pB[:, h, :], in0=PB[sbs[h]][:, 0:128],
                                                                     in1=decT[:, h, :], op=ALU.mult),
                     reads=[K("pb", sbs[h]), K("decT")], writes=[K("pB", h)])
            bo = 2 if INTERLEAVE else 6
            for h in range(2):
                hr = slice(h * 64, (h + 1) * 64)
                ps = PB[bo][hr, 0:128] if INTERLEAVE else PB[bo][hr, cs]
                mm(ps, vtok[:, j, 128 + h * 64:128 + (h + 1) * 64], pB[:, h, :], True, False,
                   [K("vtok", j), K("pB", h)], [K("pb", bo)])
                mm(ps, state_b[:, h, :], qwB[:, cs], False, True, [K("state_b"), K("qwB")], [K("pb", bo)])
            if INTERLEAVE:
                s.op("dve", CALL("tensor_copy", out=oB[:, cs], in_=PB[bo][:, 0:128]), reads=[K("pb", bo)], writes=[K("oB")])
            for h in range(2):
                hr = slice(h * 64, (h + 1) * 64)
                mm(PB[2][hr, 448:512], ktok[:, hr], vtok[:, j, 128 + h * 64:128 + (h + 1) * 64], True, True,
                   [K("ktok"), K("vtok", j)], [K("pb", 2)])
            s.op("dve", CALL("scalar_tensor_tensor", out=state_f[:], in0=state_f[:], scalar=cdv[:, 0:1], in1=PB[2][:, 448:512],
                                                          op0=ALU.mult, op1=ALU.add),
                 reads=[K("state_f"), K("cdv"), K("pb", 2)], writes=[K("state_f")])
            for h in range(2):
                s.op("act", CALL("copy", out=state_b[h * 64:(h + 1) * 64, h, :], in_=state_f[h * 64:(h + 1) * 64, :]),
                     reads=[K("state_f")], writes=[K("state_b")])
            yield
        if INTERLEAVE:
            s.op("pool", CALL("tensor_copy", out=oBb[:], in_=oB[:]), reads=[K("oB")], writes=[K("oBb")])
        else:
            s.op("act", CALL("copy", out=oB[:], in_=PB[6][:, :]), reads=[K("pb", 6)], writes=[K("oB")])
            s.op("dve", CALL("tensor_copy", out=oBb[:], in_=PB[6][:, :]), reads=[K("pb", 6)], writes=[K("oBb")])
        b = pj_bank()
        mm(PB[b][:, :], blk[:], oBb[:], True, True, [K("blk"), K("oBb")], [K("pb", b)])
        s.op("dve", CALL("tensor_tensor", out=cen[:], in0=oB[:], in1=PB[b][:, :], op=ALU.subtract),
             reads=[K("oB"), K("pb", b)], writes=[K("cen")])
        s.op("act", CALL("activation", out=csq[:], in_=cen[:], func=AF.Square), reads=[K("cen")], writes=[K("csq")])
        b = pj_bank()
        mm(PB[b][:, :], blk[:], csq[:], True, True, [K("blk"), K("csq")], [K("pb", b)])
        rms_rstd(r2[:], PB[b][:, :], 1.0, [K("pb", b), K("eps")], [K("r2")])
        s.op("dve", CALL("tensor_tensor", out=cen[:], in0=cen[:], in1=r2[:], op=ALU.mult), reads=[K("cen"), K("r2")], writes=[K("cen")])
        s.op("pool", CALL("tensor_tensor", out=yt[1][:], in0=cen[:], in1=gT[:], op=ALU.mult),
             reads=[K("cen"), K("gT")], writes=[K("yt", 1)])
        s.dma("sp", ydst(1, i), yt[1][:], reads=[K("yt", 1)], stream="y")
        yield

    def mixer_c(i, proj, tok, qb):
        rr = slice(64, 96)
        b1 = proj("dkr")
        b2 = proj("dkrs")
        s.op("act", CALL("activation", out=qc[qb][0][0:64, :], in_=PB[b1][0:64, :], func=AF.Copy, scale=0.125),
             reads=[K("pb", b1)], writes=[K("qc", qb, 0)])
        s.op("act", CALL("copy", out=kc[0][0:64, tok], in_=PB[b2][0:64, :]), reads=[K("pb", b2)], writes=[K("kc", 0, i)])
        s.op("dve", CALL("tensor_tensor", out=r1[rr, :], in0=PB[b1][rr, :], in1=tD[rr, 0, :], op=ALU.mult),
             reads=[K("pb", b1), K("tD")], writes=[K("r1")])
        s.op("dve", CALL("tensor_tensor", out=r2[rr, :], in0=PB[b2][rr, :], in1=tD[rr, 1, :], op=ALU.mult),
             reads=[K("pb", b2), K("tD")], writes=[K("r2")])
        for h in range(2):
            s.op("pool", CALL("tensor_tensor", out=kd[h][rr, tok], in0=r1[rr, :], in1=r2[rr, :], op=ALU.add),
                 reads=[K("r1"), K("r2")], writes=[K("kd", h, i)])
        yield
        for h in range(1, 2):
            b = proj("cq%d" % h)
            s.op("act", CALL("activation", out=qc[qb][h][0:64, :], in_=PB[b][0:64, :], func=AF.Copy, scale=0.125),
                 reads=[K("pb", b)], writes=[K("qc", qb, h)])
            b = proj("ck%d" % h)
            s.op("act", CALL("copy", out=kc[h][0:64, tok], in_=PB[b][0:64, :]),
                 reads=[K("pb", b)], writes=[K("kc", h, i)])
            yield
        b = proj("cf")
        s.op("act", CALL("activation", out=fl[:], in_=PB[b][0:2, :], func=AF.Exp, scale=-1.0, bias=negbf[:]),
             reads=[K("pb", b), K("negbf")], writes=[K("fl")])
        s.op("act", CALL("activation", out=fl[:], in_=fl[:], func=AF.Ln, scale=1.0, bias=one_t[0:2, :]),
             reads=[K("fl"), K("one_t")], writes=[K("fl")])
        s.op("dve", CALL("tensor_tensor_scan", out=cpos[:], data0=ones2[:], data1=fl[:], initial=cum_prev[:],
                                                    op0=ALU.mult, op1=ALU.add),
             reads=[K("ones2"), K("fl"), K("cum_prev")], writes=[K("cpos")])
        s.op("dve", CALL("tensor_copy", out=cum_prev[:], in_=cpos[:, 511:512]), reads=[K("cpos")], writes=[K("cum_prev")])
        s.op("dve", CALL("tensor_copy", out=csp[:, 0, :], in_=cpos[:]), reads=[K("cpos")], writes=[K("csp")])
        s.op("dve", CALL("tensor_tensor", out=cr[:], in0=cpos[:], in1=csp[:, 0, :], op=ALU.subtract),
             reads=[K("cpos"), K("csp")], writes=[K("cr")])
        s.op("dve", CALL("tensor_copy", out=csp[:, 1, :], in_=cr[:]), reads=[K("cr")], writes=[K("csp")])
        s.op("dve", CALL("tensor_tensor", out=cr[:], in0=cr[:], in1=csp[:, 1, :], op=ALU.subtract),
             reads=[K("cr"), K("csp")], writes=[K("cr")])
        s.op("dve", CALL("tensor_copy", out=csp[:, 2, :], in_=cr[:]), reads=[K("cr")], writes=[K("csp")])
        s.op("dve", CALL("tensor_scalar", out=csn[:], in0=csp[:], scalar1=-1.0, scalar2=None, op0=ALU.mult),
             reads=[K("csp")], writes=[K("csn")])
        for h in range(2):
            s.dma("sp", kc[h][64:67, tok], csp[h:h + 1, :, :], reads=[K("csp")], writes=[K("kc", h, i)],
                  stream="a")
            s.dma("sp", qc[qb][h][67:70, :], csn[h:h + 1, :, :], reads=[K("csn")],
                  writes=[K("qc", qb, h)], stream="a")
        yield

    def mixer_d(i, proj, tok, qb):
        rr = slice(64, 96)
        tabkeyD = K("tD")
        for k2 in range(2):
            b = proj("dcq%d" % k2)
            s.op("act", CALL("copy", out=cqf[:, k2, :], in_=PB[b][:, :]), reads=[K("pb", b)], writes=[K("cqf")])
            s.op("act", CALL("activation", out=cqs[:, k2, :], in_=PB[b][:, :], func=AF.Square),
                 reads=[K("pb", b)], writes=[K("cqs")])
        b = pj_bank()
        for k2 in range(2):
            mm(PB[b][:, :], ones_bf[:], cqs[:, k2, :], k2 == 0, k2 == 1, [K("ones_bf"), K("cqs")], [K("pb", b)])
        rms_rstd(r1[:], PB[b][:, :], 256.0, [K("pb", b), K("eps")], [K("r1")])
        for k2 in range(2):
            s.op("dve", CALL("scalar_tensor_tensor", out=cqn[:, k2, :], in0=cqf[:, k2, :], scalar=gq[:, k2:k2 + 1],
                                                                 in1=r1[:], op0=ALU.mult, op1=ALU.mult),
                 reads=[K("cqf"), K("gq"), K("r1")], writes=[K("cqn")])
        yield
        b = proj("dckv")
        s.op("act", CALL("copy", out=ckf[:], in_=PB[b][:, :]), reads=[K("pb", b)], writes=[K("ckf")])
        s.op("act", CALL("activation", out=cks[:], in_=PB[b][:, :], func=AF.Square), reads=[K("pb", b)], writes=[K("cks")])
        b = pj_bank()
        mm(PB[b][:, :], ones_bf[:], cks[:], True, True, [K("ones_bf"), K("cks")], [K("pb", b)])
        rms_rstd(r2[:], PB[b][:, :], 128.0, [K("pb", b), K("eps")], [K("r2")])
        s.op("dve", CALL("scalar_tensor_tensor", out=ckn[:], in0=ckf[:], scalar=gkv[:, 0:1], in1=r2[:],
                                                      op0=ALU.mult, op1=ALU.mult),
             reads=[K("ckf"), K("gkv"), K("r2")], writes=[K("ckn")])
        yield
        for h in range(2):
            b1 = pj_bank()
            for k2 in range(2):
                mm(PB[b1][0:96, :], wuq[:, k2, h * 192:h * 192 + 96], cqn[:, k2, :], k2 == 0, k2 == 1,
                   [K("wuq"), K("cqn")], [K("pb", b1)])
            b2 = pj_bank()
            for k2 in range(2):
                mm(PB[b2][0:96, :], wuq[:, k2, h * 192 + 96:h * 192 + 192], cqn[:, k2, :], k2 == 0, k2 == 1,
                   [K("wuq"), K("cqn")], [K("pb", b2)])
            s.op("act", CALL("copy", out=qd[qb][h][0:64, :], in_=PB[b1][0:64, :]),
                 reads=[K("pb", b1)], writes=[K("qd", qb, h)])
            s.op("dve", CALL("tensor_tensor", out=r1[rr, :], in0=PB[b1][rr, :], in1=tD[rr, 0, :], op=ALU.mult),
                 reads=[K("pb", b1), tabkeyD], writes=[K("r1")])
            s.op("dve", CALL("tensor_tensor", out=r2[rr, :], in0=PB[b2][rr, :], in1=tD[rr, 1, :], op=ALU.mult),
                 reads=[K("pb", b2), tabkeyD], writes=[K("r2")])
            s.op("pool", CALL("tensor_tensor", out=qd[qb][h][rr, :], in0=r1[rr, :], in1=r2[rr, :], op=ALU.add),
                 reads=[K("r1"), K("r2")], writes=[K("qd", qb, h)])
            b = pj_bank()
            mm(PB[b][0:64, :], wukv[:, h * 64:(h + 1) * 64], ckn[:], True, True, [K("wukv"), K("ckn")], [K("pb", b)])
            s.op("act", CALL("copy", out=kd[h][0:64, tok], in_=PB[b][0:64, :]), reads=[K("pb", b)], writes=[K("kd", h, i)])
            yield
        for j in range(4):
            blkid = i * 4 + j
            mm(PB[2][:, 0:128], ckn[:, j * 128:(j + 1) * 128], wukv[:, 128:256], True, True, [K("wukv"), K("ckn")], [K("pb", 2)])
            s.op("act", CALL("copy", out=vd[0][:, blkid, 64:128], in_=PB[2][:, 0:64]), reads=[K("pb", 2)], writes=[K("vd", 0, i)])
            s.op("act", CALL("copy", out=vd[1][:, blkid, 64:128], in_=PB[2][:, 64:128]), reads=[K("pb", 2)], writes=[K("vd", 1, i)])
        yield

    att_ctr = [0]
    LOOK = 2
    DEFER = 6

    def attention(i):
        qb = i % 2
        nkb = 4 * (i + 1)
        cfg = ((qc, kc, vc, 70, 1.0, "kc", "vc", "qc"), (qd, kd, vd, 96, 96 ** -0.5, "kd", "vd", "qd"))
        iters = []
        for g in range(2):
            if "CD"[g] not in PARTS or "T" not in PARTS:
                continue
            for h in range(2):
                for kb in range(nkb):
                    iters.append((g, h, kb))
        slot = {}
        pend = []

        def emit_s(t):
            g, h, kb = iters[t]
            qs, ks, vs, kdim, scale, kname, vname, qname = cfg[g]
            m = kb - 4 * i
            c0 = 128 * m if m > 0 else 0
            r = att_ctr[0] % 3
            att_ctr[0] += 1
            sb_ = 3 + r
            diag = m >= 0
            mm(PB[sb_][:, c0:512], ks[h][0:kdim, kb * 128:(kb + 1) * 128], qs[qb][h][0:kdim, c0:512], True, not diag,
               [K(kname, h, kb // 4), K(qname, qb, h)], [K("pb", sb_)])
            if diag:
                mm(PB[sb_][:, c0:c0 + 128], ident[:], maskneg[:], False, True, [K("ident"), K("maskneg")], [K("pb", sb_)])
            pt = pT[r]
            pk = K("pT", r)
            s.op("act", CALL("activation", out=pt[:, c0:512], in_=PB[sb_][:, c0:512], func=AF.Exp, scale=scale),
                 reads=[K("pb", sb_)], writes=[pk])
            slot[t] = (pt, pk, c0)

        def emit_pv(t):
            g, h, kb = iters[t]
            qs, ks, vs, kdim, scale, kname, vname, qname = cfg[g]
            pt, pk, c0 = slot.pop(t)
            ob = 6 + h
            mm(PB[ob][:, c0:512], vs[h][:, kb, :], pt[:, c0:512], kb == 0, kb == nkb - 1,
               [K(vname, h, kb // 4), pk], [K("pb", ob)])
            if kb == nkb - 1:
                o_ = osb[h]
                r_ = yt[2 + g + 2 * h]
                dr = slice(0, 1)
                s.op("act", CALL("copy", out=o_[:], in_=PB[ob][:, :]), reads=[K("pb", ob)], writes=[K("osb", h)])
                s.op("act", CALL("activation", out=r_[dr, :], in_=o_[dr, :], func=AF.Ln),
                     reads=[K("osb", h)], writes=[K("rc", g, h)])
                s.op("act", CALL("activation", out=r_[dr, :], in_=r_[dr, :], func=AF.Exp, scale=-1.0),
                     reads=[K("rc", g, h)], writes=[K("rc", g, h)])
                pend.append((t + DEFER, g, h))

        def emit_norm(g, h):
            o_ = osb[h]
            yi = 2 + g + 2 * h
            r_ = yt[yi]
            dr = slice(0, 1)
            yr = slice(64, 128)
            b = pj_bank()
            s.op("pe", CALL("matmul", PB[b][:, :], lhsT=ones_f[dr, :], rhs=r_[dr, :], start=True, stop=True),
                 reads=[K("ones_f"), K("rc", g, h)], writes=[K("pb", b)])
            s.op("dve", CALL("tensor_tensor", out=yt[yi][yr, :], in0=o_[yr, :], in1=PB[b][yr, :], op=ALU.mult),
                 reads=[K("osb", h), K("pb", b)], writes=[K("yt", yi)])
            s.dma("sp", ydst(2 + g, i)[h * 64:(h + 1) * 64, :], yt[yi][yr, :], reads=[K("yt", yi)], stream="y")

        n = len(iters)
        for t in range(n + LOOK):
            if t < n:
                emit_s(t)
            if t >= LOOK:
                emit_pv(t - LOOK)
            while pend and pend[0][0] <= t:
                _, g, h = pend.pop(0)
                emit_norm(g, h)
            yield
        for _, g, h in pend:
            emit_norm(g, h)
        yield

    def drain(gen):
        for _ in gen:
            pass

    def interleave(ga, na, gp, np_):
        ca = cp = 0
        a_done = p_done = False
        while not (a_done and p_done):
            take_a = (not a_done) and (p_done or ca * np_ <= cp * na)
            if take_a:
                try:
                    next(ga)
                    ca += 1
                except StopIteration:
                    a_done = True
            else:
                try:
                    next(gp)
                    cp += 1
                except StopIteration:
                    p_done = True

    NP_STEPS = 40
    drain(project(0))
    for i in range(ntiles):
        if i + 1 < ntiles:
            if INTERLEAVE:
                interleave(attention(i), 16 * (i + 1) + LOOK + 1, project(i + 1), NP_STEPS)
            else:
                drain(project(i + 1))
                drain(attention(i))
        else:
            drain(attention(i))


def build_ab(ntiles=NT):
    nc = bass.Bass("TRN2", target_bir_lowering=False)
    D = {n: nc.dram_tensor(n, sh, F32, kind="ExternalInput").ap() for n, sh in AB_IN_SHAPES.items()}
    y = nc.dram_tensor("y", [4, 128, S], F32, kind="ExternalOutput").ap()
    with contextlib.ExitStack() as st:
        cx = Ctx(nc, st)
        s = Sched(nc)
        xv = D["xT"].rearrange("(k p) t -> p k t", p=128)
        phase_ab(s, cx, D, lambda i: xv[:, :, i * 512:(i + 1) * 512], lambda g, i: y[g, :, i * 512:(i + 1) * 512], ntiles)
        s.emit()
    return nc


TC = 256
NTC = 2048 // TC

C_IN_SHAPES = dict(yT=[8, 128, 2048], xT=[DM, 2048], ggo=[128, 8], gffn=[128, 8], gfin=[128, 8],
                   w_out=[DM, DM], w_up=[DM, 4 * DM], w_down=[4 * DM, DM])


def c_inputs(y_parts, x_b, l, p, P):
    tk = slice(2048 * p, 2048 * (p + 1))
    yT = np.empty((8, 128, 2048), np.float32)
    for g in range(4):
        for q in range(2):
            yT[2 * g + q] = y_parts[q][g][:, tk]
    arr = lambda v: np.ascontiguousarray(v.reshape(8, 128).T)
    return dict(yT=yT, xT=np.ascontiguousarray(x_b[tk].T), ggo=arr(P["g_group_out"][l]), gffn=arr(P["g_ffn_norm"][l]),
                gfin=arr(P["g_final"]), w_out=P["w_out"][l], w_up=P["w_up"][l], w_down=P["w_down"][l])


def phase_c(s, cx, D, ysrc, xsrc, xdst, final, ntiles=NTC, tag="c"):
    K = lambda *a: (tag,) + a
    wo = cx.sb([128, 8, DM], BF16)
    wu = cx.sb([128, 8, 4 * DM], BF16)
    wd = cx.sb([128, 32, DM], BF16)
    wov = D["w_out"].rearrange("(k p) n -> p k n", p=128)
    for k0 in range(0, 8, 4):
        s.dma("pool", wo[:, k0:k0 + 4, :], wov[:, k0:k0 + 4, :], writes=[K("wo")], stream="w")
    wuv = D["w_up"].rearrange("(k p) n -> p k n", p=128)
    for c0 in range(0, 4 * DM, 2048):
        for k in range(8):
            s.dma("pool", wu[:, k, c0:c0 + 2048], wuv[:, k, c0:c0 + 2048], writes=[K("wu", c0 // 2048)], stream="w")
    wdv = D["w_down"].rearrange("(k p) n -> p k n", p=128)
    for k0 in range(0, 32, 4):
        s.dma("pool", wd[:, k0:k0 + 4, :], wdv[:, k0:k0 + 4, :], writes=[K("wd", k0 // 4)], stream="w")

    def ld(name):
        t = cx.sb([128, 8], F32)
        s.dma("sp", t[:], D[name], writes=[K(name)], stream="c")
        return t

    ggo, gffn = ld("ggo"), ld("gffn")
    gfin = ld("gfin") if final else None
    ones_bf = cx.sb([128, 128], BF16)
    s.op("dve", CALL("memset", ones_bf[:], 1.0), writes=[K("ones")])
    eps_t = cx.sb([128, 1], F32)
    s.op("dve", CALL("memset", eps_t[:], EPS), writes=[K("eps")])
    PB = [cx.ps([128, 512], F32) for _ in range(8)]
    rr = [0]

    def bank():
        b = rr[0] % 8
        rr[0] += 1
        return b

    yt = cx.sb([128, 8, TC], F32)
    xt = cx.sb([128, 8, TC], F32)
    sq = cx.sb([128, 8, TC], BF16)
    yn = cx.sb([128, 8, TC], BF16)
    h2 = cx.sb([128, 8, TC], BF16)
    a = cx.sb([128, 32, TC], BF16)
    rl = [cx.sb([128, TC], F32) for _ in range(2)]
    rstd = cx.sb([128, 4, TC], F32)
    ot = cx.sb([128, 8, TC], F32)

    def mm(out, lhsT, rhs, start, stop, reads, writes):
        return s.op("pe", CALL("matmul", out, lhsT=lhsT, rhs=rhs, start=start, stop=stop), reads=reads, writes=writes)

    def rstd_from(dst, ps, n, bkey, dkey):
        s.op("act", CALL("activation", out=dst, in_=ps, func=AF.Ln, scale=1.0 / n, bias=eps_t[:]),
             reads=[bkey, K("eps")], writes=[dkey])
        s.op("act", CALL("activation", out=dst, in_=dst, func=AF.Exp, scale=-0.5), reads=[dkey], writes=[dkey])

    def full_norm(src, gain, dst, dkey_fn, gkey):
        for k in range(8):
            s.op("act", CALL("activation", out=sq[:, k, :], in_=src[:, k, :], func=AF.Square), reads=[K("xt", k)], writes=[K("sq", k)])
        b = bank()
        for k in range(8):
            mm(PB[b][:, 0:TC], ones_bf[:], sq[:, k, :], k == 0, k == 7, [K("ones"), K("sq", k)], [K("pb", b)])
        rstd_from(rstd[:, 0, :], PB[b][:, 0:TC], DM, K("pb", b), K("rstd", 0))
        for k in range(8):
            s.op("dve", CALL("scalar_tensor_tensor", out=dst[:, k, :], in0=src[:, k, :], scalar=gain[:, k:k + 1],
                             in1=rstd[:, 0, :], op0=ALU.mult, op1=ALU.mult),
                 reads=[K("xt", k), K("rstd", 0), gkey], writes=[dkey_fn(k)])

    def ystage(tt):
        s.dma("sp", yt[:], ysrc(tt), writes=[K("yt", c) for c in range(8)], stream="y")
        for c in range(8):
            s.op("act", CALL("activation", out=sq[:, c, :], in_=yt[:, c, :], func=AF.Square), reads=[K("yt", c)], writes=[K("sq", c)])
        for g in range(4):
            b = bank()
            for q in range(2):
                mm(PB[b][:, 0:TC], ones_bf[:], sq[:, 2 * g + q, :], q == 0, q == 1, [K("ones"), K("sq", 2 * g + q)], [K("pb", b)])
            rstd_from(rstd[:, g, :], PB[b][:, 0:TC], 256.0, K("pb", b), K("rstd", g))
        for c in range(8):
            s.op("dve", CALL("scalar_tensor_tensor", out=yn[:, c, :], in0=yt[:, c, :], scalar=ggo[:, c:c + 1],
                             in1=rstd[:, c // 2, :], op0=ALU.mult, op1=ALU.mult),
                 reads=[K("yt", c), K("rstd", c // 2), K("ggo")], writes=[K("yn", c)])

    ystage(0)
    for tt in range(ntiles):
        s.dma("sp", xt[:], xsrc(tt), writes=[K("xt", k) for k in range(8)], stream="x")
        for oc in range(8):
            b = bank()
            for k in range(8):
                mm(PB[b][:, 0:TC], wo[:, k, oc * 128:(oc + 1) * 128], yn[:, k, :], k == 0, k == 7, [K("wo"), K("yn", k)], [K("pb", b)])
            s.op("dve", CALL("tensor_tensor", out=xt[:, oc, :], in0=xt[:, oc, :], in1=PB[b][:, 0:TC], op=ALU.add),
                 reads=[K("xt", oc), K("pb", b)], writes=[K("xt", oc)])
        for k in range(8):
            s.op("dve", CALL("tensor_scalar", out=h2[:, k, :], in0=xt[:, k, :], scalar1=gffn[:, k:k + 1], scalar2=None, op0=ALU.mult),
                 reads=[K("xt", k), K("gffn")], writes=[K("h2", k)])
        for k in range(8):
            s.op("act", CALL("activation", out=sq[:, k, :], in_=xt[:, k, :], func=AF.Square), reads=[K("xt", k)], writes=[K("sq", k)])

        def up_group(fc):
            b = bank()
            for k in range(8):
                mm(PB[b][:, 0:TC], wu[:, k, fc * 128:(fc + 1) * 128], h2[:, k, :], k == 0, k == 7, [K("wu", fc // 16), K("h2", k)], [K("pb", b)])
            return b

        def up_evac(fc, b):
            r_ = rl[fc % 2]
            s.op("dve", CALL("scalar_tensor_tensor", out=r_[:], in0=PB[b][:, 0:TC], scalar=0.0, in1=rstd[:, 0, :],
                             op0=ALU.max, op1=ALU.mult),
                 reads=[K("pb", b), K("rstd", 0)], writes=[K("rl", fc % 2)])
            s.op("act", CALL("activation", out=a[:, fc, :], in_=r_[:], func=AF.Square), reads=[K("rl", fc % 2)], writes=[K("a", fc)])

        AHEAD = 4
        ub = {fc: up_group(fc) for fc in range(AHEAD)}
        b = bank()
        for k in range(8):
            mm(PB[b][:, 0:TC], ones_bf[:], sq[:, k, :], k == 0, k == 7, [K("ones"), K("sq", k)], [K("pb", b)])
        rstd_from(rstd[:, 0, :], PB[b][:, 0:TC], DM, K("pb", b), K("rstd", 0))
        for fc in range(AHEAD):
            up_evac(fc, ub[fc])
        for fc in range(AHEAD, 32):
            up_evac(fc, up_group(fc))
        if tt + 1 < ntiles:
            ystage(tt + 1)
        for oc in range(8):
            b = bank()
            for fc in range(32):
                mm(PB[b][:, 0:TC], wd[:, fc, oc * 128:(oc + 1) * 128], a[:, fc, :], fc == 0, fc == 31, [K("wd", fc // 4), K("a", fc)], [K("pb", b)])
            if final:
                s.op("dve", CALL("tensor_tensor", out=xt[:, oc, :], in0=xt[:, oc, :], in1=PB[b][:, 0:TC], op=ALU.add),
                     reads=[K("xt", oc), K("pb", b)], writes=[K("xt", oc)])
            else:
                s.op("dve", CALL("tensor_tensor", out=ot[:, oc, :], in0=xt[:, oc, :], in1=PB[b][:, 0:TC], op=ALU.add),
                     reads=[K("xt", oc), K("pb", b)], writes=[K("ot", oc)])
        if final:
            full_norm(xt, gfin, ot, lambda k: K("ot", k), K("gfin"))
            s.dma("sp", xdst(tt), ot[:], reads=[K("ot", k) for k in range(8)], stream="o")
        else:
            s.dma("sp", xdst(tt), ot[:], reads=[K("ot", k) for k in range(8)], stream="o")


def build_c(final, ntiles=NTC):
    nc = bass.Bass("TRN2", target_bir_lowering=False)
    D = {n: nc.dram_tensor(n, sh, F32, kind="ExternalInput").ap() for n, sh in C_IN_SHAPES.items()}
    xo = nc.dram_tensor("xo", [DM, 2048], F32, kind="ExternalOutput").ap()
    with contextlib.ExitStack() as st:
        cx = Ctx(nc, st)
        s = Sched(nc)
        yv = D["yT"].rearrange("c p t -> p c t")
        xv = D["xT"].rearrange("(k p) t -> p k t", p=128)
        xov = xo.rearrange("(k p) t -> p k t", p=128)
        sl = lambda tt: slice(tt * TC, (tt + 1) * TC)
        phase_c(s, cx, D, lambda tt: yv[:, :, sl(tt)], lambda tt: xv[:, :, sl(tt)], lambda tt: xov[:, :, sl(tt)], final, ntiles)
        s.emit()
    return nc


SHARED_CONST = ("tabB", "tabD", "mask01T", "maskneg", "ident")
PARITY_CONST = ("decT", "qwt", "kwt", "cdv")
AB_LP = ("wfm", "wtm", "bf", "gsgu", "wsT", "bs", "wuq", "wukv")
AB_L = ("gmix", "gq", "gkv")
C_L = ("ggo", "gffn", "w_out", "w_up", "w_down")


def fused_input_shapes():
    sh = {"xT": [DM, S], "gfin": [128, 8]}
    for n in SHARED_CONST:
        sh[n] = AB_IN_SHAPES[n]
    for p in range(2):
        for n in PARITY_CONST:
            sh["%s_p%d" % (n, p)] = AB_IN_SHAPES[n]
    for l in range(2):
        for n in AB_L:
            sh["%s_l%d" % (n, l)] = AB_IN_SHAPES[n]
        for n in C_L:
            sh["%s_l%d" % (n, l)] = C_IN_SHAPES[n]
        for p in range(2):
            for n in AB_LP:
                sh["%s_l%dp%d" % (n, l, p)] = AB_IN_SHAPES[n]
    return sh


def fused_inputs(x_b, P, C):
    d = {"xT": np.ascontiguousarray(x_b.T), "gfin": np.ascontiguousarray(P["g_final"].reshape(8, 128).T)}
    for n in SHARED_CONST:
        d[n] = C[0][n]
    for p in range(2):
        for n in PARITY_CONST:
            d["%s_p%d" % (n, p)] = C[p][n]
    arr = lambda v: np.ascontiguousarray(v.reshape(8, 128).T)
    for l in range(2):
        d["ggo_l%d" % l] = arr(P["g_group_out"][l])
        d["gffn_l%d" % l] = arr(P["g_ffn_norm"][l])
        d["w_out_l%d" % l] = P["w_out"][l]
        d["w_up_l%d" % l] = P["w_up"][l]
        d["w_down_l%d" % l] = P["w_down"][l]
        for p in range(2):
            ab = ab_inputs(x_b, l, p, P, C)
            for n in AB_L:
                d["%s_l%d" % (n, l)] = ab[n]
            for n in AB_LP:
                d["%s_l%dp%d" % (n, l, p)] = ab[n]
    return d


def build_fused(nlayers=2, nt_ab=NT, nt_c=S // TC):
    nc = bass.Bass("TRN2", target_bir_lowering=False)
    I = {n: nc.dram_tensor(n, sh, F32, kind="ExternalInput").ap() for n, sh in fused_input_shapes().items()}
    xo = nc.dram_tensor("xo", [DM, S], F32, kind="ExternalOutput").ap()
    yscr = nc.dram_tensor("y_scratch", [8, 128, S], F32).ap()
    xscr = nc.dram_tensor("x_scratch", [DM, S], F32).ap()
    bd = nc.alloc_sbuf_tensor("bar_dummy", [1, 8], F32)
    s = Sched(nc)
    for l in range(nlayers):
        xin = I["xT"] if l == 0 else xscr
        xv = xin.rearrange("(k p) t -> p k t", p=128)
        for p in range(2):
            D = {n: I[n] for n in SHARED_CONST}
            D.update({n: I["%s_p%d" % (n, p)] for n in PARITY_CONST})
            D.update({n: I["%s_l%d" % (n, l)] for n in AB_L})
            D.update({n: I["%s_l%dp%d" % (n, l, p)] for n in AB_LP})
            with contextlib.ExitStack() as st:
                cx = Ctx(nc, st, "ab%d%d_" % (l, p))
                phase_ab(s, cx, D, lambda i: xv[:, :, i * 512:(i + 1) * 512],
                         lambda g, i, p=p: yscr[2 * g + p, :, i * 512:(i + 1) * 512], nt_ab, tag="ab%d%d" % (l, p))
            s.barrier(CALL("memset", bd[0:1, 0:8], 0.0))
        final = l == nlayers - 1
        D = {n: I["%s_l%d" % (n, l)] for n in C_L}
        D["gfin"] = I["gfin"]
        xdst = xo if final else xscr
        yv = yscr.rearrange("c p t -> p c t")
        xdv = xdst.rearrange("(k p) t -> p k t", p=128)
        sl = lambda tt: slice(tt * TC, (tt + 1) * TC)
        with contextlib.ExitStack() as st:
            cx = Ctx(nc, st, "c%d_" % l)
            phase_c(s, cx, D, lambda tt: yv[:, :, sl(tt)], lambda tt: xv[:, :, sl(tt)], lambda tt: xdv[:, :, sl(tt)],
                    final, nt_c, tag="c%d" % l)
        s.barrier(CALL("memset", bd[0:1, 0:8], 0.0))
    s.emit()
    return nc


def kernel(**inputs):
    P = {k: np.asarray(v, dtype=np.float32) for k, v in inputs.items()}
    x = P["x"]
    C = [consts_for_parity(p) for p in range(2)]
    nc = build_fused()
    per_b = [fused_inputs(x[b], P, C) for b in range(4)]
    in_maps = [per_b[c % 4] for c in range(8)]
    res = run_bass_kernel_spmd(nc, in_maps, core_ids=list(range(8)))
    out = np.empty_like(x)
    for b in range(4):
        out[b] = np.asarray(res.results[b]["xo"]).T
    return out.astype(np.float32)
```

```python
import contextlib
import numpy as np
import concourse.bass as bass
import concourse.mybir as mybir
from concourse.alu_op_type import AluOpType as ALU
from concourse.bass_utils import run_bass_kernel_spmd

F32 = mybir.dt.float32
BF16 = mybir.dt.bfloat16
AF = mybir.ActivationFunctionType

import os
PARTS = os.environ.get("AB_PARTS", "ABCDT")
INTERLEAVE = os.environ.get("AB_INTERLEAVE", "1") == "1"
S = 4096
DM = 1024
EPS = 1e-6
NT = S // 512
ENGS = ("pe", "act", "dve", "pool", "sp")


class Op:
    __slots__ = ("eng", "fn", "deps", "dma", "stream", "sig", "signals", "prev")

    def __init__(self, eng, fn, dma=False, stream=None):
        self.eng = eng
        self.fn = fn
        self.deps = set()
        self.dma = dma
        self.stream = stream
        self.sig = None
        self.signals = False


class Sched:
    def __init__(self, nc):
        self.nc = nc
        self.ops = {e: [] for e in ENGS}
        self.last_w = {}
        self.readers = {}
        self.all = []
        self.alias = {}
        self.cur_barrier = None
        self.since_barrier = []

    def _expand(self, keys):
        out = []
        for k in keys:
            out.append(k)
            a = self.alias.get(k)
            if a is not None:
                out.extend(a)
        return out

    def op(self, eng, fn, reads=(), writes=(), dma=False, stream=None):
        reads = self._expand(reads)
        writes = self._expand(writes)
        o = Op(eng, fn, dma, stream)
        deps = set()
        for k in reads:
            w = self.last_w.get(k)
            if w is not None:
                deps.add(w)
        for k in writes:
            w = self.last_w.get(k)
            if w is not None:
                deps.add(w)
            for r in self.readers.get(k, ()):
                deps.add(r)
        if self.cur_barrier is not None:
            deps.add(self.cur_barrier)
        o.deps = deps
        self.since_barrier.append(o)
        for k in reads:
            self.readers.setdefault(k, []).append(o)
        for k in writes:
            self.last_w[k] = o
            self.readers[k] = []
        self.all.append(o)
        self.ops[eng].append(o)
        return o

    def barrier(self, fn):
        o = Op("dve", fn)
        last = {}
        deps = set()
        for p in self.since_barrier:
            if p.dma:
                deps.add(p)
            else:
                last[p.eng] = p
        deps.update(last.values())
        if self.cur_barrier is not None:
            deps.add(self.cur_barrier)
        o.deps = deps
        self.all.append(o)
        self.ops["dve"].append(o)
        self.cur_barrier = o
        self.since_barrier = []
        return o

    def dma(self, q, out, in_, reads=(), writes=(), stream="d0", **kw):
        return self.op(q, CALL("dma_start", out=out, in_=in_, **kw), reads, writes,
                       dma=True, stream=(q, stream))

    def emit(self, dma_pool=None):
        nc = self.nc
        dma_pool = dma_pool or {"sp": 24, "pool": 8, "act": 4, "dve": 2, "pe": 2}
        for o in self.all:
            if o.dma:
                o.signals = True
            for d in o.deps:
                if d.eng == "pe" and o.eng == "pe" and not d.dma:
                    continue
                d.signals = True
        stack = contextlib.ExitStack()
        sems = {}
        counts = {}

        def get_sem(name):
            if name not in sems:
                sems[name] = stack.enter_context(nc.semaphore("s%d" % len(sems)))
                counts[name] = 0
            return sems[name]

        ndma = {e: 0 for e in ENGS}
        prev_use = {}
        for o in self.all:
            if not o.signals:
                continue
            if o.dma:
                n = ndma[o.eng]
                ndma[o.eng] += 1
                name = ("dma", o.eng, n % dma_pool[o.eng])
                get_sem(name)
                o.prev = (name, counts[name]) if counts[name] > 0 else None
                counts[name] += 16
            else:
                name = o.eng
                get_sem(name)
                counts[name] += 1
            o.sig = (name, counts[name])
        engmap = {"pe": "tensor", "act": "scalar", "dve": "vector", "pool": "gpsimd", "sp": "sync"}
        final = [(nm, cnt) for nm, cnt in counts.items() if isinstance(nm, tuple)]
        with stack, nc.Block() as block:
            for eng in ENGS:
                ops = self.ops[eng]
                extra_final = final if eng == "sp" else []
                if not ops and not extra_final:
                    continue

                def body(e, ops=ops, eng=eng, extra_final=extra_final):
                    waited = {}
                    for o in ops:
                        need = {}
                        for d in o.deps:
                            if d.eng == "pe" and eng == "pe" and not d.dma:
                                continue
                            nm, val = d.sig
                            if val > need.get(nm, 0):
                                need[nm] = val
                        if o.dma and o.prev is not None:
                            nm, val = o.prev
                            if val > need.get(nm, 0):
                                need[nm] = val
                        for nm, val in need.items():
                            if waited.get(nm, 0) >= val:
                                continue
                            e.wait_ge(sems[nm], val)
                            waited[nm] = val
                        inst = o.fn(e)
                        if o.signals:
                            inst.then_inc(sems[o.sig[0]], 16 if o.dma else 1)
                    for nm, val in extra_final:
                        if waited.get(nm, 0) >= val:
                            continue
                        e.wait_ge(sems[nm], val)
                        waited[nm] = val

                getattr(block, engmap[eng])(body)


def CALL(name, *args, **kwargs):
    return lambda e: getattr(e, name)(*args, **kwargs)


class Ctx:
    def __init__(self, nc, st, prefix=""):
        self.nc = nc
        self.st = st
        self.n = 0
        self.prefix = prefix

    def sb(self, shape, dt, name=None):
        self.n += 1
        return self.st.enter_context(self.nc.sbuf_tensor(self.prefix + (name or ("t%d" % self.n)), list(shape), dt))

    def ps(self, shape, dt, name=None):
        self.n += 1
        return self.st.enter_context(self.nc.psum_tensor(self.prefix + (name or ("p%d" % self.n)), list(shape), dt))


PERM64 = np.concatenate([np.arange(32, 64), np.arange(0, 32)])
PERM32 = np.concatenate([np.arange(16, 32), np.arange(0, 16)])


def _rot_tables(dim, s=S):
    half = dim // 2
    inv = np.power(np.float32(10000.0), -np.arange(half, dtype=np.float32) / np.float32(half)).astype(np.float32)
    ang = (np.arange(s, dtype=np.float32)[:, None] * inv[None, :]).astype(np.float32)
    c = np.cos(ang.astype(np.float64)).T
    sn = np.sin(ang.astype(np.float64)).T
    cos = np.concatenate([c, c], 0)
    sin = np.concatenate([-sn, sn], 0)
    return cos.astype(np.float32), sin.astype(np.float32)


def consts_for_parity(p):
    cB, sB = _rot_tables(64)
    tabB = np.stack([np.concatenate([cB, cB], 0), np.concatenate([sB, sB], 0)])
    cD, sD = _rot_tables(32)
    z = np.zeros((64, S), np.float32)
    tabD = np.stack([np.concatenate([z, cD], 0), np.concatenate([z, sD], 0)])
    lg = np.log1p(-np.exp2(-5.0 - np.arange(4, dtype=np.float64)))
    j = np.arange(128, dtype=np.float64)
    decT = np.zeros((2, 128, 128), np.float32)
    qwt = np.zeros((128, 128), np.float32)
    kwt = np.zeros((128, 2), np.float32)
    cdv = np.zeros((128, 1), np.float32)
    for hl in range(2):
        g = lg[2 * p + hl]
        rel = j[None, :] - j[:, None]
        decT[hl] = np.where(rel >= 0, 0.125 * np.exp(np.maximum(rel, 0) * g), 0.0)
        qwt[hl * 64:(hl + 1) * 64, :] = np.exp((j + 1.0) * g)[None, :]
        kwt[:, hl] = 0.125 * np.exp((127 - j) * g)
        cdv[hl * 64:(hl + 1) * 64, 0] = np.exp(128 * g)
    sidx = np.arange(128)
    mask01T = (sidx[:, None] <= sidx[None, :]).astype(np.float32)
    maskneg = np.where(sidx[:, None] <= sidx[None, :], 0.0, -30000.0).astype(np.float32)
    ident = np.eye(128, dtype=np.float32)
    return dict(tabB=tabB, tabD=tabD, decT=decT, qwt=qwt, kwt=kwt, cdv=cdv, mask01T=mask01T,
                maskneg=maskneg, ident=ident)


O_AU, O_AV = 0, 256
O_BQ, O_BK, O_BV, O_BG = 512, 768, 1024, 1280
O_CQ, O_CK, O_CV, O_CF = 1536, 1792, 2048, 2304
O_DCQ, O_DCKV, O_DKR = 2308, 2564, 2692

FM = {}
_o = 0
for _n, _m in [("au", 128), ("bq", 128), ("bqs", 128), ("bk", 128), ("bks", 128), ("bg", 128),
               ("cq1", 64), ("ck1", 64), ("cf", 2),
               ("dcq0", 128), ("dcq1", 128), ("dckv", 128), ("dkr", 96), ("dkrs", 96)]:
    FM[_n] = (_o, _m)
    _o += _m
NFM = _o
NTM = 384


def ab_inputs(x_b, l, p, P, C):
    w = P["w_in"][l]
    hs = slice(128 * p, 128 * p + 128)

    def grp(o):
        return w[:, o:o + 256][:, hs]

    def swap64(m):
        return np.concatenate([m[:, 0:64][:, PERM64], m[:, 64:128][:, PERM64]], 1)

    bq, bk = grp(O_BQ), grp(O_BK)
    cq, ck = grp(O_CQ), grp(O_CK)
    kr = w[:, O_DKR:O_DKR + 32]
    z64 = np.zeros((DM, 64), np.float32)
    wfm = np.concatenate([
        grp(O_AU), bq, swap64(bq), bk, swap64(bk), grp(O_BG),
        cq[:, 64:128], ck[:, 64:128],
        w[:, O_CF + 2 * p:O_CF + 2 * p + 2],
        w[:, O_DCQ:O_DCQ + 128], w[:, O_DCQ + 128:O_DCQ + 256], w[:, O_DCKV:O_DCKV + 128],
        np.concatenate([cq[:, 0:64], kr], 1), np.concatenate([ck[:, 0:64], kr[:, PERM32]], 1)], 1)
    assert wfm.shape[1] == NFM
    wtm = np.concatenate([grp(O_AV), grp(O_BV), grp(O_CV)], 1)
    wuq = P["w_uq"][l].reshape(256, 4, 96)
    z = np.zeros((256, 64), np.float32)
    uq = []
    for hl in range(2):
        wh = wuq[:, 2 * p + hl, :]
        uq += [wh, np.concatenate([z, wh[:, 64:96][:, PERM32]], 1)]
    wuqc = np.concatenate(uq, 1)
    wukv = P["w_ukv"][l].reshape(128, 4, 128)
    wukvc = np.concatenate([wukv[:, 2 * p, 0:64], wukv[:, 2 * p + 1, 0:64],
                            wukv[:, 2 * p, 64:128], wukv[:, 2 * p + 1, 64:128]], 1)
    ws = P["w_spatial"][l][2 * p:2 * p + 2]
    d = dict(
        xT=np.ascontiguousarray(x_b.T),
        gmix=np.ascontiguousarray(P["g_mix_norm"][l].reshape(8, 128).T),
        wfm=np.ascontiguousarray(wfm), wtm=np.ascontiguousarray(wtm),
        bf=np.ascontiguousarray(P["b_forget"][l][2 * p:2 * p + 2].reshape(2, 1)),
        gsgu=np.ascontiguousarray(np.broadcast_to(P["g_sgu"][l][hs][None, :], (128, 128))),
        wsT=np.ascontiguousarray(ws.transpose(0, 2, 1)),
        bs=np.ascontiguousarray(P["b_spatial"][l][2 * p:2 * p + 2].reshape(1, 256)),
        gq=np.ascontiguousarray(P["g_mla_q"][l].reshape(2, 128).T),
        gkv=np.ascontiguousarray(P["g_mla_kv"][l].reshape(128, 1)),
        wuq=np.ascontiguousarray(wuqc), wukv=np.ascontiguousarray(wukvc),
    )
    d.update(C[p])
    return d


AB_IN_SHAPES = dict(
    xT=[DM, S], gmix=[128, 8], wfm=[DM, NFM], wtm=[DM, NTM], bf=[2, 1], gsgu=[128, 128],
    wsT=[2, 128, 128], bs=[1, 256], gq=[128, 2], gkv=[128, 1], wuq=[256, 384], wukv=[128, 256],
    tabB=[2, 128, S], tabD=[2, 96, S], decT=[2, 128, 128], qwt=[128, 128], kwt=[128, 2],
    cdv=[128, 1], mask01T=[128, 128], maskneg=[128, 128], ident=[128, 128])


def phase_ab(s, cx, D, xsrc, ydst, ntiles=NT, tag="ab"):
    nc = s.nc
    K = lambda *a: (tag,) + a

    wfm = cx.sb([128, 8, NFM], BF16)
    wtm = cx.sb([128, 8, NTM], BF16)
    WSPLIT = FM["cq1"][0]
    wfm_v = D["wfm"].rearrange("(k p) n -> p k n", p=128)
    s.dma("pool", wtm[:], D["wtm"].rearrange("(k p) n -> p k n", p=128), writes=[K("wtm")], stream="w")
    s.dma("pool", wfm[:, :, WSPLIT:NFM], wfm_v[:, :, WSPLIT:NFM], writes=[K("wfm", 1)], stream="w")
    s.dma("pool", wfm[:, :, 0:WSPLIT], wfm_v[:, :, 0:WSPLIT], writes=[K("wfm", 0)], stream="w")
    wuq = cx.sb([128, 2, 384], BF16)
    s.dma("pool", wuq[:], D["wuq"].rearrange("(k p) n -> p k n", p=128), writes=[K("wuq")], stream="w")
    wukv = cx.sb([128, 256], BF16)
    s.dma("pool", wukv[:], D["wukv"], writes=[K("wukv")], stream="w")
    ident = cx.sb([128, 128], BF16)
    s.dma("pool", ident[:], D["ident"], writes=[K("ident")], stream="w")
    maskneg = cx.sb([128, 128], BF16)
    s.dma("pool", maskneg[:], D["maskneg"], writes=[K("maskneg")], stream="w")

    def ld(name, shape, src=None, dt=F32):
        t = cx.sb(shape, dt)
        s.dma("sp", t[:], D[name] if src is None else src, writes=[K(name)], stream="c")
        return t

    gmix = ld("gmix", [128, 8])
    bf = ld("bf", [2, 1])
    gsgu = ld("gsgu", [128, 128])
    wsT = ld("wsT", [128, 2, 128], D["wsT"].rearrange("h s t -> s h t"))
    mask01 = ld("mask01T", [128, 128])
    bs = ld("bs", [1, 256])
    gq = ld("gq", [128, 2])
    gkv = ld("gkv", [128, 1])
    decT = ld("decT", [128, 2, 128], D["decT"].rearrange("h s t -> s h t"))
    qwt = ld("qwt", [128, 128])
    kwt = ld("kwt", [128, 2])
    cdv = ld("cdv", [128, 1])

    ones_bf = cx.sb([128, 128], BF16)
    s.op("dve", CALL("memset", ones_bf[:], 1.0), writes=[K("ones_bf")])
    ones_f = cx.sb([128, 128], F32)
    s.op("dve", CALL("memset", ones_f[:], 1.0), writes=[K("ones_f")])
    blk = cx.sb([128, 128], BF16)
    s.op("dve", CALL("memset", blk[:], 0.0), writes=[K("blk")])
    s.op("dve", CALL("memset", blk[0:64, 0:64], 1.0 / 64), writes=[K("blk")])
    s.op("dve", CALL("memset", blk[64:128, 64:128], 1.0 / 64), writes=[K("blk")])
    negbf = cx.sb([2, 1], F32)
    s.op("dve", CALL("tensor_scalar", out=negbf[:], in0=bf[:], scalar1=-1.0, scalar2=None, op0=ALU.mult),
         reads=[K("bf")], writes=[K("negbf")])
    wcT = cx.sb([128, 2, 128], BF16)
    for h in range(2):
        s.op("dve", CALL("tensor_tensor", out=wcT[:, h, :], in0=wsT[:, h, :], in1=mask01[:], op=ALU.mult),
             reads=[K("wsT"), K("mask01T")], writes=[K("wcT")])

    kc = [cx.sb([128, S], BF16) for _ in range(2)]
    vc = [cx.sb([128, 32, 128], BF16) for _ in range(2)]
    kd = [cx.sb([128, S], BF16) for _ in range(2)]
    vd = [cx.sb([128, 32, 128], BF16) for _ in range(2)]
    for t_, nm in ((vc, "vc"), (vd, "vd")):
        for h in range(2):
            s.op("pool", lambda e, t=t_[h]: e.memset(t[:], 0.0), writes=[K(nm, h, t2) for t2 in range(NT)])
            s.op("pool", CALL("memset", t_[h][:, :, 0:1], 1.0), writes=[K(nm, h, t2) for t2 in range(NT)])
    for h in range(2):
        s.op("pool", CALL("memset", kc[h][64:70, :], 1.0), writes=[K("kc", h, t2) for t2 in range(NT)])
    state_f = cx.sb([128, 64], F32)
    state_b = cx.sb([128, 2, 64], BF16)
    s.op("dve", CALL("memset", state_f[:], 0.0), writes=[K("state_f")])
    s.op("dve", CALL("memset", state_b[:], 0.0), writes=[K("state_b")])
    cum_prev = cx.sb([2, 1], F32)
    s.op("dve", CALL("memset", cum_prev[:], 0.0), writes=[K("cum_prev")])

    PB = [cx.ps([128, 512], F32) for _ in range(8)]
    pj_rr = [0]

    def pj_bank():
        b = pj_rr[0] % 2
        pj_rr[0] += 1
        return b

    st_rr = [0]

    def st_bank():
        b = 3 + st_rr[0] % 3
        st_rr[0] += 1
        return b

    xt = cx.sb([128, 8, 512], F32)
    hT = cx.sb([128, 8, 512], BF16)
    sq = hT
    XK = lambda k: K("xt", k)
    for k in range(8):
        s.alias[K("sq", k)] = [K("hT", k)]
    s.alias[K("osb", 0)] = [XK(0)]
    s.alias[K("osb", 1)] = [XK(1)]
    s.alias[K("cqf")] = [XK(3), XK(4)]
    s.alias[K("ckf")] = [XK(5)]
    s.alias[K("oB")] = [XK(6)]
    s.alias[K("cen")] = [XK(6)]
    s.alias[K("uT")] = [XK(7)]
    rstd = cx.sb([128, 512], F32)
    tB = cx.sb([128, 2, 512], F32)
    tD = cx.sb([96, 2, 512], F32)
    uT = xt[:, 7, :]
    gT = cx.sb([128, 512], F32)
    r1 = cx.sb([128, 512], F32)
    r2 = cx.sb([128, 512], F32)
    qB = cx.sb([128, 512], BF16)
    qwB = cx.sb([128, 512], BF16)
    kB = cx.sb([128, 2, 512], BF16)
    s.op("pool", CALL("memset", kB[:], 0.0), writes=[K("kB")])
    vtok = cx.sb([128, 4, NTM], BF16)
    avf = cx.sb([128, 128], F32)
    avn = cx.sb([128, 128], BF16)
    stats = cx.sb([128, 2, 6], F32)
    mv = cx.sb([128, 2, 2], F32)
    rs2 = cx.sb([128, 2], F32)
    ktok = cx.sb([128, 128], BF16)
    pB = cx.sb([128, 2, 128], BF16)
    oB = xt[:, 6, :]
    oBb = cx.sb([128, 512], BF16)
    cen = oB
    csq = cx.sb([128, 512], BF16)
    yt = [cx.sb([128, 512], F32) for _ in range(6)]
    qc = [[cx.sb([128, 512], BF16) for _ in range(2)] for _ in range(2)]
    qd = [[cx.sb([128, 512], BF16) for _ in range(2)] for _ in range(2)]
    for b_ in range(2):
        for h in range(2):
            s.op("pool", lambda e, t=qc[b_][h]: e.memset(t[64:70, :], 1.0), writes=[K("qc", b_, h)])
    fl = cx.sb([2, 512], F32)
    ones2 = cx.sb([2, 512], F32)
    s.op("dve", CALL("memset", ones2[:], 1.0), writes=[K("ones2")])
    cpos = cx.sb([2, 512], F32)
    csp = cx.sb([2, 3, 512], BF16)
    csn = cx.sb([2, 3, 512], BF16)
    cr = cx.sb([2, 512], F32)
    cqf = xt[:, 3:5, :]
    cqs = cx.sb([128, 2, 512], BF16)
    cqn = cx.sb([128, 2, 512], BF16)
    ckf = xt[:, 5, :]
    cks = cx.sb([128, 512], BF16)
    ckn = cx.sb([128, 512], BF16)
    pT = [cx.sb([128, 512], BF16) for _ in range(3)]
    osb = [xt[:, 0, :], xt[:, 1, :]]

    def mm(out, lhsT, rhs, start, stop, reads, writes):
        return s.op("pe", CALL("matmul", out, lhsT=lhsT, rhs=rhs, start=start, stop=stop),
                    reads=reads, writes=writes)

    def rms_rstd(dst, src_ps, n, reads, writes):
        s.op("act", CALL("activation", out=dst, in_=src_ps, func=AF.Ln, scale=1.0 / n, bias=EPS_AP[0][0:dst.shape[0], :]),
             reads=reads, writes=writes)
        s.op("act", CALL("activation", out=dst, in_=dst, func=AF.Exp, scale=-0.5), reads=writes, writes=writes)

    eps_t = cx.sb([128, 1], F32)
    s.op("dve", CALL("memset", eps_t[:], EPS), writes=[K("eps")])
    EPS_AP = [eps_t]
    one_t = cx.sb([128, 1], F32)
    s.op("dve", CALL("memset", one_t[:], 1.0), writes=[K("one_t")])

    def project(i):
        tok = slice(i * 512, (i + 1) * 512)
        qb = i % 2
        s.dma("sp", xt[:], xsrc(i), writes=[XK(k) for k in range(8)], stream="x")
        s.dma("sp", tB[:], D["tabB"][:, :, tok].rearrange("c p t -> p c t"), writes=[K("tB")], stream="t")
        s.dma("sp", tD[:], D["tabD"][:, :, tok].rearrange("c p t -> p c t"), writes=[K("tD")], stream="t")
        for k in range(8):
            if k % 2 == 0:
                s.op("act", CALL("activation", out=sq[:, k, :], in_=xt[:, k, :], func=AF.Square),
                     reads=[XK(k)], writes=[K("sq", k)])
            else:
                s.op("dve", CALL("tensor_tensor", out=sq[:, k, :], in0=xt[:, k, :], in1=xt[:, k, :], op=ALU.mult),
                     reads=[XK(k)], writes=[K("sq", k)])
        b = pj_bank()
        for k in range(8):
            mm(PB[b][:, :], ones_bf[:], sq[:, k, :], k == 0, k == 7, [K("sq", k), K("ones_bf")], [K("pb", b)])
        rms_rstd(rstd[:], PB[b][:, :], DM, [K("pb", b), K("eps")], [K("rstd")])
        for k in range(8):
            eng = "dve"
            if eng == "dve":
                s.op("dve", CALL("scalar_tensor_tensor", out=hT[:, k, :], in0=xt[:, k, :], scalar=gmix[:, k:k + 1],
                                                                   in1=rstd[:], op0=ALU.mult, op1=ALU.mult),
                     reads=[XK(k), K("gmix"), K("rstd")], writes=[K("hT", k)])
            else:
                s.op("pool", CALL("tensor_tensor", out=r1[:], in0=xt[:, k, :], in1=rstd[:], op=ALU.mult),
                     reads=[XK(k), K("rstd")], writes=[K("r1")])
                s.op("pool", CALL("tensor_scalar", out=hT[:, k, :], in0=r1[:], scalar1=gmix[:, k:k + 1], scalar2=None,
                                                             op0=ALU.mult),
                     reads=[K("r1"), K("gmix")], writes=[K("hT", k)])
        yield

        def proj(name):
            off, m = FM[name]
            b = pj_bank()
            for k in range(8):
                mm(PB[b][0:m, :], wfm[:, k, off:off + m], hT[:, k, :], k == 0, k == 7,
                   [K("wfm", 1 if off >= WSPLIT else 0), K("hT", k)], [K("pb", b)])
            return b

        hTk = [K("hT", k) for k in range(8)]
        for j in range(4):
            tb = pj_bank()
            for k in range(8):
                mm(PB[tb][:, 0:NTM], hT[:, k, j * 128:(j + 1) * 128], wtm[:, k, :], k == 0, k == 7,
                   [K("wtm"), K("hT", k)], [K("pb", tb)])
            s.op("act", CALL("activation", out=vtok[:, j, 0:128], in_=PB[tb][:, 0:128], func=AF.Gelu_apprx_tanh),
                 reads=[K("pb", tb)], writes=[K("vtok", j)])
            s.op("dve", CALL("tensor_copy", out=vtok[:, j, 128:384], in_=PB[tb][:, 128:384]),
                 reads=[K("pb", tb)], writes=[K("vtok", j)])
            blkid = i * 4 + j
            s.op("pool", CALL("tensor_copy", out=vc[0][:, blkid, 64:128], in_=vtok[:, j, 256:320]),
                 reads=[K("vtok", j)], writes=[K("vc", 0, i)])
            s.op("pool", CALL("tensor_copy", out=vc[1][:, blkid, 64:128], in_=vtok[:, j, 320:384]),
                 reads=[K("vtok", j)], writes=[K("vc", 1, i)])
            yield

        if "C" in PARTS:
            yield from mixer_c(i, proj, tok, qb)
        if "D" in PARTS:
            yield from mixer_d(i, proj, tok, qb)
        if "A" in PARTS:
            yield from mixer_a(i, proj)
        if "B" in PARTS:
            yield from mixer_b(i, proj, tok)

    def mixer_a(i, proj):
        b = proj("au")
        s.op("act", CALL("activation", out=uT[:], in_=PB[b][:, :], func=AF.Gelu_apprx_tanh),
             reads=[K("pb", b)], writes=[K("uT")])
        yield
        for j in range(4):
            s.op("dve", CALL("tensor_copy", out=avf[:], in_=vtok[:, j, 0:128]), reads=[K("vtok", j)], writes=[K("avf")])
            for h in range(2):
                s.op("dve", CALL("bn_stats", out=stats[:, h, :], in_=avf[:, h * 64:(h + 1) * 64]),
                     reads=[K("avf")], writes=[K("stats")])
                s.op("dve", CALL("bn_aggr", out=mv[:, h, :], in_=stats[:, h, :]), reads=[K("stats")], writes=[K("mv")])
            s.op("act", CALL("activation", out=rs2[:], in_=mv[:, :, 1], func=AF.Ln, bias=eps_t[:], scale=1.0),
                 reads=[K("mv"), K("eps")], writes=[K("rs2")])
            s.op("act", CALL("activation", out=rs2[:], in_=rs2[:], func=AF.Exp, scale=-0.5), reads=[K("rs2")], writes=[K("rs2")])
            for h in range(2):
                s.op("dve", CALL("tensor_scalar", out=avf[:, h * 64:(h + 1) * 64], in0=avf[:, h * 64:(h + 1) * 64],
                                                            scalar1=mv[:, h, 0:1], scalar2=rs2[:, h:h + 1],
                                                            op0=ALU.subtract, op1=ALU.mult),
                     reads=[K("avf"), K("mv"), K("rs2")], writes=[K("avf")])
            s.op("dve", CALL("tensor_tensor", out=avn[:], in0=avf[:], in1=gsgu[:], op=ALU.mult),
                 reads=[K("avf"), K("gsgu")], writes=[K("avn")])
            for h in range(2):
                ps = PB[2][h * 64:(h + 1) * 64, 384:512]
                mm(ps, avn[:, h * 64:(h + 1) * 64], wcT[:, h, :], True, False, [K("avn"), K("wcT")], [K("pb", 2)])
                s.op("pe", CALL("matmul", ps, lhsT=ones_f[0:1, 0:64], rhs=bs[0:1, h * 128:(h + 1) * 128],
                                                           start=False, stop=True),
                     reads=[K("ones_f"), K("bs")], writes=[K("pb", 2)])
            s.op("dve", CALL("tensor_tensor", out=yt[0][:, j * 128:(j + 1) * 128], in0=uT[:, j * 128:(j + 1) * 128],
                                                        in1=PB[2][:, 384:512], op=ALU.mult),
                 reads=[K("uT"), K("pb", 2)], writes=[K("yt", 0)])
            yield
        s.dma("sp", ydst(0, i), yt[0][:], reads=[K("yt", 0)], stream="y")

    def mixer_b(i, proj, tok):
        def rotary_fm(bq_, bs_, dst, tab, rows, extra=None):
            s.op("dve", CALL("tensor_tensor", out=r1[rows, :], in0=PB[bq_][rows, :], in1=tab[rows, 0, :], op=ALU.mult),
                 reads=[K("pb", bq_), K("tab", id(tab))], writes=[K("r1")])
            s.op("dve", CALL("tensor_tensor", out=r2[rows, :], in0=PB[bs_][rows, :], in1=tab[rows, 1, :], op=ALU.mult),
                 reads=[K("pb", bs_), K("tab", id(tab))], writes=[K("r2")])

        tabkeyB = K("tB")
        tabkeyD = K("tD")
        b1 = proj("bq")
        b2 = proj("bqs")
        s.op("dve", CALL("tensor_tensor", out=r1[:], in0=PB[b1][:, :], in1=tB[:, 0, :], op=ALU.mult),
             reads=[K("pb", b1), tabkeyB], writes=[K("r1")])
        s.op("dve", CALL("tensor_tensor", out=r2[:], in0=PB[b2][:, :], in1=tB[:, 1, :], op=ALU.mult),
             reads=[K("pb", b2), tabkeyB], writes=[K("r2")])
        s.op("pool", CALL("tensor_tensor", out=r1[:], in0=r1[:], in1=r2[:], op=ALU.add),
             reads=[K("r1"), K("r2")], writes=[K("r1")])
        s.op("act", CALL("copy", out=qB[:], in_=r1[:]), reads=[K("r1")], writes=[K("qB")])
        for j in range(4):
            s.op("pool", CALL("tensor_tensor", out=qwB[:, j * 128:(j + 1) * 128], in0=r1[:, j * 128:(j + 1) * 128],
                                                         in1=qwt[:], op=ALU.mult),
                 reads=[K("r1"), K("qwt")], writes=[K("qwB")])
        yield
        b1 = proj("bk")
        b2 = proj("bks")
        s.op("dve", CALL("tensor_tensor", out=r1[:], in0=PB[b1][:, :], in1=tB[:, 0, :], op=ALU.mult),
             reads=[K("pb", b1), tabkeyB], writes=[K("r1")])
        s.op("dve", CALL("tensor_tensor", out=r2[:], in0=PB[b2][:, :], in1=tB[:, 1, :], op=ALU.mult),
             reads=[K("pb", b2), tabkeyB], writes=[K("r2")])
        for h in range(2):
            hr = slice(h * 64, (h + 1) * 64)
            s.op("pool", CALL("tensor_tensor", out=kB[hr, h, :], in0=r1[hr, :], in1=r2[hr, :], op=ALU.add),
                 reads=[K("r1"), K("r2")], writes=[K("kB")])
        yield
        b = proj("bg")
        s.op("act", CALL("activation", out=gT[:], in_=PB[b][:, :], func=AF.Silu), reads=[K("pb", b)], writes=[K("gT")])
        yield
        for j in range(4):
            cs = slice(j * 128, (j + 1) * 128)
            for h in range(2):
                mm(PB[2][:, 384:512], kB[:, h, cs], ident[:], True, True, [K("kB"), K("ident")], [K("pb", 2)])
                s.op("act", CALL("activation", out=ktok[:, h * 64:(h + 1) * 64], in_=PB[2][:, 384 + h * 64:384 + (h + 1) * 64],
                                 func=AF.Copy, scale=kwt[:, h:h + 1]),
                     reads=[K("pb", 2), K("kwt")], writes=[K("ktok")])
            sbs = [st_bank(), st_bank()]
            for h in range(2):
                mm(PB[sbs[h]][:, 0:128], kB[:, h, cs], qB[:, cs], True, True, [K("kB"), K("qB")], [K("pb", sbs[h])])
            for h in range(2):
                s.op("dve", CALL("tensor_tensor", out=pB[:, h, :], in0=PB[sbs[h]][:, 0:128],
                                                                     in1=decT[:, h, :], op=ALU.mult),
                     reads=[K("pb", sbs[h]), K("decT")], writes=[K("pB", h)])
            bo = 2 if INTERLEAVE else 6
            for h in range(2):
                hr = slice(h * 64, (h + 1) * 64)
                ps = PB[bo][hr, 0:128] if INTERLEAVE else PB[bo][hr, cs]
                mm(ps, vtok[:, j, 128 + h * 64:128 + (h + 1) * 64], pB[:, h, :], True, False,
                   [K("vtok", j), K("pB", h)], [K("pb", bo)])
                mm(ps, state_b[:, h, :], qwB[:, cs], False, True, [K("state_b"), K("qwB")], [K("pb", bo)])
            if INTERLEAVE:
                s.op("dve", CALL("tensor_copy", out=oB[:, cs], in_=PB[bo][:, 0:128]), reads=[K("pb", bo)], writes=[K("oB")])
            for h in range(2):
                hr = slice(h * 64, (h + 1) * 64)
                mm(PB[2][hr, 448:512], ktok[:, hr], vtok[:, j, 128 + h * 64:128 + (h + 1) * 64], True, True,
                   [K("ktok"), K("vtok", j)], [K("pb", 2)])
            s.op("dve", CALL("scalar_tensor_tensor", out=state_f[:], in0=state_f[:], scalar=cdv[:, 0:1], in1=PB[2][:, 448:512],
                                                          op0=ALU.mult, op1=ALU.add),
                 reads=[K("state_f"), K("cdv"), K("pb", 2)], writes=[K("state_f")])
            for h in range(2):
                s.op("act", CALL("copy", out=state_b[h * 64:(h + 1) * 64, h, :], in_=state_f[h * 64:(h + 1) * 64, :]),
                     reads=[K("state_f")], writes=[K("state_b")])
            yield
        if INTERLEAVE:
            s.op("pool", CALL("tensor_copy", out=oBb[:], in_=oB[:]), reads=[K("oB")], writes=[K("oBb")])
        else:
            s.op("act", CALL("copy", out=oB[:], in_=PB[6][:, :]), reads=[K("pb", 6)], writes=[K("oB")])
            s.op("dve", CALL("tensor_copy", out=oBb[:], in_=PB[6][:, :]), reads=[K("pb", 6)], writes=[K("oBb")])
        b = pj_bank()
        mm(PB[b][:, :], blk[:], oBb[:], True, True, [K("blk"), K("oBb")], [K("pb", b)])
        s.op("dve", CALL("tensor_tensor", out=cen[:], in0=oB[:], in1=PB[b][:, :], op=ALU.subtract),
             reads=[K("oB"), K("pb", b)], writes=[K("cen")])
        s.op("act", CALL("activation", out=csq[:], in_=cen[:], func=AF.Square), reads=[K("cen")], writes=[K("csq")])
        b = pj_bank()
        mm(PB[b][:, :], blk[:], csq[:], True, True, [K("blk"), K("csq")], [K("pb", b)])
        rms_rstd(r2[:], PB[b][:, :], 1.0, [K("pb", b), K("eps")], [K("r2")])
        s.op("dve", CALL("tensor_tensor", out=cen[:], in0=cen[:], in1=r2[:], op=ALU.mult), reads=[K("cen"), K("r2")], writes=[K("cen")])
        s.op("pool", CALL("tensor_tensor", out=yt[1][:], in0=cen[:], in1=gT[:], op=ALU.mult),
             reads=[K("cen"), K("gT")], writes=[K("yt", 1)])
        s.dma("sp", ydst(1, i), yt[1][:], reads=[K("yt", 1)], stream="y")
        yield

    def mixer_c(i, proj, tok, qb):
        rr = slice(64, 96)
        b1 = proj("dkr")
        b2 = proj("dkrs")
        s.op("act", CALL("activation", out=qc[qb][0][0:64, :], in_=PB[b1][0:64, :], func=AF.Copy, scale=0.125),
             reads=[K("pb", b1)], writes=[K("qc", qb, 0)])
        s.op("act", CALL("copy", out=kc[0][0:64, tok], in_=PB[b2][0:64, :]), reads=[K("pb", b2)], writes=[K("kc", 0, i)])
        s.op("dve", CALL("tensor_tensor", out=r1[rr, :], in0=PB[b1][rr, :], in1=tD[rr, 0, :], op=ALU.mult),
             reads=[K("pb", b1), K("tD")], writes=[K("r1")])
        s.op("dve", CALL("tensor_tensor", out=r2[rr, :], in0=PB[b2][rr, :], in1=tD[rr, 1, :], op=ALU.mult),
             reads=[K("pb", b2), K("tD")], writes=[K("r2")])
        for h in range(2):
            s.op("pool", CALL("tensor_tensor", out=kd[h][rr, tok], in0=r1[rr, :], in1=r2[rr, :], op=ALU.add),
                 reads=[K("r1"), K("r2")], writes=[K("kd", h, i)])
        yield
        for h in range(1, 2):
            b = proj("cq%d" % h)
            s.op("act", CALL("activation", out=qc[qb][h][0:64, :], in_=PB[b][0:64, :], func=AF.Copy, scale=0.125),
                 reads=[K("pb", b)], writes=[K("qc", qb, h)])
            b = proj("ck%d" % h)
            s.op("act", CALL("copy", out=kc[h][0:64, tok], in_=PB[b][0:64, :]),
                 reads=[K("pb", b)], writes=[K("kc", h, i)])
            yield
        b = proj("cf")
        s.op("act", CALL("activation", out=fl[:], in_=PB[b][0:2, :], func=AF.Exp, scale=-1.0, bias=negbf[:]),
             reads=[K("pb", b), K("negbf")], writes=[K("fl")])
        s.op("act", CALL("activation", out=fl[:], in_=fl[:], func=AF.Ln, scale=1.0, bias=one_t[0:2, :]),
             reads=[K("fl"), K("one_t")], writes=[K("fl")])
        s.op("dve", CALL("tensor_tensor_scan", out=cpos[:], data0=ones2[:], data1=fl[:], initial=cum_prev[:],
                                                    op0=ALU.mult, op1=ALU.add),
             reads=[K("ones2"), K("fl"), K("cum_prev")], writes=[K("cpos")])
        s.op("dve", CALL("tensor_copy", out=cum_prev[:], in_=cpos[:, 511:512]), reads=[K("cpos")], writes=[K("cum_prev")])
        s.op("dve", CALL("tensor_copy", out=csp[:, 0, :], in_=cpos[:]), reads=[K("cpos")], writes=[K("csp")])
        s.op("dve", CALL("tensor_tensor", out=cr[:], in0=cpos[:], in1=csp[:, 0, :], op=ALU.subtract),
             reads=[K("cpos"), K("csp")], writes=[K("cr")])
        s.op("dve", CALL("tensor_copy", out=csp[:, 1, :], in_=cr[:]), reads=[K("cr")], writes=[K("csp")])
        s.op("dve", CALL("tensor_tensor", out=cr[:], in0=cr[:], in1=csp[:, 1, :], op=ALU.subtract),
             reads=[K("cr"), K("csp")], writes=[K("cr")])
        s.op("dve", CALL("tensor_copy", out=csp[:, 2, :], in_=cr[:]), reads=[K("cr")], writes=[K("csp")])
        s.op("dve", CALL("tensor_scalar", out=csn[:], in0=csp[:], scalar1=-1.0, scalar2=None, op0=ALU.mult),
             reads=[K("csp")], writes=[K("csn")])
        for h in range(2):
            s.dma("sp", kc[h][64:67, tok], csp[h:h + 1, :, :], reads=[K("csp")], writes=[K("kc", h, i)],
                  stream="a")
            s.dma("sp", qc[qb][h][67:70, :], csn[h:h + 1, :, :], reads=[K("csn")],
                  writes=[K("qc", qb, h)], stream="a")
        yield

    def mixer_d(i, proj, tok, qb):
        rr = slice(64, 96)
        tabkeyD = K("tD")
        for k2 in range(2):
            b = proj("dcq%d" % k2)
            s.op("act", CALL("copy", out=cqf[:, k2, :], in_=PB[b][:, :]), reads=[K("pb", b)], writes=[K("cqf")])
            s.op("act", CALL("activation", out=cqs[:, k2, :], in_=PB[b][:, :], func=AF.Square),
                 reads=[K("pb", b)], writes=[K("cqs")])
        b = pj_bank()
        for k2 in range(2):
            mm(PB[b][:, :], ones_bf[:], cqs[:, k2, :], k2 == 0, k2 == 1, [K("ones_bf"), K("cqs")], [K("pb", b)])
        rms_rstd(r1[:], PB[b][:, :], 256.0, [K("pb", b), K("eps")], [K("r1")])
        for k2 in range(2):
            s.op("dve", CALL("scalar_tensor_tensor", out=cqn[:, k2, :], in0=cqf[:, k2, :], scalar=gq[:, k2:k2 + 1],
                                                                 in1=r1[:], op0=ALU.mult, op1=ALU.mult),
                 reads=[K("cqf"), K("gq"), K("r1")], writes=[K("cqn")])
        yield
        b = proj("dckv")
        s.op("act", CALL("copy", out=ckf[:], in_=PB[b][:, :]), reads=[K("pb", b)], writes=[K("ckf")])
        s.op("act", CALL("activation", out=cks[:], in_=PB[b][:, :], func=AF.Square), reads=[K("pb", b)], writes=[K("cks")])
        b = pj_bank()
        mm(PB[b][:, :], ones_bf[:], cks[:], True, True, [K("ones_bf"), K("cks")], [K("pb", b)])
        rms_rstd(r2[:], PB[b][:, :], 128.0, [K("pb", b), K("eps")], [K("r2")])
        s.op("dve", CALL("scalar_tensor_tensor", out=ckn[:], in0=ckf[:], scalar=gkv[:, 0:1], in1=r2[:],
                                                      op0=ALU.mult, op1=ALU.mult),
             reads=[K("ckf"), K("gkv"), K("r2")], writes=[K("ckn")])
        yield
        for h in range(2):
            b1 = pj_bank()
            for k2 in range(2):
                mm(PB[b1][0:96, :], wuq[:, k2, h * 192:h * 192 + 96], cqn[:, k2, :], k2 == 0, k2 == 1,
                   [K("wuq"), K("cqn")], [K("pb", b1)])
            b2 = pj_bank()
            for k2 in range(2):
                mm(PB[b2][0:96, :], wuq[:, k2, h * 192 + 96:h * 192 + 192], cqn[:, k2, :], k2 == 0, k2 == 1,
                   [K("wuq"), K("cqn")], [K("pb", b2)])
            s.op("act", CALL("copy", out=qd[qb][h][0:64, :], in_=PB[b1][0:64, :]),
                 reads=[K("pb", b1)], writes=[K("qd", qb, h)])
            s.op("dve", CALL("tensor_tensor", out=r1[rr, :], in0=PB[b1][rr, :], in1=tD[rr, 0, :], op=ALU.mult),
                 reads=[K("pb", b1), tabkeyD], writes=[K("r1")])
            s.op("dve", CALL("tensor_tensor", out=r2[rr, :], in0=PB[b2][rr, :], in1=tD[rr, 1, :], op=ALU.mult),
                 reads=[K("pb", b2), tabkeyD], writes=[K("r2")])
            s.op("pool", CALL("tensor_tensor", out=qd[qb][h][rr, :], in0=r1[rr, :], in1=r2[rr, :], op=ALU.add),
                 reads=[K("r1"), K("r2")], writes=[K("qd", qb, h)])
            b = pj_bank()
            mm(PB[b][0:64, :], wukv[:, h * 64:(h + 1) * 64], ckn[:], True, True, [K("wukv"), K("ckn")], [K("pb", b)])
            s.op("act", CALL("copy", out=kd[h][0:64, tok], in_=PB[b][0:64, :]), reads=[K("pb", b)], writes=[K("kd", h, i)])
            yield
        for j in range(4):
            blkid = i * 4 + j
            mm(PB[2][:, 0:128], ckn[:, j * 128:(j + 1) * 128], wukv[:, 128:256], True, True, [K("wukv"), K("ckn")], [K("pb", 2)])
            s.op("act", CALL("copy", out=vd[0][:, blkid, 64:128], in_=PB[2][:, 0:64]), reads=[K("pb", 2)], writes=[K("vd", 0, i)])
            s.op("act", CALL("copy", out=vd[1][:, blkid, 64:128], in_=PB[2][:, 64:128]), reads=[K("pb", 2)], writes=[K("vd", 1, i)])
        yield

    att_ctr = [0]
    LOOK = 2
    DEFER = 6

    def attention(i):
        qb = i % 2
        nkb = 4 * (i + 1)
        cfg = ((qc, kc, vc, 70, 1.0, "kc", "vc", "qc"), (qd, kd, vd, 96, 96 ** -0.5, "kd", "vd", "qd"))
        iters = []
        for g in range(2):
            if "CD"[g] not in PARTS or "T" not in PARTS:
                continue
            for h in range(2):
                for kb in range(nkb):
                    iters.append((g, h, kb))
        slot = {}
        pend = []

        def emit_s(t):
            g, h, kb = iters[t]
            qs, ks, vs, kdim, scale, kname, vname, qname = cfg[g]
            m = kb - 4 * i
            c0 = 128 * m if m > 0 else 0
            r = att_ctr[0] % 3
            att_ctr[0] += 1
            sb_ = 3 + r
            diag = m >= 0
            mm(PB[sb_][:, c0:512], ks[h][0:kdim, kb * 128:(kb + 1) * 128], qs[qb][h][0:kdim, c0:512], True, not diag,
               [K(kname, h, kb // 4), K(qname, qb, h)], [K("pb", sb_)])
            if diag:
                mm(PB[sb_][:, c0:c0 + 128], ident[:], maskneg[:], False, True, [K("ident"), K("maskneg")], [K("pb", sb_)])
            pt = pT[r]
            pk = K("pT", r)
            s.op("act", CALL("activation", out=pt[:, c0:512], in_=PB[sb_][:, c0:512], func=AF.Exp, scale=scale),
                 reads=[K("pb", sb_)], writes=[pk])
            slot[t] = (pt, pk, c0)

        def emit_pv(t):
            g, h, kb = iters[t]
            qs, ks, vs, kdim, scale, kname, vname, qname = cfg[g]
            pt, pk, c0 = slot.pop(t)
            ob = 6 + h
            mm(PB[ob][:, c0:512], vs[h][:, kb, :], pt[:, c0:512], kb == 0, kb == nkb - 1,
               [K(vname, h, kb // 4), pk], [K("pb", ob)])
            if kb == nkb - 1:
                o_ = osb[h]
                r_ = yt[2 + g + 2 * h]
                dr = slice(0, 1)
                s.op("act", CALL("copy", out=o_[:], in_=PB[ob][:, :]), reads=[K("pb", ob)], writes=[K("osb", h)])
                s.op("act", CALL("activation", out=r_[dr, :], in_=o_[dr, :], func=AF.Ln),
                     reads=[K("osb", h)], writes=[K("rc", g, h)])
                s.op("act", CALL("activation", out=r_[dr, :], in_=r_[dr, :], func=AF.Exp, scale=-1.0),
                     reads=[K("rc", g, h)], writes=[K("rc", g, h)])
                pend.append((t + DEFER, g, h))

        def emit_norm(g, h):
            o_ = osb[h]
            yi = 2 + g + 2 * h
            r_ = yt[yi]
            dr = slice(0, 1)
            yr = slice(64, 128)
            b = pj_bank()
            s.op("pe", CALL("matmul", PB[b][:, :], lhsT=ones_f[dr, :], rhs=r_[dr, :], start=True, stop=True),
                 reads=[K("ones_f"), K("rc", g, h)], writes=[K("pb", b)])
            s.op("dve", CALL("tensor_tensor", out=yt[yi][yr, :], in0=o_[yr, :], in1=PB[b][yr, :], op=ALU.mult),
                 reads=[K("osb", h), K("pb", b)], writes=[K("yt", yi)])
            s.dma("sp", ydst(2 + g, i)[h * 64:(h + 1) * 64, :], yt[yi][yr, :], reads=[K("yt", yi)], stream="y")

        n = len(iters)
        for t in range(n + LOOK):
            if t < n:
                emit_s(t)
            if t >= LOOK:
                emit_pv(t - LOOK)
            while pend and pend[0][0] <= t:
                _, g, h = pend.pop(0)
                emit_norm(g, h)
            yield
        for _, g, h in pend:
            emit_norm(g, h)
        yield

    def drain(gen):
        for _ in gen:
            pass

    def interleave(ga, na, gp, np_):
        ca = cp = 0
        a_done = p_done = False
        while not (a_done and p_done):
            take_a = (not a_done) and (p_done or ca * np_ <= cp * na)
            if take_a:
                try:
                    next(ga)
                    ca += 1
                except StopIteration:
                    a_done = True
            else:
                try:
                    next(gp)
                    cp += 1
                except StopIteration:
                    p_done = True

    NP_STEPS = 40
    drain(project(0))
    for i in range(ntiles):
        if i + 1 < ntiles:
            if INTERLEAVE:
                interleave(attention(i), 16 * (i + 1) + LOOK + 1, project(i + 1), NP_STEPS)
            else:
                drain(project(i + 1))
                drain(attention(i))
        else:
            drain(attention(i))


def build_ab(ntiles=NT):
    nc = bass.Bass("TRN2", target_bir_lowering=False)
    D = {n: nc.dram_tensor(n, sh, F32, kind="ExternalInput").ap() for n, sh in AB_IN_SHAPES.items()}
    y = nc.dram_tensor("y", [4, 128, S], F32, kind="ExternalOutput").ap()
    with contextlib.ExitStack() as st:
        cx = Ctx(nc, st)
        s = Sched(nc)
        xv = D["xT"].rearrange("(k p) t -> p k t", p=128)
        phase_ab(s, cx, D, lambda i: xv[:, :, i * 512:(i + 1) * 512], lambda g, i: y[g, :, i * 512:(i + 1) * 512], ntiles)
        s.emit()
    return nc


TC = 256
NTC = 2048 // TC

C_IN_SHAPES = dict(yT=[8, 128, 2048], xT=[DM, 2048], ggo=[128, 8], gffn=[128, 8], gfin=[128, 8],
                   w_out=[DM, DM], w_up=[DM, 4 * DM], w_down=[4 * DM, DM])


def c_inputs(y_parts, x_b, l, p, P):
    tk = slice(2048 * p, 2048 * (p + 1))
    yT = np.empty((8, 128, 2048), np.float32)
    for g in range(4):
        for q in range(2):
            yT[2 * g + q] = y_parts[q][g][:, tk]
    arr = lambda v: np.ascontiguousarray(v.reshape(8, 128).T)
    return dict(yT=yT, xT=np.ascontiguousarray(x_b[tk].T), ggo=arr(P["g_group_out"][l]), gffn=arr(P["g_ffn_norm"][l]),
                gfin=arr(P["g_final"]), w_out=P["w_out"][l], w_up=P["w_up"][l], w_down=P["w_down"][l])


def phase_c(s, cx, D, ysrc, xsrc, xdst, final, ntiles=NTC, tag="c"):
    K = lambda *a: (tag,) + a
    wo = cx.sb([128, 8, DM], BF16)
    wu = cx.sb([128, 8, 4 * DM], BF16)
    wd = cx.sb([128, 32, DM], BF16)
    wov = D["w_out"].rearrange("(k p) n -> p k n", p=128)
    for k0 in range(0, 8, 4):
        s.dma("pool", wo[:, k0:k0 + 4, :], wov[:, k0:k0 + 4, :], writes=[K("wo")], stream="w")
    wuv = D["w_up"].rearrange("(k p) n -> p k n", p=128)
    for c0 in range(0, 4 * DM, 1024):
        for k0 in range(0, 8, 2):
            s.dma("pool", wu[:, k0:k0 + 2, c0:c0 + 1024], wuv[:, k0:k0 + 2, c0:c0 + 1024], writes=[K("wu", c0 // 1024)], stream="w")
    wdv = D["w_down"].rearrange("(k p) n -> p k n", p=128)
    for k0 in range(0, 32, 4):
        s.dma("pool", wd[:, k0:k0 + 4, :], wdv[:, k0:k0 + 4, :], writes=[K("wd", k0 // 4)], stream="w")

    def ld(name):
        t = cx.sb([128, 8], F32)
        s.dma("sp", t[:], D[name], writes=[K(name)], stream="c")
        return t

    ggo, gffn = ld("ggo"), ld("gffn")
    gfin = ld("gfin") if final else None
    ones_bf = cx.sb([128, 128], BF16)
    s.op("dve", CALL("memset", ones_bf[:], 1.0), writes=[K("ones")])
    eps_t = cx.sb([128, 1], F32)
    s.op("dve", CALL("memset", eps_t[:], EPS), writes=[K("eps")])
    PB = [cx.ps([128, 512], F32) for _ in range(8)]
    rr = [0]

    def bank():
        b = rr[0] % 8
        rr[0] += 1
        return b

    yt = cx.sb([128, 8, TC], F32)
    xt = cx.sb([128, 8, TC], F32)
    sq = cx.sb([128, 8, TC], BF16)
    yn = cx.sb([128, 8, TC], BF16)
    h2 = cx.sb([128, 8, TC], BF16)
    a = cx.sb([128, 32, TC], BF16)
    rl = [cx.sb([128, TC], F32) for _ in range(2)]
    rstd = cx.sb([128, 4, TC], F32)
    ot = cx.sb([128, 8, TC], F32)

    def mm(out, lhsT, rhs, start, stop, reads, writes):
        return s.op("pe", CALL("matmul", out, lhsT=lhsT, rhs=rhs, start=start, stop=stop), reads=reads, writes=writes)

    def rstd_from(dst, ps, n, bkey, dkey):
        s.op("act", CALL("activation", out=dst, in_=ps, func=AF.Ln, scale=1.0 / n, bias=eps_t[:]),
             reads=[bkey, K("eps")], writes=[dkey])
        s.op("act", CALL("activation", out=dst, in_=dst, func=AF.Exp, scale=-0.5), reads=[dkey], writes=[dkey])

    def full_norm(src, gain, dst, dkey_fn, gkey):
        for k in range(8):
            s.op("act", CALL("activation", out=sq[:, k, :], in_=src[:, k, :], func=AF.Square), reads=[K("xt", k)], writes=[K("sq", k)])
        b = bank()
        for k in range(8):
            mm(PB[b][:, 0:TC], ones_bf[:], sq[:, k, :], k == 0, k == 7, [K("ones"), K("sq", k)], [K("pb", b)])
        rstd_from(rstd[:, 0, :], PB[b][:, 0:TC], DM, K("pb", b), K("rstd", 0))
        for k in range(8):
            s.op("dve", CALL("scalar_tensor_tensor", out=dst[:, k, :], in0=src[:, k, :], scalar=gain[:, k:k + 1],
                             in1=rstd[:, 0, :], op0=ALU.mult, op1=ALU.mult),
                 reads=[K("xt", k), K("rstd", 0), gkey], writes=[dkey_fn(k)])

    def ystage(tt):
        s.dma("sp", yt[:], ysrc(tt), writes=[K("yt", c) for c in range(8)], stream="y")
        for c in range(8):
            s.op("act", CALL("activation", out=sq[:, c, :], in_=yt[:, c, :], func=AF.Square), reads=[K("yt", c)], writes=[K("sq", c)])
        for g in range(4):
            b = bank()
            for q in range(2):
                mm(PB[b][:, 0:TC], ones_bf[:], sq[:, 2 * g + q, :], q == 0, q == 1, [K("ones"), K("sq", 2 * g + q)], [K("pb", b)])
            rstd_from(rstd[:, g, :], PB[b][:, 0:TC], 256.0, K("pb", b), K("rstd", g))
        for c in range(8):
            s.op("dve", CALL("scalar_tensor_tensor", out=yn[:, c, :], in0=yt[:, c, :], scalar=ggo[:, c:c + 1],
                             in1=rstd[:, c // 2, :], op0=ALU.mult, op1=ALU.mult),
                 reads=[K("yt", c), K("rstd", c // 2), K("ggo")], writes=[K("yn", c)])

    ystage(0)
    for tt in range(ntiles):
        s.dma("sp", xt[:], xsrc(tt), writes=[K("xt", k) for k in range(8)], stream="x")
        for oc in range(8):
            b = bank()
            for k in range(8):
                mm(PB[b][:, 0:TC], wo[:, k, oc * 128:(oc + 1) * 128], yn[:, k, :], k == 0, k == 7, [K("wo"), K("yn", k)], [K("pb", b)])
            s.op("dve", CALL("tensor_tensor", out=xt[:, oc, :], in0=xt[:, oc, :], in1=PB[b][:, 0:TC], op=ALU.add),
                 reads=[K("xt", oc), K("pb", b)], writes=[K("xt", oc)])
        for k in range(8):
            s.op("dve", CALL("tensor_scalar", out=h2[:, k, :], in0=xt[:, k, :], scalar1=gffn[:, k:k + 1], scalar2=None, op0=ALU.mult),
                 reads=[K("xt", k), K("gffn")], writes=[K("h2", k)])
        for k in range(8):
            s.op("act", CALL("activation", out=sq[:, k, :], in_=xt[:, k, :], func=AF.Square), reads=[K("xt", k)], writes=[K("sq", k)])

        def up_group(fc):
            b = bank()
            for k in range(8):
                mm(PB[b][:, 0:TC], wu[:, k, fc * 128:(fc + 1) * 128], h2[:, k, :], k == 0, k == 7, [K("wu", fc // 8), K("h2", k)], [K("pb", b)])
            return b

        def up_evac(fc, b):
            r_ = rl[fc % 2]
            s.op("dve", CALL("scalar_tensor_tensor", out=r_[:], in0=PB[b][:, 0:TC], scalar=0.0, in1=rstd[:, 0, :],
                             op0=ALU.max, op1=ALU.mult),
                 reads=[K("pb", b), K("rstd", 0)], writes=[K("rl", fc % 2)])
            s.op("act", CALL("activation", out=a[:, fc, :], in_=r_[:], func=AF.Square), reads=[K("rl", fc % 2)], writes=[K("a", fc)])

        AHEAD = 4
        ub = {fc: up_group(fc) for fc in range(AHEAD)}
        b = bank()
        for k in range(8):
            mm(PB[b][:, 0:TC], ones_bf[:], sq[:, k, :], k == 0, k == 7, [K("ones"), K("sq", k)], [K("pb", b)])
        rstd_from(rstd[:, 0, :], PB[b][:, 0:TC], DM, K("pb", b), K("rstd", 0))
        for fc in range(AHEAD):
            up_evac(fc, ub[fc])
        for fc in range(AHEAD, 32):
            up_evac(fc, up_group(fc))
        if tt + 1 < ntiles:
            ystage(tt + 1)
        for oc in range(8):
            b = bank()
            for fc in range(32):
                mm(PB[b][:, 0:TC], wd[:, fc, oc * 128:(oc + 1) * 128], a[:, fc, :], fc == 0, fc == 31, [K("wd", fc // 4), K("a", fc)], [K("pb", b)])
            if final:
                s.op("dve", CALL("tensor_tensor", out=xt[:, oc, :], in0=xt[:, oc, :], in1=PB[b][:, 0:TC], op=ALU.add),
                     reads=[K("xt", oc), K("pb", b)], writes=[K("xt", oc)])
            else:
                s.op("dve", CALL("tensor_tensor", out=ot[:, oc, :], in0=xt[:, oc, :], in1=PB[b][:, 0:TC], op=ALU.add),
                     reads=[K("xt", oc), K("pb", b)], writes=[K("ot", oc)])
        if final:
            full_norm(xt, gfin, ot, lambda k: K("ot", k), K("gfin"))
            s.dma("sp", xdst(tt), ot[:], reads=[K("ot", k) for k in range(8)], stream="o")
        else:
            s.dma("sp", xdst(tt), ot[:], reads=[K("ot", k) for k in range(8)], stream="o")


def build_c(final, ntiles=NTC):
    nc = bass.Bass("TRN2", target_bir_lowering=False)
    D = {n: nc.dram_tensor(n, sh, F32, kind="ExternalInput").ap() for n, sh in C_IN_SHAPES.items()}
    xo = nc.dram_tensor("xo", [DM, 2048], F32, kind="ExternalOutput").ap()
    with contextlib.ExitStack() as st:
        cx = Ctx(nc, st)
        s = Sched(nc)
        yv = D["yT"].rearrange("c p t -> p c t")
        xv = D["xT"].rearrange("(k p) t -> p k t", p=128)
        xov = xo.rearrange("(k p) t -> p k t", p=128)
        sl = lambda tt: slice(tt * TC, (tt + 1) * TC)
        phase_c(s, cx, D, lambda tt: yv[:, :, sl(tt)], lambda tt: xv[:, :, sl(tt)], lambda tt: xov[:, :, sl(tt)], final, ntiles)
        s.emit()
    return nc


SHARED_CONST = ("tabB", "tabD", "mask01T", "maskneg", "ident")
PARITY_CONST = ("decT", "qwt", "kwt", "cdv")
AB_LP = ("wfm", "wtm", "bf", "gsgu", "wsT", "bs", "wuq", "wukv")
AB_L = ("gmix", "gq", "gkv")
C_L = ("ggo", "gffn", "w_out", "w_up", "w_down")


def fused_input_shapes():
    sh = {"xT": [DM, S], "gfin": [128, 8]}
    for n in SHARED_CONST:
        sh[n] = AB_IN_SHAPES[n]
    for p in range(2):
        for n in PARITY_CONST:
            sh["%s_p%d" % (n, p)] = AB_IN_SHAPES[n]
    for l in range(2):
        for n in AB_L:
            sh["%s_l%d" % (n, l)] = AB_IN_SHAPES[n]
        for n in C_L:
            sh["%s_l%d" % (n, l)] = C_IN_SHAPES[n]
        for p in range(2):
            for n in AB_LP:
                sh["%s_l%dp%d" % (n, l, p)] = AB_IN_SHAPES[n]
    return sh


def fused_inputs(x_b, P, C):
    d = {"xT": np.ascontiguousarray(x_b.T), "gfin": np.ascontiguousarray(P["g_final"].reshape(8, 128).T)}
    for n in SHARED_CONST:
        d[n] = C[0][n]
    for p in range(2):
        for n in PARITY_CONST:
            d["%s_p%d" % (n, p)] = C[p][n]
    arr = lambda v: np.ascontiguousarray(v.reshape(8, 128).T)
    for l in range(2):
        d["ggo_l%d" % l] = arr(P["g_group_out"][l])
        d["gffn_l%d" % l] = arr(P["g_ffn_norm"][l])
        d["w_out_l%d" % l] = P["w_out"][l]
        d["w_up_l%d" % l] = P["w_up"][l]
        d["w_down_l%d" % l] = P["w_down"][l]
        for p in range(2):
            ab = ab_inputs(x_b, l, p, P, C)
            for n in AB_L:
                d["%s_l%d" % (n, l)] = ab[n]
            for n in AB_LP:
                d["%s_l%dp%d" % (n, l, p)] = ab[n]
    return d


def build_fused(nlayers=2, nt_ab=NT, nt_c=S // TC):
    nc = bass.Bass("TRN2", target_bir_lowering=False)
    I = {n: nc.dram_tensor(n, sh, F32, kind="ExternalInput").ap() for n, sh in fused_input_shapes().items()}
    xo = nc.dram_tensor("xo", [DM, S], F32, kind="ExternalOutput").ap()
    yscr = nc.dram_tensor("y_scratch", [8, 128, S], F32).ap()
    xscr = nc.dram_tensor("x_scratch", [DM, S], F32).ap()
    bd = nc.alloc_sbuf_tensor("bar_dummy", [1, 8], F32)
    s = Sched(nc)
    for l in range(nlayers):
        xin = I["xT"] if l == 0 else xscr
        xv = xin.rearrange("(k p) t -> p k t", p=128)
        for p in range(2):
            D = {n: I[n] for n in SHARED_CONST}
            D.update({n: I["%s_p%d" % (n, p)] for n in PARITY_CONST})
            D.update({n: I["%s_l%d" % (n, l)] for n in AB_L})
            D.update({n: I["%s_l%dp%d" % (n, l, p)] for n in AB_LP})
            with contextlib.ExitStack() as st:
                cx = Ctx(nc, st, "ab%d%d_" % (l, p))
                phase_ab(s, cx, D, lambda i: xv[:, :, i * 512:(i + 1) * 512],
                         lambda g, i, p=p: yscr[2 * g + p, :, i * 512:(i + 1) * 512], nt_ab, tag="ab%d%d" % (l, p))
            s.barrier(CALL("memset", bd[0:1, 0:8], 0.0))
        final = l == nlayers - 1
        D = {n: I["%s_l%d" % (n, l)] for n in C_L}
        D["gfin"] = I["gfin"]
        xdst = xo if final else xscr
        yv = yscr.rearrange("c p t -> p c t")
        xdv = xdst.rearrange("(k p) t -> p k t", p=128)
        sl = lambda tt: slice(tt * TC, (tt + 1) * TC)
        with contextlib.ExitStack() as st:
            cx = Ctx(nc, st, "c%d_" % l)
            phase_c(s, cx, D, lambda tt: yv[:, :, sl(tt)], lambda tt: xv[:, :, sl(tt)], lambda tt: xdv[:, :, sl(tt)],
                    final, nt_c, tag="c%d" % l)
        s.barrier(CALL("memset", bd[0:1, 0:8], 0.0))
    s.emit()
    return nc


def kernel(**inputs):
    P = {k: np.asarray(v, dtype=np.float32) for k, v in inputs.items()}
    x = P["x"]
    C = [consts_for_parity(p) for p in range(2)]
    nc = build_fused()
    per_b = [fused_inputs(x[b], P, C) for b in range(4)]
    in_maps = [per_b[c % 4] for c in range(8)]
    res = run_bass_kernel_spmd(nc, in_maps, core_ids=list(range(8)))
    out = np.empty_like(x)
    for b in range(4):
        out[b] = np.asarray(res.results[b]["xo"]).T
    return out.astype(np.float32)
```

```python
import contextlib
import numpy as np
import concourse.bass as bass
import concourse.mybir as mybir
from concourse.alu_op_type import AluOpType as ALU
from concourse.bass_utils import run_bass_kernel_spmd

F32 = mybir.dt.float32
BF16 = mybir.dt.bfloat16
AF = mybir.ActivationFunctionType

import os
PARTS = os.environ.get("AB_PARTS", "ABCDT")
INTERLEAVE = os.environ.get("AB_INTERLEAVE", "1") == "1"
S = 4096
DM = 1024
EPS = 1e-6
NT = S // 512
ENGS = ("pe", "act", "dve", "pool", "sp")


class Op:
    __slots__ = ("eng", "fn", "deps", "dma", "stream", "sig", "signals", "prev")

    def __init__(self, eng, fn, dma=False, stream=None):
        self.eng = eng
        self.fn = fn
        self.deps = set()
        self.dma = dma
        self.stream = stream
        self.sig = None
        self.signals = False


class Sched:
    def __init__(self, nc):
        self.nc = nc
        self.ops = {e: [] for e in ENGS}
        self.last_w = {}
        self.readers = {}
        self.all = []
        self.alias = {}
        self.cur_barrier = None
        self.since_barrier = []

    def _expand(self, keys):
        out = []
        for k in keys:
            out.append(k)
            a = self.alias.get(k)
            if a is not None:
                out.extend(a)
        return out

    def op(self, eng, fn, reads=(), writes=(), dma=False, stream=None):
        reads = self._expand(reads)
        writes = self._expand(writes)
        o = Op(eng, fn, dma, stream)
        deps = set()
        for k in reads:
            w = self.last_w.get(k)
            if w is not None:
                deps.add(w)
        for k in writes:
            w = self.last_w.get(k)
            if w is not None:
                deps.add(w)
            for r in self.readers.get(k, ()):
                deps.add(r)
        if self.cur_barrier is not None:
            deps.add(self.cur_barrier)
        o.deps = deps
        self.since_barrier.append(o)
        for k in reads:
            self.readers.setdefault(k, []).append(o)
        for k in writes:
            self.last_w[k] = o
            self.readers[k] = []
        self.all.append(o)
        self.ops[eng].append(o)
        return o

    def barrier(self, fn):
        o = Op("dve", fn)
        last = {}
        deps = set()
        for p in self.since_barrier:
            if p.dma:
                deps.add(p)
            else:
                last[p.eng] = p
        deps.update(last.values())
        if self.cur_barrier is not None:
            deps.add(self.cur_barrier)
        o.deps = deps
        self.all.append(o)
        self.ops["dve"].append(o)
        self.cur_barrier = o
        self.since_barrier = []
        return o

    def dma(self, q, out, in_, reads=(), writes=(), stream="d0", **kw):
        return self.op(q, CALL("dma_start", out=out, in_=in_, **kw), reads, writes,
                       dma=True, stream=(q, stream))

    def emit(self, dma_pool=None):
        nc = self.nc
        dma_pool = dma_pool or {"sp": 24, "pool": 8, "act": 4, "dve": 2, "pe": 2}
        for o in self.all:
            if o.dma:
                o.signals = True
            for d in o.deps:
                if d.eng == "pe" and o.eng == "pe" and not d.dma:
                    continue
                d.signals = True
        stack = contextlib.ExitStack()
        sems = {}
        counts = {}

        def get_sem(name):
            if name not in sems:
                sems[name] = stack.enter_context(nc.semaphore("s%d" % len(sems)))
                counts[name] = 0
            return sems[name]

        ndma = {e: 0 for e in ENGS}
        prev_use = {}
        for o in self.all:
            if not o.signals:
                continue
            if o.dma:
                n = ndma[o.eng]
                ndma[o.eng] += 1
                name = ("dma", o.eng, n % dma_pool[o.eng])
                get_sem(name)
                o.prev = (name, counts[name]) if counts[name] > 0 else None
                counts[name] += 16
            else:
                name = o.eng
                get_sem(name)
                counts[name] += 1
            o.sig = (name, counts[name])
        engmap = {"pe": "tensor", "act": "scalar", "dve": "vector", "pool": "gpsimd", "sp": "sync"}
        final = [(nm, cnt) for nm, cnt in counts.items() if isinstance(nm, tuple)]
        with stack, nc.Block() as block:
            for eng in ENGS:
                ops = self.ops[eng]
                extra_final = final if eng == "sp" else []
                if not ops and not extra_final:
                    continue

                def body(e, ops=ops, eng=eng, extra_final=extra_final):
                    waited = {}
                    for o in ops:
                        need = {}
                        for d in o.deps:
                            if d.eng == "pe" and eng == "pe" and not d.dma:
                                continue
                            nm, val = d.sig
                            if val > need.get(nm, 0):
                                need[nm] = val
                        if o.dma and o.prev is not None:
                            nm, val = o.prev
                            if val > need.get(nm, 0):
                                need[nm] = val
                        for nm, val in need.items():
                            if waited.get(nm, 0) >= val:
                                continue
                            e.wait_ge(sems[nm], val)
                            waited[nm] = val
                        inst = o.fn(e)
                        if o.signals:
                            inst.then_inc(sems[o.sig[0]], 16 if o.dma else 1)
                    for nm, val in extra_final:
                        if waited.get(nm, 0) >= val:
                            continue
                        e.wait_ge(sems[nm], val)
                        waited[nm] = val

                getattr(block, engmap[eng])(body)


def CALL(name, *args, **kwargs):
    return lambda e: getattr(e, name)(*args, **kwargs)


class Ctx:
    def __init__(self, nc, st, prefix=""):
        self.nc = nc
        self.st = st
        self.n = 0
        self.prefix = prefix

    def sb(self, shape, dt, name=None):
        self.n += 1
        return self.st.enter_context(self.nc.sbuf_tensor(self.prefix + (name or ("t%d" % self.n)), list(shape), dt))

    def ps(self, shape, dt, name=None):
        self.n += 1
        return self.st.enter_context(self.nc.psum_tensor(self.prefix + (name or ("p%d" % self.n)), list(shape), dt))


PERM64 = np.concatenate([np.arange(32, 64), np.arange(0, 32)])
PERM32 = np.concatenate([np.arange(16, 32), np.arange(0, 16)])


def _rot_tables(dim, s=S):
    half = dim // 2
    inv = np.power(np.float32(10000.0), -np.arange(half, dtype=np.float32) / np.float32(half)).astype(np.float32)
    ang = (np.arange(s, dtype=np.float32)[:, None] * inv[None, :]).astype(np.float32)
    c = np.cos(ang.astype(np.float64)).T
    sn = np.sin(ang.astype(np.float64)).T
    cos = np.concatenate([c, c], 0)
    sin = np.concatenate([-sn, sn], 0)
    return cos.astype(np.float32), sin.astype(np.float32)


def consts_for_parity(p):
    cB, sB = _rot_tables(64)
    tabB = np.stack([np.concatenate([cB, cB], 0), np.concatenate([sB, sB], 0)])
    cD, sD = _rot_tables(32)
    z = np.zeros((64, S), np.float32)
    tabD = np.stack([np.concatenate([z, cD], 0), np.concatenate([z, sD], 0)])
    lg = np.log1p(-np.exp2(-5.0 - np.arange(4, dtype=np.float64)))
    j = np.arange(128, dtype=np.float64)
    decT = np.zeros((2, 128, 128), np.float32)
    qwt = np.zeros((128, 128), np.float32)
    kwt = np.zeros((128, 2), np.float32)
    cdv = np.zeros((128, 1), np.float32)
    for hl in range(2):
        g = lg[2 * p + hl]
        rel = j[None, :] - j[:, None]
        decT[hl] = np.where(rel >= 0, 0.125 * np.exp(np.maximum(rel, 0) * g), 0.0)
        qwt[hl * 64:(hl + 1) * 64, :] = np.exp((j + 1.0) * g)[None, :]
        kwt[:, hl] = 0.125 * np.exp((127 - j) * g)
        cdv[hl * 64:(hl + 1) * 64, 0] = np.exp(128 * g)
    sidx = np.arange(128)
    mask01T = (sidx[:, None] <= sidx[None, :]).astype(np.float32)
    maskneg = np.where(sidx[:, None] <= sidx[None, :], 0.0, -30000.0).astype(np.float32)
    ident = np.eye(128, dtype=np.float32)
    return dict(tabB=tabB, tabD=tabD, decT=decT, qwt=qwt, kwt=kwt, cdv=cdv, mask01T=mask01T,
                maskneg=maskneg, ident=ident)


O_AU, O_AV = 0, 256
O_BQ, O_BK, O_BV, O_BG = 512, 768, 1024, 1280
O_CQ, O_CK, O_CV, O_CF = 1536, 1792, 2048, 2304
O_DCQ, O_DCKV, O_DKR = 2308, 2564, 2692

FM = {}
_o = 0
for _n, _m in [("au", 128), ("bq", 128), ("bqs", 128), ("bk", 128), ("bks", 128), ("bg", 128),
               ("cq1", 64), ("ck1", 64), ("cf", 2),
               ("dcq0", 128), ("dcq1", 128), ("dckv", 128), ("dkr", 96), ("dkrs", 96)]:
    FM[_n] = (_o, _m)
    _o += _m
NFM = _o
NTM = 384


def ab_inputs(x_b, l, p, P, C):
    w = P["w_in"][l]
    hs = slice(128 * p, 128 * p + 128)

    def grp(o):
        return w[:, o:o + 256][:, hs]

    def swap64(m):
        return np.concatenate([m[:, 0:64][:, PERM64], m[:, 64:128][:, PERM64]], 1)

    bq, bk = grp(O_BQ), grp(O_BK)
    cq, ck = grp(O_CQ), grp(O_CK)
    kr = w[:, O_DKR:O_DKR + 32]
    z64 = np.zeros((DM, 64), np.float32)
    wfm = np.concatenate([
        grp(O_AU), bq, swap64(bq), bk, swap64(bk), grp(O_BG),
        cq[:, 64:128], ck[:, 64:128],
        w[:, O_CF + 2 * p:O_CF + 2 * p + 2],
        w[:, O_DCQ:O_DCQ + 128], w[:, O_DCQ + 128:O_DCQ + 256], w[:, O_DCKV:O_DCKV + 128],
        np.concatenate([cq[:, 0:64], kr], 1), np.concatenate([ck[:, 0:64], kr[:, PERM32]], 1)], 1)
    assert wfm.shape[1] == NFM
    wtm = np.concatenate([grp(O_AV), grp(O_BV), grp(O_CV)], 1)
    wuq = P["w_uq"][l].reshape(256, 4, 96)
    z = np.zeros((256, 64), np.float32)
    uq = []
    for hl in range(2):
        wh = wuq[:, 2 * p + hl, :]
        uq += [wh, np.concatenate([z, wh[:, 64:96][:, PERM32]], 1)]
    wuqc = np.concatenate(uq, 1)
    wukv = P["w_ukv"][l].reshape(128, 4, 128)
    wukvc = np.concatenate([wukv[:, 2 * p, 0:64], wukv[:, 2 * p + 1, 0:64],
                            wukv[:, 2 * p, 64:128], wukv[:, 2 * p + 1, 64:128]], 1)
    ws = P["w_spatial"][l][2 * p:2 * p + 2]
    d = dict(
        xT=np.ascontiguousarray(x_b.T),
        gmix=np.ascontiguousarray(P["g_mix_norm"][l].reshape(8, 128).T),
        wfm=np.ascontiguousarray(wfm), wtm=np.ascontiguousarray(wtm),
        bf=np.ascontiguousarray(P["b_forget"][l][2 * p:2 * p + 2].reshape(2, 1)),
        gsgu=np.ascontiguousarray(np.broadcast_to(P["g_sgu"][l][hs][None, :], (128, 128))),
        wsT=np.ascontiguousarray(ws.transpose(0, 2, 1)),
        bs=np.ascontiguousarray(P["b_spatial"][l][2 * p:2 * p + 2].reshape(1, 256)),
        gq=np.ascontiguousarray(P["g_mla_q"][l].reshape(2, 128).T),
        gkv=np.ascontiguousarray(P["g_mla_kv"][l].reshape(128, 1)),
        wuq=np.ascontiguousarray(wuqc), wukv=np.ascontiguousarray(wukvc),
    )
    d.update(C[p])
    return d


AB_IN_SHAPES = dict(
    xT=[DM, S], gmix=[128, 8], wfm=[DM, NFM], wtm=[DM, NTM], bf=[2, 1], gsgu=[128, 128],
    wsT=[2, 128, 128], bs=[1, 256], gq=[128, 2], gkv=[128, 1], wuq=[256, 384], wukv=[128, 256],
    tabB=[2, 128, S], tabD=[2, 96, S], decT=[2, 128, 128], qwt=[128, 128], kwt=[128, 2],
    cdv=[128, 1], mask01T=[128, 128], maskneg=[128, 128], ident=[128, 128])


def phase_ab(s, cx, D, xsrc, ydst, ntiles=NT, tag="ab"):
    nc = s.nc
    K = lambda *a: (tag,) + a

    wfm = cx.sb([128, 8, NFM], BF16)
    wtm = cx.sb([128, 8, NTM], BF16)
    WSPLIT = FM["cq1"][0]
    wfm_v = D["wfm"].rearrange("(k p) n -> p k n", p=128)
    s.dma("pool", wtm[:], D["wtm"].rearrange("(k p) n -> p k n", p=128), writes=[K("wtm")], stream="w")
    s.dma("pool", wfm[:, :, WSPLIT:NFM], wfm_v[:, :, WSPLIT:NFM], writes=[K("wfm", 1)], stream="w")
    WS2 = FM["bq"][0]
    s.dma("pool", wfm[:, :, 0:WS2], wfm_v[:, :, 0:WS2], writes=[K("wfm", 0)], stream="w")
    s.dma("pool", wfm[:, :, WS2:WSPLIT], wfm_v[:, :, WS2:WSPLIT], writes=[K("wfm", 2)], stream="w")
    wuq = cx.sb([128, 2, 384], BF16)
    s.dma("pool", wuq[:], D["wuq"].rearrange("(k p) n -> p k n", p=128), writes=[K("wuq")], stream="w")
    wukv = cx.sb([128, 256], BF16)
    s.dma("pool", wukv[:], D["wukv"], writes=[K("wukv")], stream="w")
    ident = cx.sb([128, 128], BF16)
    s.dma("pool", ident[:], D["ident"], writes=[K("ident")], stream="w")
    maskneg = cx.sb([128, 128], BF16)
    s.dma("pool", maskneg[:], D["maskneg"], writes=[K("maskneg")], stream="w")

    def ld(name, shape, src=None, dt=F32):
        t = cx.sb(shape, dt)
        s.dma("sp", t[:], D[name] if src is None else src, writes=[K(name)], stream="c")
        return t

    gmix = ld("gmix", [128, 8])
    bf = ld("bf", [2, 1])
    gsgu = ld("gsgu", [128, 128])
    wsT = ld("wsT", [128, 2, 128], D["wsT"].rearrange("h s t -> s h t"))
    mask01 = ld("mask01T", [128, 128])
    bs = ld("bs", [1, 256])
    gq = ld("gq", [128, 2])
    gkv = ld("gkv", [128, 1])
    decT = ld("decT", [128, 2, 128], D["decT"].rearrange("h s t -> s h t"))
    qwt = ld("qwt", [128, 128])
    kwt = ld("kwt", [128, 2])
    cdv = ld("cdv", [128, 1])

    ones_bf = cx.sb([128, 128], BF16)
    s.op("dve", CALL("memset", ones_bf[:], 1.0), writes=[K("ones_bf")])
    ones_f = cx.sb([128, 128], F32)
    s.op("dve", CALL("memset", ones_f[:], 1.0), writes=[K("ones_f")])
    blk = cx.sb([128, 128], BF16)
    s.op("dve", CALL("memset", blk[:], 0.0), writes=[K("blk")])
    s.op("dve", CALL("memset", blk[0:64, 0:64], 1.0 / 64), writes=[K("blk")])
    s.op("dve", CALL("memset", blk[64:128, 64:128], 1.0 / 64), writes=[K("blk")])
    negbf = cx.sb([2, 1], F32)
    s.op("dve", CALL("tensor_scalar", out=negbf[:], in0=bf[:], scalar1=-1.0, scalar2=None, op0=ALU.mult),
         reads=[K("bf")], writes=[K("negbf")])
    wcT = cx.sb([128, 2, 128], BF16)
    for h in range(2):
        s.op("dve", CALL("tensor_tensor", out=wcT[:, h, :], in0=wsT[:, h, :], in1=mask01[:], op=ALU.mult),
             reads=[K("wsT"), K("mask01T")], writes=[K("wcT")])

    kc = [cx.sb([128, S], BF16) for _ in range(2)]
    vc = [cx.sb([128, 32, 128], BF16) for _ in range(2)]
    kd = [cx.sb([128, S], BF16) for _ in range(2)]
    vd = [cx.sb([128, 32, 128], BF16) for _ in range(2)]
    for t_, nm in ((vc, "vc"), (vd, "vd")):
        for h in range(2):
            s.op("pool", lambda e, t=t_[h]: e.memset(t[:], 0.0), writes=[K(nm, h, t2) for t2 in range(NT)])
            s.op("pool", CALL("memset", t_[h][:, :, 0:1], 1.0), writes=[K(nm, h, t2) for t2 in range(NT)])
    for h in range(2):
        s.op("pool", CALL("memset", kc[h][64:70, :], 1.0), writes=[K("kc", h, t2) for t2 in range(NT)])
    state_f = cx.sb([128, 64], F32)
    state_b = cx.sb([128, 2, 64], BF16)
    s.op("dve", CALL("memset", state_f[:], 0.0), writes=[K("state_f")])
    s.op("dve", CALL("memset", state_b[:], 0.0), writes=[K("state_b")])
    cum_prev = cx.sb([2, 1], F32)
    s.op("dve", CALL("memset", cum_prev[:], 0.0), writes=[K("cum_prev")])

    PB = [cx.ps([128, 512], F32) for _ in range(8)]
    pj_rr = [0]

    def pj_bank():
        b = pj_rr[0] % 2
        pj_rr[0] += 1
        return b

    st_rr = [0]

    def st_bank():
        b = 3 + st_rr[0] % 3
        st_rr[0] += 1
        return b

    xt = cx.sb([128, 8, 512], F32)
    hT = cx.sb([128, 8, 512], BF16)
    sq = hT
    XK = lambda k: K("xt", k)
    for k in range(8):
        s.alias[K("sq", k)] = [K("hT", k)]
    s.alias[K("osb", 0)] = [XK(0)]
    s.alias[K("osb", 1)] = [XK(1)]
    s.alias[K("cqf")] = [XK(3), XK(4)]
    s.alias[K("ckf")] = [XK(5)]
    s.alias[K("oB")] = [XK(6)]
    s.alias[K("cen")] = [XK(6)]
    s.alias[K("uT")] = [XK(7)]
    rstd = cx.sb([128, 512], F32)
    tB = cx.sb([128, 2, 512], F32)
    tD = cx.sb([96, 2, 512], F32)
    uT = xt[:, 7, :]
    gT = cx.sb([128, 512], F32)
    r1 = cx.sb([128, 512], F32)
    r2 = cx.sb([128, 512], F32)
    qB = cx.sb([128, 512], BF16)
    qwB = cx.sb([128, 512], BF16)
    kB = cx.sb([128, 2, 512], BF16)
    s.op("pool", CALL("memset", kB[:], 0.0), writes=[K("kB")])
    vtok = cx.sb([128, 4, NTM], BF16)
    avf = cx.sb([128, 128], F32)
    avn = cx.sb([128, 128], BF16)
    stats = cx.sb([128, 2, 6], F32)
    mv = cx.sb([128, 2, 2], F32)
    rs2 = cx.sb([128, 2], F32)
    ktok = cx.sb([128, 128], BF16)
    pB = cx.sb([128, 2, 128], BF16)
    oB = xt[:, 6, :]
    oBb = cx.sb([128, 512], BF16)
    cen = oB
    csq = cx.sb([128, 512], BF16)
    yt = [cx.sb([128, 512], F32) for _ in range(6)]
    qc = [[cx.sb([128, 512], BF16) for _ in range(2)] for _ in range(2)]
    qd = [[cx.sb([128, 512], BF16) for _ in range(2)] for _ in range(2)]
    for b_ in range(2):
        for h in range(2):
            s.op("pool", lambda e, t=qc[b_][h]: e.memset(t[64:70, :], 1.0), writes=[K("qc", b_, h)])
    fl = cx.sb([2, 512], F32)
    ones2 = cx.sb([2, 512], F32)
    s.op("dve", CALL("memset", ones2[:], 1.0), writes=[K("ones2")])
    cpos = cx.sb([2, 512], F32)
    csp = cx.sb([2, 3, 512], BF16)
    csn = cx.sb([2, 3, 512], BF16)
    cr = cx.sb([2, 512], F32)
    cqf = xt[:, 3:5, :]
    cqs = cx.sb([128, 2, 512], BF16)
    cqn = cx.sb([128, 2, 512], BF16)
    ckf = xt[:, 5, :]
    cks = cx.sb([128, 512], BF16)
    ckn = cx.sb([128, 512], BF16)
    pT = [cx.sb([128, 512], BF16) for _ in range(3)]
    osb = [xt[:, 0, :], xt[:, 1, :]]

    def mm(out, lhsT, rhs, start, stop, reads, writes):
        return s.op("pe", CALL("matmul", out, lhsT=lhsT, rhs=rhs, start=start, stop=stop),
                    reads=reads, writes=writes)

    def rms_rstd(dst, src_ps, n, reads, writes):
        s.op("act", CALL("activation", out=dst, in_=src_ps, func=AF.Ln, scale=1.0 / n, bias=EPS_AP[0][0:dst.shape[0], :]),
             reads=reads, writes=writes)
        s.op("act", CALL("activation", out=dst, in_=dst, func=AF.Exp, scale=-0.5), reads=writes, writes=writes)

    eps_t = cx.sb([128, 1], F32)
    s.op("dve", CALL("memset", eps_t[:], EPS), writes=[K("eps")])
    EPS_AP = [eps_t]
    one_t = cx.sb([128, 1], F32)
    s.op("dve", CALL("memset", one_t[:], 1.0), writes=[K("one_t")])

    def project(i):
        tok = slice(i * 512, (i + 1) * 512)
        qb = i % 2
        s.dma("sp", xt[:], xsrc(i), writes=[XK(k) for k in range(8)], stream="x")
        s.dma("sp", tB[:], D["tabB"][:, :, tok].rearrange("c p t -> p c t"), writes=[K("tB")], stream="t")
        s.dma("sp", tD[:], D["tabD"][:, :, tok].rearrange("c p t -> p c t"), writes=[K("tD")], stream="t")
        for k in range(8):
            if k % 2 == 0:
                s.op("act", CALL("activation", out=sq[:, k, :], in_=xt[:, k, :], func=AF.Square),
                     reads=[XK(k)], writes=[K("sq", k)])
            else:
                s.op("dve", CALL("tensor_tensor", out=sq[:, k, :], in0=xt[:, k, :], in1=xt[:, k, :], op=ALU.mult),
                     reads=[XK(k)], writes=[K("sq", k)])
        b = pj_bank()
        for k in range(8):
            mm(PB[b][:, :], ones_bf[:], sq[:, k, :], k == 0, k == 7, [K("sq", k), K("ones_bf")], [K("pb", b)])
        rms_rstd(rstd[:], PB[b][:, :], DM, [K("pb", b), K("eps")], [K("rstd")])
        for k in range(8):
            eng = "dve"
            if eng == "dve":
                s.op("dve", CALL("scalar_tensor_tensor", out=hT[:, k, :], in0=xt[:, k, :], scalar=gmix[:, k:k + 1],
                                                                   in1=rstd[:], op0=ALU.mult, op1=ALU.mult),
                     reads=[XK(k), K("gmix"), K("rstd")], writes=[K("hT", k)])
            else:
                s.op("pool", CALL("tensor_tensor", out=r1[:], in0=xt[:, k, :], in1=rstd[:], op=ALU.mult),
                     reads=[XK(k), K("rstd")], writes=[K("r1")])
                s.op("pool", CALL("tensor_scalar", out=hT[:, k, :], in0=r1[:], scalar1=gmix[:, k:k + 1], scalar2=None,
                                                             op0=ALU.mult),
                     reads=[K("r1"), K("gmix")], writes=[K("hT", k)])
        yield

        def proj(name):
            off, m = FM[name]
            b = pj_bank()
            for k in range(8):
                mm(PB[b][0:m, :], wfm[:, k, off:off + m], hT[:, k, :], k == 0, k == 7,
                   [K("wfm", 1 if off >= WSPLIT else (0 if off < WS2 else 2)), K("hT", k)], [K("pb", b)])
            return b

        hTk = [K("hT", k) for k in range(8)]
        for j in range(4):
            tb = pj_bank()
            for k in range(8):
                mm(PB[tb][:, 0:NTM], hT[:, k, j * 128:(j + 1) * 128], wtm[:, k, :], k == 0, k == 7,
                   [K("wtm"), K("hT", k)], [K("pb", tb)])
            s.op("act", CALL("activation", out=vtok[:, j, 0:128], in_=PB[tb][:, 0:128], func=AF.Gelu_apprx_tanh),
                 reads=[K("pb", tb)], writes=[K("vtok", j)])
            s.op("dve", CALL("tensor_copy", out=vtok[:, j, 128:384], in_=PB[tb][:, 128:384]),
                 reads=[K("pb", tb)], writes=[K("vtok", j)])
            blkid = i * 4 + j
            s.op("pool", CALL("tensor_copy", out=vc[0][:, blkid, 64:128], in_=vtok[:, j, 256:320]),
                 reads=[K("vtok", j)], writes=[K("vc", 0, i)])
            s.op("pool", CALL("tensor_copy", out=vc[1][:, blkid, 64:128], in_=vtok[:, j, 320:384]),
                 reads=[K("vtok", j)], writes=[K("vc", 1, i)])
            yield

        if "C" in PARTS:
            yield from mixer_c(i, proj, tok, qb)
        if "D" in PARTS:
            yield from mixer_d(i, proj, tok, qb)
        if "A" in PARTS:
            yield from mixer_a(i, proj)
        if "B" in PARTS:
            yield from mixer_b(i, proj, tok)

    def mixer_a(i, proj):
        b = proj("au")
        s.op("act", CALL("activation", out=uT[:], in_=PB[b][:, :], func=AF.Gelu_apprx_tanh),
             reads=[K("pb", b)], writes=[K("uT")])
        yield
        for j in range(4):
            s.op("dve", CALL("tensor_copy", out=avf[:], in_=vtok[:, j, 0:128]), reads=[K("vtok", j)], writes=[K("avf")])
            for h in range(2):
                s.op("dve", CALL("bn_stats", out=stats[:, h, :], in_=avf[:, h * 64:(h + 1) * 64]),
                     reads=[K("avf")], writes=[K("stats")])
                s.op("dve", CALL("bn_aggr", out=mv[:, h, :], in_=stats[:, h, :]), reads=[K("stats")], writes=[K("mv")])
            s.op("act", CALL("activation", out=rs2[:], in_=mv[:, :, 1], func=AF.Ln, bias=eps_t[:], scale=1.0),
                 reads=[K("mv"), K("eps")], writes=[K("rs2")])
            s.op("act", CALL("activation", out=rs2[:], in_=rs2[:], func=AF.Exp, scale=-0.5), reads=[K("rs2")], writes=[K("rs2")])
            for h in range(2):
                s.op("dve", CALL("tensor_scalar", out=avf[:, h * 64:(h + 1) * 64], in0=avf[:, h * 64:(h + 1) * 64],
                                                            scalar1=mv[:, h, 0:1], scalar2=rs2[:, h:h + 1],
                                                            op0=ALU.subtract, op1=ALU.mult),
                     reads=[K("avf"), K("mv"), K("rs2")], writes=[K("avf")])
            s.op("dve", CALL("tensor_tensor", out=avn[:], in0=avf[:], in1=gsgu[:], op=ALU.mult),
                 reads=[K("avf"), K("gsgu")], writes=[K("avn")])
            for h in range(2):
                ps = PB[2][h * 64:(h + 1) * 64, 384:512]
                mm(ps, avn[:, h * 64:(h + 1) * 64], wcT[:, h, :], True, False, [K("avn"), K("wcT")], [K("pb", 2)])
                s.op("pe", CALL("matmul", ps, lhsT=ones_f[0:1, 0:64], rhs=bs[0:1, h * 128:(h + 1) * 128],
                                                           start=False, stop=True),
                     reads=[K("ones_f"), K("bs")], writes=[K("pb", 2)])
            s.op("dve", CALL("tensor_tensor", out=yt[0][:, j * 128:(j + 1) * 128], in0=uT[:, j * 128:(j + 1) * 128],
                                                        in1=PB[2][:, 384:512], op=ALU.mult),
                 reads=[K("uT"), K("pb", 2)], writes=[K("yt", 0)])
            yield
        s.dma("sp", ydst(0, i), yt[0][:], reads=[K("yt", 0)], stream="y")

    def mixer_b(i, proj, tok):
        def rotary_fm(bq_, bs_, dst, tab, rows, extra=None):
            s.op("dve", CALL("tensor_tensor", out=r1[rows, :], in0=PB[bq_][rows, :], in1=tab[rows, 0, :], op=ALU.mult),
                 reads=[K("pb", bq_), K("tab", id(tab))], writes=[K("r1")])
            s.op("dve", CALL("tensor_tensor", out=r2[rows, :], in0=PB[bs_][rows, :], in1=tab[rows, 1, :], op=ALU.mult),
                 reads=[K("pb", bs_), K("tab", id(tab))], writes=[K("r2")])

        tabkeyB = K("tB")
        tabkeyD = K("tD")
        b1 = proj("bq")
        b2 = proj("bqs")
        s.op("dve", CALL("tensor_tensor", out=r1[:], in0=PB[b1][:, :], in1=tB[:, 0, :], op=ALU.mult),
             reads=[K("pb", b1), tabkeyB], writes=[K("r1")])
        s.op("dve", CALL("tensor_tensor", out=r2[:], in0=PB[b2][:, :], in1=tB[:, 1, :], op=ALU.mult),
             reads=[K("pb", b2), tabkeyB], writes=[K("r2")])
        s.op("pool", CALL("tensor_tensor", out=r1[:], in0=r1[:], in1=r2[:], op=ALU.add),
             reads=[K("r1"), K("r2")], writes=[K("r1")])
        s.op("act", CALL("copy", out=qB[:], in_=r1[:]), reads=[K("r1")], writes=[K("qB")])
        for j in range(4):
            s.op("pool", CALL("tensor_tensor", out=qwB[:, j * 128:(j + 1) * 128], in0=r1[:, j * 128:(j + 1) * 128],
                                                         in1=qwt[:], op=ALU.mult),
                 reads=[K("r1"), K("qwt")], writes=[K("qwB")])
        yield
        b1 = proj("bk")
        b2 = proj("bks")
        s.op("dve", CALL("tensor_tensor", out=r1[:], in0=PB[b1][:, :], in1=tB[:, 0, :], op=ALU.mult),
             reads=[K("pb", b1), tabkeyB], writes=[K("r1")])
        s.op("dve", CALL("tensor_tensor", out=r2[:], in0=PB[b2][:, :], in1=tB[:, 1, :], op=ALU.mult),
             reads=[K("pb", b2), tabkeyB], writes=[K("r2")])
        for h in range(2):
            hr = slice(h * 64, (h + 1) * 64)
            s.op("pool", CALL("tensor_tensor", out=kB[hr, h, :], in0=r1[hr, :], in1=r2[hr, :], op=ALU.add),
                 reads=[K("r1"), K("r2")], writes=[K("kB")])
        yield
        b = proj("bg")
        s.op("act", CALL("activation", out=gT[:], in_=PB[b][:, :], func=AF.Silu), reads=[K("pb", b)], writes=[K("gT")])
        yield
        for j in range(4):
            cs = slice(j * 128, (j + 1) * 128)
            for h in range(2):
                mm(PB[2][:, 384:512], kB[:, h, cs], ident[:], True, True, [K("kB"), K("ident")], [K("pb", 2)])
                s.op("act", CALL("activation", out=ktok[:, h * 64:(h + 1) * 64], in_=PB[2][:, 384 + h * 64:384 + (h + 1) * 64],
                                 func=AF.Copy, scale=kwt[:, h:h + 1]),
                     reads=[K("pb", 2), K("kwt")], writes=[K("ktok")])
            sbs = [st_bank(), st_bank()]
            for h in range(2):
                mm(PB[sbs[h]][:, 0:128], kB[:, h, cs], qB[:, cs], True, True, [K("kB"), K("qB")], [K("pb", sbs[h])])
            for h in range(2):
                s.op("dve", CALL("tensor_tensor", out=pB[:, h, :], in0=PB[sbs[h]][:, 0:128],
                                                                     in1=decT[:, h, :], op=ALU.mult),
                     reads=[K("pb", sbs[h]), K("decT")], writes=[K("pB", h)])
            bo = 2 if INTERLEAVE else 6
            for h in range(2):
                hr = slice(h * 64, (h + 1) * 64)
                ps = PB[bo][hr, 0:128] if INTERLEAVE else PB[bo][hr, cs]
                mm(ps, vtok[:, j, 128 + h * 64:128 + (h + 1) * 64], pB[:, h, :], True, False,
                   [K("vtok", j), K("pB", h)], [K("pb", bo)])
                mm(ps, state_b[:, h, :], qwB[:, cs], False, True, [K("state_b"), K("qwB")], [K("pb", bo)])
            if INTERLEAVE:
                s.op("dve", CALL("tensor_copy", out=oB[:, cs], in_=PB[bo][:, 0:128]), reads=[K("pb", bo)], writes=[K("oB")])
            for h in range(2):
                hr = slice(h * 64, (h + 1) * 64)
                mm(PB[2][hr, 448:512], ktok[:, hr], vtok[:, j, 128 + h * 64:128 + (h + 1) * 64], True, True,
                   [K("ktok"), K("vtok", j)], [K("pb", 2)])
            s.op("dve", CALL("scalar_tensor_tensor", out=state_f[:], in0=state_f[:], scalar=cdv[:, 0:1], in1=PB[2][:, 448:512],
                                                          op0=ALU.mult, op1=ALU.add),
                 reads=[K("state_f"), K("cdv"), K("pb", 2)], writes=[K("state_f")])
            for h in range(2):
                s.op("act", CALL("copy", out=state_b[h * 64:(h + 1) * 64, h, :], in_=state_f[h * 64:(h + 1) * 64, :]),
                     reads=[K("state_f")], writes=[K("state_b")])
            yield
        if INTERLEAVE:
            s.op("pool", CALL("tensor_copy", out=oBb[:], in_=oB[:]), reads=[K("oB")], writes=[K("oBb")])
        else:
            s.op("act", CALL("copy", out=oB[:], in_=PB[6][:, :]), reads=[K("pb", 6)], writes=[K("oB")])
            s.op("dve", CALL("tensor_copy", out=oBb[:], in_=PB[6][:, :]), reads=[K("pb", 6)], writes=[K("oBb")])
        b = pj_bank()
        mm(PB[b][:, :], blk[:], oBb[:], True, True, [K("blk"), K("oBb")], [K("pb", b)])
        s.op("dve", CALL("tensor_tensor", out=cen[:], in0=oB[:], in1=PB[b][:, :], op=ALU.subtract),
             reads=[K("oB"), K("pb", b)], writes=[K("cen")])
        s.op("act", CALL("activation", out=csq[:], in_=cen[:], func=AF.Square), reads=[K("cen")], writes=[K("csq")])
        b = pj_bank()
        mm(PB[b][:, :], blk[:], csq[:], True, True, [K("blk"), K("csq")], [K("pb", b)])
        rms_rstd(r2[:], PB[b][:, :], 1.0, [K("pb", b), K("eps")], [K("r2")])
        s.op("dve", CALL("tensor_tensor", out=cen[:], in0=cen[:], in1=r2[:], op=ALU.mult), reads=[K("cen"), K("r2")], writes=[K("cen")])
        s.op("pool", CALL("tensor_tensor", out=yt[1][:], in0=cen[:], in1=gT[:], op=ALU.mult),
             reads=[K("cen"), K("gT")], writes=[K("yt", 1)])
        s.dma("sp", ydst(1, i), yt[1][:], reads=[K("yt", 1)], stream="y")
        yield

    def mixer_c(i, proj, tok, qb):
        rr = slice(64, 96)
        b1 = proj("dkr")
        b2 = proj("dkrs")
        s.op("act", CALL("activation", out=qc[qb][0][0:64, :], in_=PB[b1][0:64, :], func=AF.Copy, scale=0.125),
             reads=[K("pb", b1)], writes=[K("qc", qb, 0)])
        s.op("act", CALL("copy", out=kc[0][0:64, tok], in_=PB[b2][0:64, :]), reads=[K("pb", b2)], writes=[K("kc", 0, i)])
        s.op("dve", CALL("tensor_tensor", out=r1[rr, :], in0=PB[b1][rr, :], in1=tD[rr, 0, :], op=ALU.mult),
             reads=[K("pb", b1), K("tD")], writes=[K("r1")])
        s.op("dve", CALL("tensor_tensor", out=r2[rr, :], in0=PB[b2][rr, :], in1=tD[rr, 1, :], op=ALU.mult),
             reads=[K("pb", b2), K("tD")], writes=[K("r2")])
        for h in range(2):
            s.op("pool", CALL("tensor_tensor", out=kd[h][rr, tok], in0=r1[rr, :], in1=r2[rr, :], op=ALU.add),
                 reads=[K("r1"), K("r2")], writes=[K("kd", h, i)])
        yield
        for h in range(1, 2):
            b = proj("cq%d" % h)
            s.op("act", CALL("activation", out=qc[qb][h][0:64, :], in_=PB[b][0:64, :], func=AF.Copy, scale=0.125),
                 reads=[K("pb", b)], writes=[K("qc", qb, h)])
            b = proj("ck%d" % h)
            s.op("act", CALL("copy", out=kc[h][0:64, tok], in_=PB[b][0:64, :]),
                 reads=[K("pb", b)], writes=[K("kc", h, i)])
            yield
        b = proj("cf")
        s.op("act", CALL("activation", out=fl[:], in_=PB[b][0:2, :], func=AF.Exp, scale=-1.0, bias=negbf[:]),
             reads=[K("pb", b), K("negbf")], writes=[K("fl")])
        s.op("act", CALL("activation", out=fl[:], in_=fl[:], func=AF.Ln, scale=1.0, bias=one_t[0:2, :]),
             reads=[K("fl"), K("one_t")], writes=[K("fl")])
        s.op("dve", CALL("tensor_tensor_scan", out=cpos[:], data0=ones2[:], data1=fl[:], initial=cum_prev[:],
                                                    op0=ALU.mult, op1=ALU.add),
             reads=[K("ones2"), K("fl"), K("cum_prev")], writes=[K("cpos")])
        s.op("dve", CALL("tensor_copy", out=cum_prev[:], in_=cpos[:, 511:512]), reads=[K("cpos")], writes=[K("cum_prev")])
        s.op("dve", CALL("tensor_copy", out=csp[:, 0, :], in_=cpos[:]), reads=[K("cpos")], writes=[K("csp")])
        s.op("dve", CALL("tensor_tensor", out=cr[:], in0=cpos[:], in1=csp[:, 0, :], op=ALU.subtract),
             reads=[K("cpos"), K("csp")], writes=[K("cr")])
        s.op("dve", CALL("tensor_copy", out=csp[:, 1, :], in_=cr[:]), reads=[K("cr")], writes=[K("csp")])
        s.op("dve", CALL("tensor_tensor", out=cr[:], in0=cr[:], in1=csp[:, 1, :], op=ALU.subtract),
             reads=[K("cr"), K("csp")], writes=[K("cr")])
        s.op("dve", CALL("tensor_copy", out=csp[:, 2, :], in_=cr[:]), reads=[K("cr")], writes=[K("csp")])
        s.op("dve", CALL("tensor_scalar", out=csn[:], in0=csp[:], scalar1=-1.0, scalar2=None, op0=ALU.mult),
             reads=[K("csp")], writes=[K("csn")])
        for h in range(2):
            s.dma("sp", kc[h][64:67, tok], csp[h:h + 1, :, :], reads=[K("csp")], writes=[K("kc", h, i)],
                  stream="a")
            s.dma("sp", qc[qb][h][67:70, :], csn[h:h + 1, :, :], reads=[K("csn")],
                  writes=[K("qc", qb, h)], stream="a")
        yield

    def mixer_d(i, proj, tok, qb):
        rr = slice(64, 96)
        tabkeyD = K("tD")
        for k2 in range(2):
            b = proj("dcq%d" % k2)
            s.op("act", CALL("copy", out=cqf[:, k2, :], in_=PB[b][:, :]), reads=[K("pb", b)], writes=[K("cqf")])
            s.op("act", CALL("activation", out=cqs[:, k2, :], in_=PB[b][:, :], func=AF.Square),
                 reads=[K("pb", b)], writes=[K("cqs")])
        b = pj_bank()
        for k2 in range(2):
            mm(PB[b][:, :], ones_bf[:], cqs[:, k2, :], k2 == 0, k2 == 1, [K("ones_bf"), K("cqs")], [K("pb", b)])
        rms_rstd(r1[:], PB[b][:, :], 256.0, [K("pb", b), K("eps")], [K("r1")])
        for k2 in range(2):
            s.op("dve", CALL("scalar_tensor_tensor", out=cqn[:, k2, :], in0=cqf[:, k2, :], scalar=gq[:, k2:k2 + 1],
                                                                 in1=r1[:], op0=ALU.mult, op1=ALU.mult),
                 reads=[K("cqf"), K("gq"), K("r1")], writes=[K("cqn")])
        yield
        b = proj("dckv")
        s.op("act", CALL("copy", out=ckf[:], in_=PB[b][:, :]), reads=[K("pb", b)], writes=[K("ckf")])
        s.op("act", CALL("activation", out=cks[:], in_=PB[b][:, :], func=AF.Square), reads=[K("pb", b)], writes=[K("cks")])
        b = pj_bank()
        mm(PB[b][:, :], ones_bf[:], cks[:], True, True, [K("ones_bf"), K("cks")], [K("pb", b)])
        rms_rstd(r2[:], PB[b][:, :], 128.0, [K("pb", b), K("eps")], [K("r2")])
        s.op("dve", CALL("scalar_tensor_tensor", out=ckn[:], in0=ckf[:], scalar=gkv[:, 0:1], in1=r2[:],
                                                      op0=ALU.mult, op1=ALU.mult),
             reads=[K("ckf"), K("gkv"), K("r2")], writes=[K("ckn")])
        yield
        for h in range(2):
            b1 = pj_bank()
            for k2 in range(2):
                mm(PB[b1][0:96, :], wuq[:, k2, h * 192:h * 192 + 96], cqn[:, k2, :], k2 == 0, k2 == 1,
                   [K("wuq"), K("cqn")], [K("pb", b1)])
            b2 = pj_bank()
            for k2 in range(2):
                mm(PB[b2][0:96, :], wuq[:, k2, h * 192 + 96:h * 192 + 192], cqn[:, k2, :], k2 == 0, k2 == 1,
                   [K("wuq"), K("cqn")], [K("pb", b2)])
            s.op("act", CALL("copy", out=qd[qb][h][0:64, :], in_=PB[b1][0:64, :]),
                 reads=[K("pb", b1)], writes=[K("qd", qb, h)])
            s.op("dve", CALL("tensor_tensor", out=r1[rr, :], in0=PB[b1][rr, :], in1=tD[rr, 0, :], op=ALU.mult),
                 reads=[K("pb", b1), tabkeyD], writes=[K("r1")])
            s.op("dve", CALL("tensor_tensor", out=r2[rr, :], in0=PB[b2][rr, :], in1=tD[rr, 1, :], op=ALU.mult),
                 reads=[K("pb", b2), tabkeyD], writes=[K("r2")])
            s.op("pool", CALL("tensor_tensor", out=qd[qb][h][rr, :], in0=r1[rr, :], in1=r2[rr, :], op=ALU.add),
                 reads=[K("r1"), K("r2")], writes=[K("qd", qb, h)])
            b = pj_bank()
            mm(PB[b][0:64, :], wukv[:, h * 64:(h + 1) * 64], ckn[:], True, True, [K("wukv"), K("ckn")], [K("pb", b)])
            s.op("act", CALL("copy", out=kd[h][0:64, tok], in_=PB[b][0:64, :]), reads=[K("pb", b)], writes=[K("kd", h, i)])
            yield
        for j in range(4):
            blkid = i * 4 + j
            mm(PB[2][:, 0:128], ckn[:, j * 128:(j + 1) * 128], wukv[:, 128:256], True, True, [K("wukv"), K("ckn")], [K("pb", 2)])
            s.op("act", CALL("copy", out=vd[0][:, blkid, 64:128], in_=PB[2][:, 0:64]), reads=[K("pb", 2)], writes=[K("vd", 0, i)])
            s.op("act", CALL("copy", out=vd[1][:, blkid, 64:128], in_=PB[2][:, 64:128]), reads=[K("pb", 2)], writes=[K("vd", 1, i)])
        yield

    att_ctr = [0]
    LOOK = 2
    DEFER = 6

    def attention(i):
        qb = i % 2
        nkb = 4 * (i + 1)
        cfg = ((qc, kc, vc, 70, 1.0, "kc", "vc", "qc"), (qd, kd, vd, 96, 96 ** -0.5, "kd", "vd", "qd"))
        iters = []
        for g in range(2):
            if "CD"[g] not in PARTS or "T" not in PARTS:
                continue
            for h in range(2):
                for kb in range(nkb):
                    iters.append((g, h, kb))
        slot = {}
        pend = []

        def emit_s(t):
            g, h, kb = iters[t]
            qs, ks, vs, kdim, scale, kname, vname, qname = cfg[g]
            m = kb - 4 * i
            c0 = 128 * m if m > 0 else 0
            r = att_ctr[0] % 3
            att_ctr[0] += 1
            sb_ = 3 + r
            diag = m >= 0
            mm(PB[sb_][:, c0:512], ks[h][0:kdim, kb * 128:(kb + 1) * 128], qs[qb][h][0:kdim, c0:512], True, not diag,
               [K(kname, h, kb // 4), K(qname, qb, h)], [K("pb", sb_)])
            if diag:
                mm(PB[sb_][:, c0:c0 + 128], ident[:], maskneg[:], False, True, [K("ident"), K("maskneg")], [K("pb", sb_)])
            pt = pT[r]
            pk = K("pT", r)
            s.op("act", CALL("activation", out=pt[:, c0:512], in_=PB[sb_][:, c0:512], func=AF.Exp, scale=scale),
                 reads=[K("pb", sb_)], writes=[pk])
            slot[t] = (pt, pk, c0)

        def emit_pv(t):
            g, h, kb = iters[t]
            qs, ks, vs, kdim, scale, kname, vname, qname = cfg[g]
            pt, pk, c0 = slot.pop(t)
            ob = 6 + h
            mm(PB[ob][:, c0:512], vs[h][:, kb, :], pt[:, c0:512], kb == 0, kb == nkb - 1,
               [K(vname, h, kb // 4), pk], [K("pb", ob)])
            if kb == nkb - 1:
                o_ = osb[h]
                r_ = yt[2 + g + 2 * h]
                dr = slice(0, 1)
                s.op("act", CALL("copy", out=o_[:], in_=PB[ob][:, :]), reads=[K("pb", ob)], writes=[K("osb", h)])
                s.op("act", CALL("activation", out=r_[dr, :], in_=o_[dr, :], func=AF.Ln),
                     reads=[K("osb", h)], writes=[K("rc", g, h)])
                s.op("act", CALL("activation", out=r_[dr, :], in_=r_[dr, :], func=AF.Exp, scale=-1.0),
                     reads=[K("rc", g, h)], writes=[K("rc", g, h)])
                pend.append((t + DEFER, g, h))

        def emit_norm(g, h):
            o_ = osb[h]
            yi = 2 + g + 2 * h
            r_ = yt[yi]
            dr = slice(0, 1)
            yr = slice(64, 128)
            b = pj_bank()
            s.op("pe", CALL("matmul", PB[b][:, :], lhsT=ones_f[dr, :], rhs=r_[dr, :], start=True, stop=True),
                 reads=[K("ones_f"), K("rc", g, h)], writes=[K("pb", b)])
            s.op("dve", CALL("tensor_tensor", out=yt[yi][yr, :], in0=o_[yr, :], in1=PB[b][yr, :], op=ALU.mult),
                 reads=[K("osb", h), K("pb", b)], writes=[K("yt", yi)])
            s.dma("sp", ydst(2 + g, i)[h * 64:(h + 1) * 64, :], yt[yi][yr, :], reads=[K("yt", yi)], stream="y")

        n = len(iters)
        for t in range(n + LOOK):
            if t < n:
                emit_s(t)
            if t >= LOOK:
                emit_pv(t - LOOK)
            while pend and pend[0][0] <= t:
                _, g, h = pend.pop(0)
                emit_norm(g, h)
            yield
        for _, g, h in pend:
            emit_norm(g, h)
        yield

    def drain(gen):
        for _ in gen:
            pass

    def interleave(ga, na, gp, np_):
        ca = cp = 0
        a_done = p_done = False
        while not (a_done and p_done):
            take_a = (not a_done) and (p_done or ca * np_ <= cp * na)
            if take_a:
                try:
                    next(ga)
                    ca += 1
                except StopIteration:
                    a_done = True
            else:
                try:
                    next(gp)
                    cp += 1
                except StopIteration:
                    p_done = True

    NP_STEPS = 40
    drain(project(0))
    for i in range(ntiles):
        if i + 1 < ntiles:
            if INTERLEAVE:
                interleave(attention(i), 16 * (i + 1) + LOOK + 1, project(i + 1), NP_STEPS)
            else:
                drain(project(i + 1))
                drain(attention(i))
        else:
            drain(attention(i))


def build_ab(ntiles=NT):
    nc = bass.Bass("TRN2", target_bir_lowering=False)
    D = {n: nc.dram_tensor(n, sh, F32, kind="ExternalInput").ap() for n, sh in AB_IN_SHAPES.items()}
    y = nc.dram_tensor("y", [4, 128, S], F32, kind="ExternalOutput").ap()
    with contextlib.ExitStack() as st:
        cx = Ctx(nc, st)
        s = Sched(nc)
        xv = D["xT"].rearrange("(k p) t -> p k t", p=128)
        phase_ab(s, cx, D, lambda i: xv[:, :, i * 512:(i + 1) * 512], lambda g, i: y[g, :, i * 512:(i + 1) * 512], ntiles)
        s.emit()
    return nc


TC = 256
NTC = 2048 // TC

C_IN_SHAPES = dict(yT=[8, 128, 2048], xT=[DM, 2048], ggo=[128, 8], gffn=[128, 8], gfin=[128, 8],
                   w_out=[DM, DM], w_up=[DM, 4 * DM], w_down=[4 * DM, DM])


def c_inputs(y_parts, x_b, l, p, P):
    tk = slice(2048 * p, 2048 * (p + 1))
    yT = np.empty((8, 128, 2048), np.float32)
    for g in range(4):
        for q in range(2):
            yT[2 * g + q] = y_parts[q][g][:, tk]
    arr = lambda v: np.ascontiguousarray(v.reshape(8, 128).T)
    return dict(yT=yT, xT=np.ascontiguousarray(x_b[tk].T), ggo=arr(P["g_group_out"][l]), gffn=arr(P["g_ffn_norm"][l]),
                gfin=arr(P["g_final"]), w_out=P["w_out"][l], w_up=P["w_up"][l], w_down=P["w_down"][l])


def phase_c(s, cx, D, ysrc, xsrc, xdst, final, ntiles=NTC, tag="c"):
    K = lambda *a: (tag,) + a
    wo = cx.sb([128, 8, DM], BF16)
    wu = cx.sb([128, 8, 4 * DM], BF16)
    wd = cx.sb([128, 32, DM], BF16)
    wov = D["w_out"].rearrange("(k p) n -> p k n", p=128)
    for k0 in range(0, 8, 4):
        s.dma("pool", wo[:, k0:k0 + 4, :], wov[:, k0:k0 + 4, :], writes=[K("wo")], stream="w")
    wuv = D["w_up"].rearrange("(k p) n -> p k n", p=128)
    for c0 in range(0, 4 * DM, 1024):
        for k0 in range(0, 8, 2):
            s.dma("pool", wu[:, k0:k0 + 2, c0:c0 + 1024], wuv[:, k0:k0 + 2, c0:c0 + 1024], writes=[K("wu", c0 // 1024)], stream="w")
    wdv = D["w_down"].rearrange("(k p) n -> p k n", p=128)
    for k0 in range(0, 32, 4):
        s.dma("pool", wd[:, k0:k0 + 4, :], wdv[:, k0:k0 + 4, :], writes=[K("wd", k0 // 4)], stream="w")

    def ld(name):
        t = cx.sb([128, 8], F32)
        s.dma("sp", t[:], D[name], writes=[K(name)], stream="c")
        return t

    ggo, gffn = ld("ggo"), ld("gffn")
    gfin = ld("gfin") if final else None
    ones_bf = cx.sb([128, 128], BF16)
    s.op("dve", CALL("memset", ones_bf[:], 1.0), writes=[K("ones")])
    eps_t = cx.sb([128, 1], F32)
    s.op("dve", CALL("memset", eps_t[:], EPS), writes=[K("eps")])
    PB = [cx.ps([128, 512], F32) for _ in range(8)]
    rr = [0]

    def bank():
        b = rr[0] % 8
        rr[0] += 1
        return b

    yt = cx.sb([128, 8, TC], F32)
    xt = cx.sb([128, 8, TC], F32)
    sq = cx.sb([128, 8, TC], BF16)
    yn = cx.sb([128, 8, TC], BF16)
    h2 = cx.sb([128, 8, TC], BF16)
    a = cx.sb([128, 32, TC], BF16)
    rl = [cx.sb([128, TC], F32) for _ in range(2)]
    rstd = cx.sb([128, 4, TC], F32)
    ot = cx.sb([128, 8, TC], F32)

    def mm(out, lhsT, rhs, start, stop, reads, writes):
        return s.op("pe", CALL("matmul", out, lhsT=lhsT, rhs=rhs, start=start, stop=stop), reads=reads, writes=writes)

    def rstd_from(dst, ps, n, bkey, dkey):
        s.op("act", CALL("activation", out=dst, in_=ps, func=AF.Ln, scale=1.0 / n, bias=eps_t[:]),
             reads=[bkey, K("eps")], writes=[dkey])
        s.op("act", CALL("activation", out=dst, in_=dst, func=AF.Exp, scale=-0.5), reads=[dkey], writes=[dkey])

    def full_norm(src, gain, dst, dkey_fn, gkey):
        for k in range(8):
            s.op("act", CALL("activation", out=sq[:, k, :], in_=src[:, k, :], func=AF.Square), reads=[K("xt", k)], writes=[K("sq", k)])
        b = bank()
        for k in range(8):
            mm(PB[b][:, 0:TC], ones_bf[:], sq[:, k, :], k == 0, k == 7, [K("ones"), K("sq", k)], [K("pb", b)])
        rstd_from(rstd[:, 0, :], PB[b][:, 0:TC], DM, K("pb", b), K("rstd", 0))
        for k in range(8):
            s.op("dve", CALL("scalar_tensor_tensor", out=dst[:, k, :], in0=src[:, k, :], scalar=gain[:, k:k + 1],
                             in1=rstd[:, 0, :], op0=ALU.mult, op1=ALU.mult),
                 reads=[K("xt", k), K("rstd", 0), gkey], writes=[dkey_fn(k)])

    def ystage(tt):
        s.dma("sp", yt[:], ysrc(tt), writes=[K("yt", c) for c in range(8)], stream="y")
        for c in range(8):
            s.op("act", CALL("activation", out=sq[:, c, :], in_=yt[:, c, :], func=AF.Square), reads=[K("yt", c)], writes=[K("sq", c)])
        for g in range(4):
            b = bank()
            for q in range(2):
                mm(PB[b][:, 0:TC], ones_bf[:], sq[:, 2 * g + q, :], q == 0, q == 1, [K("ones"), K("sq", 2 * g + q)], [K("pb", b)])
            rstd_from(rstd[:, g, :], PB[b][:, 0:TC], 256.0, K("pb", b), K("rstd", g))
        for c in range(8):
            s.op("dve", CALL("scalar_tensor_tensor", out=yn[:, c, :], in0=yt[:, c, :], scalar=ggo[:, c:c + 1],
                             in1=rstd[:, c // 2, :], op0=ALU.mult, op1=ALU.mult),
                 reads=[K("yt", c), K("rstd", c // 2), K("ggo")], writes=[K("yn", c)])

    ystage(0)
    for tt in range(ntiles):
        s.dma("sp", xt[:], xsrc(tt), writes=[K("xt", k) for k in range(8)], stream="x")
        for oc in range(8):
            b = bank()
            for k in range(8):
                mm(PB[b][:, 0:TC], wo[:, k, oc * 128:(oc + 1) * 128], yn[:, k, :], k == 0, k == 7, [K("wo"), K("yn", k)], [K("pb", b)])
            s.op("dve", CALL("tensor_tensor", out=xt[:, oc, :], in0=xt[:, oc, :], in1=PB[b][:, 0:TC], op=ALU.add),
                 reads=[K("xt", oc), K("pb", b)], writes=[K("xt", oc)])
        for k in range(8):
            s.op("dve", CALL("tensor_scalar", out=h2[:, k, :], in0=xt[:, k, :], scalar1=gffn[:, k:k + 1], scalar2=None, op0=ALU.mult),
                 reads=[K("xt", k), K("gffn")], writes=[K("h2", k)])
        for k in range(8):
            s.op("act", CALL("activation", out=sq[:, k, :], in_=xt[:, k, :], func=AF.Square), reads=[K("xt", k)], writes=[K("sq", k)])

        def up_group(fc):
            b = bank()
            for k in range(8):
                mm(PB[b][:, 0:TC], wu[:, k, fc * 128:(fc + 1) * 128], h2[:, k, :], k == 0, k == 7, [K("wu", fc // 8), K("h2", k)], [K("pb", b)])
            return b

        def up_evac(fc, b):
            r_ = rl[fc % 2]
            s.op("dve", CALL("scalar_tensor_tensor", out=r_[:], in0=PB[b][:, 0:TC], scalar=0.0, in1=rstd[:, 0, :],
                             op0=ALU.max, op1=ALU.mult),
                 reads=[K("pb", b), K("rstd", 0)], writes=[K("rl", fc % 2)])
            s.op("act", CALL("activation", out=a[:, fc, :], in_=r_[:], func=AF.Square), reads=[K("rl", fc % 2)], writes=[K("a", fc)])

        AHEAD = 4
        ub = {fc: up_group(fc) for fc in range(AHEAD)}
        b = bank()
        for k in range(8):
            mm(PB[b][:, 0:TC], ones_bf[:], sq[:, k, :], k == 0, k == 7, [K("ones"), K("sq", k)], [K("pb", b)])
        rstd_from(rstd[:, 0, :], PB[b][:, 0:TC], DM, K("pb", b), K("rstd", 0))
        for fc in range(AHEAD):
            up_evac(fc, ub[fc])
        for fc in range(AHEAD, 32):
            up_evac(fc, up_group(fc))
        if tt + 1 < ntiles:
            ystage(tt + 1)
        for oc in range(8):
            b = bank()
            for fc in range(32):
                mm(PB[b][:, 0:TC], wd[:, fc, oc * 128:(oc + 1) * 128], a[:, fc, :], fc == 0, fc == 31, [K("wd", fc // 4), K("a", fc)], [K("pb", b)])
            if final:
                s.op("dve", CALL("tensor_tensor", out=xt[:, oc, :], in0=xt[:, oc, :], in1=PB[b][:, 0:TC], op=ALU.add),
                     reads=[K("xt", oc), K("pb", b)], writes=[K("xt", oc)])
            else:
                s.op("dve", CALL("tensor_tensor", out=ot[:, oc, :], in0=xt[:, oc, :], in1=PB[b][:, 0:TC], op=ALU.add),
                     reads=[K("xt", oc), K("pb", b)], writes=[K("ot", oc)])
        if final:
            full_norm(xt, gfin, ot, lambda k: K("ot", k), K("gfin"))
            s.dma("sp", xdst(tt), ot[:], reads=[K("ot", k) for k in range(8)], stream="o")
        else:
            s.dma("sp", xdst(tt), ot[:], reads=[K("ot", k) for k in range(8)], stream="o")


def build_c(final, ntiles=NTC):
    nc = bass.Bass("TRN2", target_bir_lowering=False)
    D = {n: nc.dram_tensor(n, sh, F32, kind="ExternalInput").ap() for n, sh in C_IN_SHAPES.items()}
    xo = nc.dram_tensor("xo", [DM, 2048], F32, kind="ExternalOutput").ap()
    with contextlib.ExitStack() as st:
        cx = Ctx(nc, st)
        s = Sched(nc)
        yv = D["yT"].rearrange("c p t -> p c t")
        xv = D["xT"].rearrange("(k p) t -> p k t", p=128)
        xov = xo.rearrange("(k p) t -> p k t", p=128)
        sl = lambda tt: slice(tt * TC, (tt + 1) * TC)
        phase_c(s, cx, D, lambda tt: yv[:, :, sl(tt)], lambda tt: xv[:, :, sl(tt)], lambda tt: xov[:, :, sl(tt)], final, ntiles)
        s.emit()
    return nc


SHARED_CONST = ("tabB", "tabD", "mask01T", "maskneg", "ident")
PARITY_CONST = ("decT", "qwt", "kwt", "cdv")
AB_LP = ("wfm", "wtm", "bf", "gsgu", "wsT", "bs", "wuq", "wukv")
AB_L = ("gmix", "gq", "gkv")
C_L = ("ggo", "gffn", "w_out", "w_up", "w_down")


def fused_input_shapes():
    sh = {"xT": [DM, S], "gfin": [128, 8]}
    for n in SHARED_CONST:
        sh[n] = AB_IN_SHAPES[n]
    for p in range(2):
        for n in PARITY_CONST:
            sh["%s_p%d" % (n, p)] = AB_IN_SHAPES[n]
    for l in range(2):
        for n in AB_L:
            sh["%s_l%d" % (n, l)] = AB_IN_SHAPES[n]
        for n in C_L:
            sh["%s_l%d" % (n, l)] = C_IN_SHAPES[n]
        for p in range(2):
            for n in AB_LP:
                sh["%s_l%dp%d" % (n, l, p)] = AB_IN_SHAPES[n]
    return sh


def fused_inputs(x_b, P, C):
    d = {"xT": np.ascontiguousarray(x_b.T), "gfin": np.ascontiguousarray(P["g_final"].reshape(8, 128).T)}
    for n in SHARED_CONST:
        d[n] = C[0][n]
    for p in range(2):
        for n in PARITY_CONST:
            d["%s_p%d" % (n, p)] = C[p][n]
    arr = lambda v: np.ascontiguousarray(v.reshape(8, 128).T)
    for l in range(2):
        d["ggo_l%d" % l] = arr(P["g_group_out"][l])
        d["gffn_l%d" % l] = arr(P["g_ffn_norm"][l])
        d["w_out_l%d" % l] = P["w_out"][l]
        d["w_up_l%d" % l] = P["w_up"][l]
        d["w_down_l%d" % l] = P["w_down"][l]
        for p in range(2):
            ab = ab_inputs(x_b, l, p, P, C)
            for n in AB_L:
                d["%s_l%d" % (n, l)] = ab[n]
            for n in AB_LP:
                d["%s_l%dp%d" % (n, l, p)] = ab[n]
    return d


def build_fused(nlayers=2, nt_ab=NT, nt_c=S // TC):
    nc = bass.Bass("TRN2", target_bir_lowering=False)
    I = {n: nc.dram_tensor(n, sh, F32, kind="ExternalInput").ap() for n, sh in fused_input_shapes().items()}
    xo = nc.dram_tensor("xo", [DM, S], F32, kind="ExternalOutput").ap()
    yscr = nc.dram_tensor("y_scratch", [8, 128, S], F32).ap()
    xscr = nc.dram_tensor("x_scratch", [DM, S], F32).ap()
    bd = nc.alloc_sbuf_tensor("bar_dummy", [1, 8], F32)
    s = Sched(nc)
    for l in range(nlayers):
        xin = I["xT"] if l == 0 else xscr
        xv = xin.rearrange("(k p) t -> p k t", p=128)
        for p in range(2):
            D = {n: I[n] for n in SHARED_CONST}
            D.update({n: I["%s_p%d" % (n, p)] for n in PARITY_CONST})
            D.update({n: I["%s_l%d" % (n, l)] for n in AB_L})
            D.update({n: I["%s_l%dp%d" % (n, l, p)] for n in AB_LP})
            with contextlib.ExitStack() as st:
                cx = Ctx(nc, st, "ab%d%d_" % (l, p))
                phase_ab(s, cx, D, lambda i: xv[:, :, i * 512:(i + 1) * 512],
                         lambda g, i, p=p: yscr[2 * g + p, :, i * 512:(i + 1) * 512], nt_ab, tag="ab%d%d" % (l, p))
            s.barrier(CALL("memset", bd[0:1, 0:8], 0.0))
        final = l == nlayers - 1
        D = {n: I["%s_l%d" % (n, l)] for n in C_L}
        D["gfin"] = I["gfin"]
        xdst = xo if final else xscr
        yv = yscr.rearrange("c p t -> p c t")
        xdv = xdst.rearrange("(k p) t -> p k t", p=128)
        sl = lambda tt: slice(tt * TC, (tt + 1) * TC)
        with contextlib.ExitStack() as st:
            cx = Ctx(nc, st, "c%d_" % l)
            phase_c(s, cx, D, lambda tt: yv[:, :, sl(tt)], lambda tt: xv[:, :, sl(tt)], lambda tt: xdv[:, :, sl(tt)],
                    final, nt_c, tag="c%d" % l)
        s.barrier(CALL("memset", bd[0:1, 0:8], 0.0))
    s.emit()
    return nc


def kernel(**inputs):
    P = {k: np.asarray(v, dtype=np.float32) for k, v in inputs.items()}
    x = P["x"]
    C = [consts_for_parity(p) for p in range(2)]
    nc = build_fused()
    per_b = [fused_inputs(x[b], P, C) for b in range(4)]
    in_maps = [per_b[c % 4] for c in range(8)]
    res = run_bass_kernel_spmd(nc, in_maps, core_ids=list(range(8)))
    out = np.empty_like(x)
    for b in range(4):
        out[b] = np.asarray(res.results[b]["xo"]).T
    return out.astype(np.float32)
```
